# Optimizing a Trainium2 kernel written in Bass

```python
import jax, jax.numpy as jnp
from jax import lax
import numpy as np

D_MODEL = 1024
BATCH = 2
SEQ = 16384
DEPTH = 2
DEC_BATCH = 8
DEC_SEQ = 2048
PAST_LEN = 128

HEAD_DIM = 64
ATT_HEADS = 8
WINDOW_DILATIONS = ((128, 1), (512, 4), (2048, 16))
N_ATT_GROUPS = len(WINDOW_DILATIONS)
ATT_WIDTH = ATT_HEADS * HEAD_DIM
ATT_COLS = N_ATT_GROUPS * ATT_WIDTH
ROPE_THETA = 10000.0
HGRN_HEADS = 4
HGRN_DK = 128
HGRN_DV = 128
HGRN_KEY_COLS = HGRN_HEADS * HGRN_DK
HGRN_WIDTH = HGRN_HEADS * HGRN_DV
HGRN_CHUNK = 64
FNET_GROUPS = 4
FNET_GROUP_DIM = 128
FNET_WIDTH = FNET_GROUPS * FNET_GROUP_DIM
N_BRANCHES = 3
D_FF = 2816
EPS = 1e-6
NEG_INF = -1e30

SPLIT_SIZES = (ATT_COLS, ATT_COLS, ATT_COLS, HGRN_KEY_COLS, HGRN_KEY_COLS, HGRN_KEY_COLS,
               HGRN_WIDTH, HGRN_WIDTH, FNET_WIDTH, N_BRANCHES * D_MODEL)
IN_COLS = sum(SPLIT_SIZES)
SPLIT_POINTS = tuple(int(s) for s in np.cumsum(SPLIT_SIZES)[:-1])

kernel_name = 'hybrid_dilated_hgrn2_fnet_encoder'


def rms_norm(x, g):
    xf = x.astype(jnp.float32)
    y = xf * lax.rsqrt(jnp.mean(xf * xf, axis=-1, keepdims=True) + EPS)
    return (y * g.astype(jnp.float32)).astype(x.dtype)


def swiglu(x, w_gate, w_up, w_down):
    return (jax.nn.silu(x @ w_gate) * (x @ w_up)) @ w_down


def rope(x, pos):
    half = HEAD_DIM // 2
    inv_freq = ROPE_THETA ** (-jnp.arange(half, dtype=jnp.float32) / half)
    ang = pos.astype(jnp.float32)[:, None] * inv_freq[None, :]
    cos = jnp.cos(ang)[None, :, None, :]
    sin = jnp.sin(ang)[None, :, None, :]
    xf = x.astype(jnp.float32)
    x1, x2 = xf[..., :half], xf[..., half:]
    return jnp.concatenate([x1 * cos - x2 * sin, x2 * cos + x1 * sin], axis=-1).astype(x.dtype)


def banded_attention(q, k, v, n):
    G, L, H, hd = q.shape
    nb = -(-L // n)
    Lp = nb * n
    pad = Lp - L
    qp = jnp.pad(q, ((0, 0), (0, pad), (0, 0), (0, 0))).reshape(G, nb, n, H, hd)
    kp = jnp.pad(k, ((0, 0), (n, pad + n), (0, 0), (0, 0))).reshape(G, nb + 2, n, H, hd)
    vp = jnp.pad(v, ((0, 0), (n, pad + n), (0, 0), (0, 0))).reshape(G, nb + 2, n, H, hd)
    kw = jnp.concatenate([kp[:, :-2], kp[:, 1:-1], kp[:, 2:]], axis=2)
    vw = jnp.concatenate([vp[:, :-2], vp[:, 1:-1], vp[:, 2:]], axis=2)
    blk = jnp.arange(nb)[:, None] * n
    qpos = blk + jnp.arange(n)[None, :]
    kpos = blk - n + jnp.arange(3 * n)[None, :]
    rel = kpos[:, None, :] - qpos[:, :, None]
    valid = (jnp.abs(rel) <= n) & (kpos[:, None, :] >= 0) & (kpos[:, None, :] < L)
    scores = jnp.einsum('gbqhd,gbkhd->gbhqk', qp, kw).astype(jnp.float32) * (HEAD_DIM ** -0.5)
    scores = jnp.where(valid[None, :, None], scores, NEG_INF)
    lse = jax.nn.logsumexp(scores, axis=-1, keepdims=True)
    p = jnp.exp(scores - lse).astype(v.dtype)
    out = jnp.einsum('gbhqk,gbkhd->gbqhd', p, vw).reshape(G, Lp, H, hd)[:, :L]
    lse = lse[..., 0].transpose(0, 1, 3, 2).reshape(G, Lp, H)[:, :L]
    return out, lse


def dilated_attention(q, k, v, r, n):
    B, S, H, hd = q.shape
    L = S // r

    def to_classes(t):
        return t.reshape(B, L, r, H, hd).transpose(0, 2, 1, 3, 4).reshape(B * r, L, H, hd)

    out, lse = banded_attention(to_classes(q), to_classes(k), to_classes(v), n)
    out = out.reshape(B, r, L, H, hd).transpose(0, 2, 1, 3, 4).reshape(B, S, H, hd)
    lse = lse.reshape(B, r, L, H).transpose(0, 2, 1, 3).reshape(B, S, H)
    return out, lse


def attention_branch(qa, ka, va, q_gain, k_gain, pos):
    B, S, _ = qa.shape
    nh = N_ATT_GROUPS * ATT_HEADS
    q = rope(rms_norm(qa.reshape(B, S, nh, HEAD_DIM), q_gain), pos).reshape(B, S, N_ATT_GROUPS, ATT_HEADS, HEAD_DIM)
    k = rope(rms_norm(ka.reshape(B, S, nh, HEAD_DIM), k_gain), pos).reshape(B, S, N_ATT_GROUPS, ATT_HEADS, HEAD_DIM)
    v = va.reshape(B, S, N_ATT_GROUPS, ATT_HEADS, HEAD_DIM)
    outs, lses = [], []
    for gi, (w, r) in enumerate(WINDOW_DILATIONS):
        o, l = dilated_attention(q[:, :, gi], k[:, :, gi], v[:, :, gi], r, w // (2 * r))
        outs.append(o)
        lses.append(l)
    alpha = jax.nn.softmax(jnp.stack(lses, axis=0), axis=0)[..., None]
    att = jnp.sum(alpha * jnp.stack(outs, axis=0).astype(jnp.float32), axis=0)
    return att.reshape(B, S, ATT_WIDTH).astype(qa.dtype)


def hgrn2_scan(z, q, i, lb):
    B, S, H, dk = z.shape
    dv = i.shape[-1]
    C = HGRN_CHUNK
    nc = S // C
    zf = z.astype(jnp.float32)
    log_f = jnp.logaddexp(jnp.log(lb), jnp.log1p(-lb) + jax.nn.log_sigmoid(zf))
    kf = (1.0 - lb) * jax.nn.sigmoid(-zf)

    def chunks(t):
        return t.reshape(B, nc, C, H, t.shape[-1]).transpose(1, 0, 2, 3, 4)

    xs = (chunks(q.astype(jnp.float32)), chunks(kf), chunks(i.astype(jnp.float32)), chunks(log_f))
    causal = jnp.tril(jnp.ones((C, C), dtype=bool))[None, :, :, None, None]

    def step(state, inp):
        qc, kc, ic, lfc = inp
        b = jnp.cumsum(lfc, axis=1)
        diff = b[:, :, None] - b[:, None, :]
        decay = jnp.exp(jnp.where(causal, diff, -jnp.inf))
        a = jnp.einsum('bthk,bshk,btshk->bhts', qc, kc, decay)
        o = jnp.einsum('bhts,bshv->bthv', a, ic) + jnp.einsum('bthk,bhkv->bthv', qc * jnp.exp(b), state)
        b_end = b[:, -1]
        k_end = kc * jnp.exp(b_end[:, None] - b)
        state = jnp.exp(b_end)[..., None] * state + jnp.einsum('bshk,bshv->bhkv', k_end, ic)
        return state, o

    state0 = jnp.zeros((B, H, dk, dv), jnp.float32)
    _, o = lax.scan(step, state0, xs)
    return o.transpose(1, 0, 2, 3, 4).reshape(B, S, H, dv)


def hgrn_branch(zf, zb, qh, ih, gh, lb_fwd, lb_bwd, out_gain):
    B, S, _ = qh.shape
    z_f = zf.reshape(B, S, HGRN_HEADS, HGRN_DK)
    z_b = zb.reshape(B, S, HGRN_HEADS, HGRN_DK)
    q = qh.reshape(B, S, HGRN_HEADS, HGRN_DK)
    i = ih.reshape(B, S, HGRN_HEADS, HGRN_DV)
    lbf = lb_fwd.reshape(HGRN_HEADS, HGRN_DK)
    lbb = lb_bwd.reshape(HGRN_HEADS, HGRN_DK)
    o_fwd = hgrn2_scan(z_f, q, i, lbf)
    o_bwd = jnp.flip(hgrn2_scan(jnp.flip(z_b, 1), jnp.flip(q, 1), jnp.flip(i, 1), lbb), axis=1)
    o = rms_norm(o_fwd + o_bwd, out_gain) * jax.nn.silu(gh.reshape(B, S, HGRN_HEADS, HGRN_DV).astype(jnp.float32))
    return o.reshape(B, S, HGRN_WIDTH).astype(qh.dtype)


def fourier_branch(u):
    B, S, _ = u.shape
    uf = u.astype(jnp.float32).reshape(B, S, FNET_GROUPS, FNET_GROUP_DIM).transpose(0, 2, 1, 3)
    y = jnp.fft.fft2(uf, axes=(-2, -1), norm='ortho').real
    return y.transpose(0, 2, 1, 3).reshape(B, S, FNET_WIDTH).astype(u.dtype)


def encoder_layer(x, lb_fwd, lb_bwd, ffn1_norm, ffn1_w_gate, ffn1_w_up, ffn1_w_down,
                  mix_norm, w_in, q_norm, k_norm, w_att_out, hgrn_out_norm, w_hgrn_out,
                  w_fnet_out, w_out, ffn2_norm, ffn2_w_gate, ffn2_w_up, ffn2_w_down):
    B, S, _ = x.shape
    pos = jnp.arange(S)
    x = x + 0.5 * swiglu(rms_norm(x, ffn1_norm), ffn1_w_gate, ffn1_w_up, ffn1_w_down)
    h = rms_norm(x, mix_norm)
    proj = h @ w_in
    qa, ka, va, zf, zb, qh, ih, gh, uf, gl = jnp.split(proj, SPLIT_POINTS, axis=-1)
    a = attention_branch(qa, ka, va, q_norm, k_norm, pos) @ w_att_out
    bh = hgrn_branch(zf, zb, qh, ih, gh, lb_fwd, lb_bwd, hgrn_out_norm) @ w_hgrn_out
    c = fourier_branch(uf) @ w_fnet_out
    gates = jax.nn.sigmoid(gl.reshape(B, S, N_BRANCHES, D_MODEL))
    merged = gates[:, :, 0] * a + gates[:, :, 1] * bh + gates[:, :, 2] * c
    x = x + merged @ w_out
    x = x + 0.5 * swiglu(rms_norm(x, ffn2_norm), ffn2_w_gate, ffn2_w_up, ffn2_w_down)
    return x


def setup_inputs(seed: int = 0) -> dict:
    key = jax.random.key(seed)
    ks = jax.random.split(key, 24)
    f32 = jnp.float32

    def nrm(k, shape, fan_in):
        return jax.random.normal(k, shape, f32) * (fan_in ** -0.5)

    def gain(k, shape):
        return 1.0 + 0.02 * jax.random.normal(k, shape, f32)

    return {
        'x_prompt': jax.random.normal(ks[0], (BATCH, SEQ, D_MODEL), f32),
        'x_sample': jax.random.normal(ks[1], (DEC_BATCH, DEC_SEQ, D_MODEL), f32),
        'hgrn_lb_logits': 0.1 * jax.random.normal(ks[2], (DEPTH, 2, HGRN_KEY_COLS), f32),
        'ffn1_norm': gain(ks[3], (DEPTH, D_MODEL)),
        'ffn1_w_gate': nrm(ks[4], (DEPTH, D_MODEL, D_FF), D_MODEL),
        'ffn1_w_up': nrm(ks[5], (DEPTH, D_MODEL, D_FF), D_MODEL),
        'ffn1_w_down': nrm(ks[6], (DEPTH, D_FF, D_MODEL), D_FF),
        'mix_norm': gain(ks[7], (DEPTH, D_MODEL)),
        'w_in': nrm(ks[8], (DEPTH, D_MODEL, IN_COLS), D_MODEL),
        'q_norm': gain(ks[9], (DEPTH, HEAD_DIM)),
        'k_norm': gain(ks[10], (DEPTH, HEAD_DIM)),
        'w_att_out': nrm(ks[11], (DEPTH, ATT_WIDTH, D_MODEL), ATT_WIDTH),
        'hgrn_out_norm': gain(ks[12], (DEPTH, HGRN_DV)),
        'w_hgrn_out': nrm(ks[13], (DEPTH, HGRN_WIDTH, D_MODEL), HGRN_WIDTH),
        'w_fnet_out': nrm(ks[14], (DEPTH, FNET_WIDTH, D_MODEL), FNET_WIDTH),
        'w_out': nrm(ks[15], (DEPTH, D_MODEL, D_MODEL), D_MODEL),
        'ffn2_norm': gain(ks[16], (DEPTH, D_MODEL)),
        'ffn2_w_gate': nrm(ks[17], (DEPTH, D_MODEL, D_FF), D_MODEL),
        'ffn2_w_up': nrm(ks[18], (DEPTH, D_MODEL, D_FF), D_MODEL),
        'ffn2_w_down': nrm(ks[19], (DEPTH, D_FF, D_MODEL), D_FF),
    }


def reference(x_prompt, x_sample, hgrn_lb_logits, ffn1_norm, ffn1_w_gate, ffn1_w_up, ffn1_w_down,
              mix_norm, w_in, q_norm, k_norm, w_att_out, hgrn_out_norm, w_hgrn_out, w_fnet_out,
              w_out, ffn2_norm, ffn2_w_gate, ffn2_w_up, ffn2_w_down):
    p = jax.nn.softmax(hgrn_lb_logits.astype(jnp.float32), axis=0)
    cum = jnp.cumsum(p, axis=0)
    lb = cum - cum[0:1]

    def trunk(x):
        for l in range(DEPTH):
            x = encoder_layer(x, lb[l, 0], lb[l, 1], ffn1_norm[l], ffn1_w_gate[l], ffn1_w_up[l], ffn1_w_down[l],
                              mix_norm[l], w_in[l], q_norm[l], k_norm[l], w_att_out[l], hgrn_out_norm[l],
                              w_hgrn_out[l], w_fnet_out[l], w_out[l], ffn2_norm[l], ffn2_w_gate[l],
                              ffn2_w_up[l], ffn2_w_down[l])
        return x

    y_prompt = trunk(x_prompt)
    y_sample = trunk(x_sample)
    return (y_prompt, y_sample)
```

```python
import contextlib
import numpy as np
import ml_dtypes
import concourse.bass as bass
import concourse.mybir as mybir
from concourse.bass_utils import run_bass_kernel_spmd

F32 = mybir.dt.float32
BF16 = mybir.dt.bfloat16
AF = mybir.ActivationFunctionType
ALU = mybir.AluOpType
NPBF = ml_dtypes.bfloat16

D = 1024
DFF = 2816
DEPTH = 2
KC = D // 128
FC = DFF // 128
EPS = 1e-6
T = 512
HD = 64
NQK = 3072
O_Q, O_K, O_V, O_ZF, O_ZB, O_QH, O_IH, O_GH, O_UF, O_GL = 0, 1536, 3072, 4608, 5120, 5632, 6144, 6656, 7168, 7680
IN_COLS = 10752
GW = 256
GWT = 512


class Buf:
    __slots__ = ("name", "w", "r")

    def __init__(self, name=""):
        self.name = name
        self.w = None
        self.r = {}


class Eng:
    def __init__(self, name, obj, sem):
        self.name, self.obj, self.sem = name, obj, sem
        self.cnt = 0
        self.waited = {}


class KB:
    def __init__(self):
        self.nc = bass.Bass("TRN2", target_bir_lowering=False)
        self.es = contextlib.ExitStack()
        nc = self.nc
        self.eng = {}
        for nm, obj in (("pe", nc.tensor), ("act", nc.scalar), ("dve", nc.vector),
                        ("pool", nc.gpsimd), ("sp", nc.sync)):
            sem = self.es.enter_context(nc.semaphore("s_" + nm))
            self.eng[nm] = Eng(nm, obj, sem)
        self.ND = 12
        self.dslots = {}
        self.dcnt = {}
        for q in ("sp", "pool", "act"):
            self.dslots[q] = [[self.es.enter_context(nc.semaphore(f"d_{q}{i}")), 0] for i in range(self.ND)]
            self.dcnt[q] = 0
        self.nuid = 0
        self.psum_banks = []
        self.psum_i = 0
        self.bank_range = (0, 8)

    def sbuf(self, name, shape, dtype):
        return self.es.enter_context(self.nc.sbuf_tensor(name, list(shape), dtype))

    def psum(self, name, shape, dtype):
        return self.es.enter_context(self.nc.psum_tensor(name, list(shape), dtype))

    def dram(self, name, shape, dtype, kind="Internal"):
        return self.nc.dram_tensor(name, list(shape), dtype, kind=kind).ap()

    def _wait(self, eng, toks):
        best = {}
        for sem, val in toks:
            k = id(sem)
            if k not in best or best[k][1] < val:
                best[k] = (sem, val)
        for k, (sem, val) in best.items():
            if eng.waited.get(k, 0) >= val:
                continue
            if sem is eng.sem and eng.name == "pe":
                continue
            eng.obj.wait_ge(sem, val)
            eng.waited[k] = val

    @staticmethod
    def _deps(R, W):
        toks = []
        for b in R:
            if b.w is not None:
                toks.append(b.w)
        for b in W:
            if b.w is not None:
                toks.append(b.w)
            toks.extend(b.r.values())
        return toks

    @staticmethod
    def _commit(tok, R, W):
        k = id(tok[0])
        for b in R:
            b.r[k] = tok
        for b in W:
            b.w = tok
            b.r = {}

    def op(self, E, method, *args, R=(), W=(), **kw):
        eng = self.eng[E]
        self._wait(eng, self._deps(R, W))
        ins = getattr(eng.obj, method)(*args, **kw)
        eng.cnt += 1
        ins.then_inc(eng.sem, 1)
        tok = (eng.sem, eng.cnt)
        self._commit(tok, R, W)
        return tok

    def mm_raw(self, items, R=(), W=()):
        eng = self.eng["pe"]
        self._wait(eng, self._deps(R, W))
        ins = None
        for (out, l, r, st, sp) in items:
            ins = eng.obj.matmul(out, lhsT=l, rhs=r, start=st, stop=sp)
        eng.cnt += 1
        ins.then_inc(eng.sem, 1)
        tok = (eng.sem, eng.cnt)
        self._commit(tok, R, W)
        return tok

    def mm(self, out, pairs, R=(), W=(), Rper=None, **kw):
        eng = self.eng["pe"]
        self._wait(eng, self._deps(R, W))
        n = len(pairs)
        ins = None
        for i, (l, r) in enumerate(pairs):
            if Rper is not None:
                self._wait(eng, self._deps(Rper[i], ()))
            ins = eng.obj.matmul(out, lhsT=l, rhs=r, start=(i == 0), stop=(i == n - 1), **kw)
        eng.cnt += 1
        ins.then_inc(eng.sem, 1)
        tok = (eng.sem, eng.cnt)
        allR = list(R)
        if Rper is not None:
            for rr in Rper:
                allR += list(rr)
        self._commit(tok, allR, W)
        return tok

    def mm_multi(self, groups, R=(), W=()):
        eng = self.eng["pe"]
        self._wait(eng, self._deps(R, W))
        ins = None
        for out, pairs in groups:
            n = len(pairs)
            for i, (l, r) in enumerate(pairs):
                ins = eng.obj.matmul(out, lhsT=l, rhs=r, start=(i == 0), stop=(i == n - 1))
        eng.cnt += 1
        ins.then_inc(eng.sem, 1)
        tok = (eng.sem, eng.cnt)
        self._commit(tok, R, W)
        return tok

    def dma(self, q, out, in_, R=(), W=(), **kw):
        eng = self.eng[q]
        toks = self._deps(R, W)
        slot = self.dslots[q][self.dcnt[q] % self.ND]
        self.dcnt[q] += 1
        if slot[1] > 0:
            toks.append((slot[0], slot[1]))
        self._wait(eng, toks)
        ins = eng.obj.dma_start(out=out, in_=in_, **kw)
        slot[1] += 16
        ins.then_inc(slot[0], 16)
        tok = (slot[0], slot[1])
        self._commit(tok, R, W)
        return tok

    def barrier(self):
        toks = [(e.sem, e.cnt) for e in self.eng.values() if e.cnt > 0]
        for q in self.dslots:
            toks += [(sl[0], sl[1]) for sl in self.dslots[q] if sl[1] > 0]
        for e in self.eng.values():
            self._wait(e, toks)

    def bank(self):
        lo, hi = self.bank_range
        b = self.psum_banks[lo + self.psum_i % (hi - lo)]
        self.psum_i += 1
        return b


class Rot:
    def __init__(self, items):
        self.items = items
        self.i = 0

    def next(self):
        it = self.items[self.i % len(self.items)]
        self.i += 1
        return it


def weight_plan():
    tiles = []
    groups = {}

    def add(gname, pieces):
        groups.setdefault(gname, []).append(len(tiles))
        tiles.append(pieces)

    for f in ("ffn1", "ffn2"):
        for mt in range(DFF // GW):
            add(f + "_gu", [(f + "_w_gate", D, mt * GW, GW), (f + "_w_up", D, mt * GW, GW)])
        for m in range(KC):
            add(f + "_down", [(f + "_w_down", DFF, m * 128, 128)])
    for c0 in list(range(O_Q, O_V, 512)) + list(range(O_ZF, O_IH, 512)) + [O_GH, O_UF]:
        add("in_fm", [("w_in", D, c0, 512)])
    for c0 in list(range(O_V, O_ZF, 512)) + [O_IH]:
        add("in_tm", [("w_in", D, c0, 512)])
    for m in range(KC):
        add("p5m", [("w_in", D, O_GL + br * D + m * 128, 128) for br in range(3)]
            + [(w, 512, m * 128, 128) for w in ("w_att_out", "w_hgrn_out", "w_fnet_out")])
    for c0 in (0, 512):
        add("w_out", [("w_out", D, c0, 512)])
    sizes = [sum(k * g for (_, k, _, g) in t) for t in tiles]
    offs = list(np.cumsum([0] + sizes[:-1]))
    return tiles, groups, [int(o) for o in offs], sizes, int(sum(sizes))


WTILES, WGROUPS, WOFFS, WSIZES, WLAYER = weight_plan()
IN_FM_COLS = list(range(O_Q, O_V, 512)) + list(range(O_ZF, O_IH, 512)) + [O_GH, O_UF]
IN_TM_COLS = list(range(O_V, O_ZF, 512)) + [O_IH]
WSLOT = max(s // 128 for s in WSIZES)
CAST_CH = 128 * 8192
WTOTAL = ((DEPTH * WLAYER + CAST_CH - 1) // CAST_CH) * CAST_CH


def host_pack_weights(inputs):
    flat = np.zeros(WTOTAL, np.float32)
    ws = {k: np.asarray(v) for k, v in inputs.items() if k.startswith("w_") or "_w_" in k}
    for l in range(DEPTH):
        for ti, pieces in enumerate(WTILES):
            blks = []
            for (wname, kdim, c0, g) in pieces:
                w = ws[wname][l]
                kc = kdim // 128
                blks.append(w[:, c0:c0 + g].reshape(kc, 128, g).transpose(1, 0, 2).reshape(128, kc * g))
            blk = np.concatenate(blks, axis=1)
            o = l * WLAYER + WOFFS[ti]
            flat[o:o + blk.size] = blk.reshape(-1)
    return flat


class WStream:
    def __init__(self, kb, G, order):
        self.kb, self.G, self.order = kb, G, order
        self.pos = 0
        self.issued = 0

    def _issue(self, i):
        G = self.G
        l, ti = self.order[i]
        t, b = G.wslots[G.wslot_i % len(G.wslots)]
        G.wslot_i += 1
        n = WSIZES[ti] // 128
        o = l * WLAYER + WOFFS[ti]
        src = G.wbf[o:o + WSIZES[ti]].rearrange("(p n) -> p n", p=128)
        self.kb.dma("sp", t[:, 0:n], src, R=[G.wcast_b], W=[b])
        return (t, b)

    def get(self):
        depth = len(self.G.wslots)
        if self.pos == 0:
            self.live = []
        while self.issued < min(self.pos + depth, len(self.order)):
            self.live.append(self._issue(self.issued))
            self.issued += 1
        r = self.live[self.pos]
        self.pos += 1
        return r


class Glob:
    pass


def seq_scratch(kb, name, L, dbg):
    S = Glob()
    S.L = L
    S.name = name

    def mk(nm, shape, dt):
        kind = "Internal"
        if nm in dbg.get("out", ()):
            kind = "ExternalOutput"
        if nm in dbg.get("in", ()):
            kind = "ExternalInput"
        return kb.dram(f"{name}_{nm}", shape, dt, kind=kind)

    S.x = [kb.dram(f"{name}_x", [D, L], F32, kind="ExternalInput")]
    for l in range(1, DEPTH):
        S.x.append(mk(f"xl{l}", [D, L], F32))
    S.y = kb.dram(f"{name}_y", [D, L], F32, kind="ExternalOutput")
    S.x1 = mk("x1", [D, L], F32)
    S.qk = mk("qk", [NQK, L], BF16)
    S.v = mk("v", [L, 1536], BF16)
    S.z = mk("z", [1024, L], F32)
    S.qh = mk("qh", [512, L], BF16)
    S.gh = mk("gh", [512, L], BF16)
    S.ih = mk("ih", [L, 512], BF16)
    S.ucs = mk("ucs", [L, 1024], BF16)
    S.att = mk("att", [512, L], BF16)
    S.hg = mk("hg", [512, L], BF16)
    S.fn = mk("fn", [512, L], BF16)
    S.ob = mk("ob", [512, L], F32)
    S.fnb = kb.dram(f"{name}_fnb", [128, 4 * (L // 128) + 256], BF16, kind="ExternalInput")
    S.fnf = kb.dram(f"{name}_fnf", [128, 2 * (L // 128)], F32, kind="ExternalInput")
    S.rope = kb.dram(f"{name}_rope", [128, 2, L], BF16, kind="ExternalInput")
    S.bufs = {}
    return S


def sbuf_of(S, nm, t):
    k = (nm, t)
    if k not in S.bufs:
        S.bufs[k] = Buf(f"{S.name}_{nm}_{t}")
    return S.bufs[k]


def rmsnorm_a(kb, G, P, src, src_b):
    sq, sq_b = P.sqr.next()
    for c in range(KC):
        e = "pool" if c % 3 == 2 else "dve"
        kb.op(e, "tensor_tensor", out=sq[:, c, :], in0=src[:, c, :], in1=src[:, c, :], op=ALU.mult, R=[src_b[c]], W=[sq_b[c]])
    return sq, sq_b


def rmsnorm_b(kb, G, P, sqh, src, src_b, gcol, dst, dst_b):
    sq, sq_b = sqh
    bk, bk_b = kb.bank()
    kb.mm(bk[:, 0:T], [(G.ones[:], sq[:, c, :]) for c in range(KC)], R=[G.const_b], W=[bk_b], Rper=[[sq_b[c]] for c in range(KC)])
    rs, rs_b = P.rstd.next()
    kb.op("act", "activation", out=rs[:], in_=bk[:, 0:T], func=AF.Sqrt, bias=G.epsD[:], scale=1.0 / D, R=[bk_b, G.const_b], W=[rs_b])
    kb.op("dve", "reciprocal", out=rs[:], in_=rs[:], R=[rs_b], W=[rs_b])
    for c in range(KC):
        kb.op("dve", "scalar_tensor_tensor", out=dst[:, c, :], in0=src[:, c, :], scalar=G.vecs[:, gcol + c:gcol + c + 1],
              in1=rs[:], op0=ALU.mult, op1=ALU.mult, R=[src_b[c], rs_b, G.const_b], W=[dst_b[c]])


def rmsnorm(kb, G, P, src, src_b, gcol, dst, dst_b):
    sqh = rmsnorm_a(kb, G, P, src, src_b)
    rmsnorm_b(kb, G, P, sqh, src, src_b, gcol, dst, dst_b)


def ffn(kb, G, P, ws, xn, xn_b, resid, resid_b, out, out_b, hook=None):
    for mt in range(DFF // GW):
        if mt == 5 and hook is not None:
            hook()
        wt, wb = ws.get()
        w = wt[:, 0:2 * KC * GW].rearrange("p (s k g) -> p s k g", s=2, k=KC)
        for j in range(GW // 128):
            m = mt * (GW // 128) + j
            bg, bg_b = kb.bank()
            bu, bu_b = kb.bank()
            kb.mm(bg[:, 0:T], [(w[:, 0, k, j * 128:(j + 1) * 128], xn[:, k, :]) for k in range(KC)], R=[wb], W=[bg_b],
                  Rper=[[xn_b[k]] for k in range(KC)])
            kb.mm(bu[:, 0:T], [(w[:, 1, k, j * 128:(j + 1) * 128], xn[:, k, :]) for k in range(KC)], R=[wb] + xn_b, W=[bu_b])
            sg, sg_b = P.sgt.next()
            kb.op("act", "activation", out=sg[:], in_=bg[:, 0:T], func=AF.Silu, R=[bg_b], W=[sg_b])
            kb.op("dve", "tensor_tensor", out=P.hid[:, m, :], in0=sg[:], in1=bu[:, 0:T], op=ALU.mult,
                  R=[sg_b, bu_b], W=[P.hid_b[m]])
    for m in range(KC):
        wt, wb = ws.get()
        w = wt[:, 0:FC * 128].rearrange("p (k g) -> p k g", k=FC)
        bk, bk_b = kb.bank()
        kb.mm(bk[:, 0:T], [(w[:, k, :], P.hid[:, k, :]) for k in range(FC)], R=[wb] + P.hid_b, W=[bk_b])
        kb.op("dve", "scalar_tensor_tensor", out=out[:, m, :], in0=bk[:, 0:T], scalar=0.5, in1=resid[:, m, :],
              op0=ALU.mult, op1=ALU.add, R=[bk_b, resid_b[m]], W=[out_b[m]])


def fm_view(ap2d, c0, nchunk, t0, n):
    return ap2d[c0:c0 + nchunk * 128, t0:t0 + n].rearrange("(c p) t -> p c t", p=128)


def alloc_dense(kb, es, nc):
    P = Glob()

    kb.nuid += 1
    uid = kb.nuid

    def sb(name, shape, dt):
        return es.enter_context(nc.sbuf_tensor(f"{name}_u{uid}", list(shape), dt))
    P.sb = sb

    P.xin = Rot([(sb(f"xin{i}", [128, KC, T], F32), [Buf() for _ in range(KC)]) for i in range(2)])
    P.sqr = Rot([(sb(f"sq{i}", [128, KC, T], BF16), [Buf() for _ in range(KC)]) for i in range(2)])
    P.rstd = Rot([(sb(f"rstd{i}", [128, T], F32), Buf()) for i in range(2)])
    P.xn = Rot([(sb(f"xn{i}", [128, KC, T], BF16), [Buf() for _ in range(KC)]) for i in range(3)])
    P.hid = sb("hid", [128, FC, T], BF16)
    P.hid_b = [Buf() for _ in range(FC)]
    P.sgt = Rot([(sb(f"sgt{i}", [128, T], F32), Buf()) for i in range(3)])
    P.x1 = Rot([(sb(f"x1_{i}", [128, KC, T], F32), [Buf() for _ in range(KC)]) for i in range(1)])
    P.stg = Rot([(sb(f"stg{i}", [128, 4, T], BF16), Buf()) for i in range(2)])
    P.stgf = Rot([(sb(f"stgf{i}", [128, 4, T], F32), Buf()) for i in range(1)])
    P.wslots = [(sb(f"wslot{i}", [128, WSLOT], BF16), Buf()) for i in range(5)]
    return P


def phase_p1(kb, G, S, l):
    nc = kb.nc
    L = S.L
    NT = L // T
    with contextlib.ExitStack() as es:
        P = alloc_dense(kb, es, nc)
        G.wslots = P.wslots
        order = []
        for t in range(NT):
            order += [(l, i) for i in WGROUPS["ffn1_gu"] + WGROUPS["ffn1_down"] + WGROUPS["in_fm"] + WGROUPS["in_tm"]]
        ws = WStream(kb, G, order)
        vb = l * G.NV
        def preA(t):
            xin, xin_b = P.xin.next()
            kb.dma("sp", xin[:], fm_view(S.x[l], 0, KC, t * T, T), R=[sbuf_of(S, f"x{l}", t)], W=xin_b)
            sqh = rmsnorm_a(kb, G, P, xin, xin_b)
            return (xin, xin_b, sqh)

        def preB(pa):
            xin, xin_b, sqh = pa
            xn, xn_b = P.xn.next()
            rmsnorm_b(kb, G, P, sqh, xin, xin_b, vb + 0, xn, xn_b)
            return (xin, xin_b, xn, xn_b)

        nxt = preB(preA(0))
        for t in range(NT):
            t0 = t * T
            xin, xin_b, xn, xn_b = nxt
            nxt_a = None
            x1, x1_b = P.x1.next()
            ffn(kb, G, P, ws, xn, xn_b, xin, xin_b, x1, x1_b)
            kb.dma("pool", fm_view(S.x1, 0, KC, t0, T), x1[:], R=x1_b, W=[sbuf_of(S, "x1", t)])
            hn, hn_b = P.xn.next()
            rmsnorm(kb, G, P, x1, x1_b, vb + 8, hn, hn_b)
            if t + 1 < NT:
                nxt_a = preA(t + 1)
            for j, c0 in enumerate(IN_FM_COLS):
                if j == 5 and nxt_a is not None:
                    nxt = preB(nxt_a)
                wt, wb = ws.get()
                w = wt[:, 0:KC * 512].rearrange("p (k g) -> p k g", k=KC)
                isz = (O_ZF <= c0 < O_QH)
                isu = (c0 == O_UF)
                st, st_b = (P.stgf if isz else P.stg).next()
                for mc in range(4):
                    bk, bk_b = kb.bank()
                    kb.mm(bk[:, 0:T], [(w[:, k, mc * 128:(mc + 1) * 128], hn[:, k, :]) for k in range(KC)],
                          R=[wb], W=[bk_b], Rper=[[hn_b[k]] for k in range(KC)])
                    if c0 < O_V:
                        gcol = G.vecs[:, vb + 24:vb + 25] if c0 < O_K else G.vecs[:, vb + 25:vb + 26]
                        if mc % 2 == 0:
                            kb.op("act", "activation", out=st[:, mc, :], in_=bk[:, 0:T], func=AF.Copy, scale=gcol, R=[bk_b, G.const_b], W=[st_b])
                        else:
                            kb.op("dve", "tensor_scalar", out=st[:, mc, :], in0=bk[:, 0:T], scalar1=gcol, scalar2=None, op0=ALU.mult,
                                  R=[bk_b, G.const_b], W=[st_b])
                    elif mc % 2 == 0:
                        kb.op("act", "activation", out=st[:, mc, :], in_=bk[:, 0:T], func=AF.Copy, R=[bk_b], W=[st_b])
                    else:
                        kb.op("dve", "tensor_copy", out=st[:, mc, :], in_=bk[:, 0:T], R=[bk_b], W=[st_b])
                if c0 < O_V:
                    kb.dma("pool", fm_view(S.qk, c0, 4, t0, T), st[:], R=[st_b], W=[sbuf_of(S, "qk", t)])
                elif isz:
                    kb.dma("pool", fm_view(S.z, c0 - O_ZF, 4, t0, T), st[:], R=[st_b], W=[sbuf_of(S, "z", t)])
                elif c0 == O_QH:
                    kb.dma("pool", fm_view(S.qh, 0, 4, t0, T), st[:], R=[st_b], W=[sbuf_of(S, "qh", t)])
                elif c0 == O_GH:
                    kb.dma("pool", fm_view(S.gh, 0, 4, t0, T), st[:], R=[st_b], W=[sbuf_of(S, "gh", t)])
                elif isu:
                    for tb in range(4):
                        st2, st2_b = P.stg.next()
                        for g in range(4):
                            bk, bk_b = kb.bank()
                            kb.mm(bk[:, 0:256], [(st[:, g, tb * 128:(tb + 1) * 128], G.ccsc[:])],
                                  R=[st_b, G.const_b], W=[bk_b])
                            e = "act" if g % 2 == 0 else "dve"
                            if e == "act":
                                kb.op("act", "activation", out=st2[:, g, 0:256], in_=bk[:, 0:256], func=AF.Copy, R=[bk_b], W=[st2_b])
                            else:
                                kb.op("dve", "tensor_copy", out=st2[:, g, 0:256], in_=bk[:, 0:256], R=[bk_b], W=[st2_b])
                        dst = S.ucs[t0 + tb * 128:t0 + (tb + 1) * 128, :].rearrange("p (g c) -> p g c", g=4)
                        kb.dma("pool", dst, st2[:, :, 0:256], R=[st2_b], W=[sbuf_of(S, "ucs", t)])
            for j, c0 in enumerate(IN_TM_COLS):
                wt, wb = ws.get()
                w = wt[:, 0:KC * 512].rearrange("p (k g) -> p k g", k=KC)
                st, st_b = P.stg.next()
                for tb in range(4):
                    bk, bk_b = kb.bank()
                    kb.mm(bk[:, 0:512], [(hn[:, k, tb * 128:(tb + 1) * 128], w[:, k, :]) for k in range(KC)],
                          R=[wb] + hn_b, W=[bk_b])
                    if tb % 2 == 0:
                        kb.op("act", "activation", out=st[:, tb, :], in_=bk[:, 0:512], func=AF.Copy, R=[bk_b], W=[st_b])
                    else:
                        kb.op("dve", "tensor_copy", out=st[:, tb, :], in_=bk[:, 0:512], R=[bk_b], W=[st_b])
                if c0 < O_ZF:
                    dst = S.v[t0:t0 + T, c0 - O_V:c0 - O_V + 512].rearrange("(b p) c -> p b c", p=128)
                    kb.dma("pool", dst, st[:], R=[st_b], W=[sbuf_of(S, "v", t)])
                else:
                    dst = S.ih[t0:t0 + T, :].rearrange("(b p) c -> p b c", p=128)
                    kb.dma("pool", dst, st[:], R=[st_b], W=[sbuf_of(S, "ih", t)])
        kb.barrier()


def phase_p5(kb, G, S, l):
    nc = kb.nc
    L = S.L
    NT = L // T
    last = (l + 1 >= G.nlayers)
    xout = S.y if last else S.x[l + 1]
    xout_nm = "y" if last else f"x{l + 1}"
    with contextlib.ExitStack() as es:
        P = alloc_dense(kb, es, nc)
        G.wslots = P.wslots
        P.br = [(P.sb(f"br{i}", [128, 4, T], BF16), Buf()) for i in range(3)]
        order = []
        for t in range(NT):
            order += [(l, i) for i in WGROUPS["p5m"] + WGROUPS["w_out"] + WGROUPS["ffn2_gu"] + WGROUPS["ffn2_down"]]
        ws = WStream(kb, G, order)
        vb = l * G.NV
        def preA(t):
            x1, x1_b = P.xin.next()
            kb.dma("sp", x1[:], fm_view(S.x1, 0, KC, t * T, T), R=[sbuf_of(S, "x1", t)], W=x1_b)
            for i, (nm, src) in enumerate((("att", S.att), ("hg", S.hg), ("fn", S.fn))):
                kb.dma("sp", P.br[i][0][:], fm_view(src, 0, 4, t * T, T), R=[sbuf_of(S, nm, t)], W=[P.br[i][1]])
            sqh = rmsnorm_a(kb, G, P, x1, x1_b)
            return (x1, x1_b, sqh)

        def preB(pa):
            x1, x1_b, sqh = pa
            hn, hn_b = P.xn.next()
            rmsnorm_b(kb, G, P, sqh, x1, x1_b, vb + 8, hn, hn_b)
            return (x1, x1_b, hn, hn_b)

        nxt = [preB(preA(0))]
        for t in range(NT):
            t0 = t * T
            x1, x1_b, hn, hn_b = nxt[0]
            mg, mg_b = P.xn.next()
            for m in range(KC):
                wt, wb = ws.get()
                wg = wt[:, 0:3 * KC * 128].rearrange("p (b k g) -> p b k g", b=3, k=KC)
                wo = wt[:, 3 * KC * 128:3 * KC * 128 + 3 * 4 * 128].rearrange("p (b k g) -> p b k g", b=3, k=4)
                prods = []
                for br in range(3):
                    bg, bg_b = kb.bank()
                    bp, bp_b = kb.bank()
                    kb.mm(bg[:, 0:T], [(wg[:, br, k, :], hn[:, k, :]) for k in range(KC)], R=[wb], W=[bg_b], Rper=[[hn_b[k]] for k in range(KC)])
                    kb.mm(bp[:, 0:T], [(wo[:, br, k, :], P.br[br][0][:, k, :]) for k in range(4)],
                          R=[wb, P.br[br][1]], W=[bp_b])
                    sg, sg_b = P.sgt.next()
                    kb.op("act", "activation", out=sg[:], in_=bg[:, 0:T], func=AF.Sigmoid, R=[bg_b], W=[sg_b])
                    kb.op("dve", "tensor_tensor", out=sg[:], in0=sg[:], in1=bp[:, 0:T], op=ALU.mult,
                          R=[sg_b, bp_b], W=[sg_b])
                    prods.append((sg, sg_b))
                kb.op("pool", "tensor_tensor", out=prods[0][0][:], in0=prods[0][0][:], in1=prods[1][0][:], op=ALU.add,
                      R=[prods[1][1]], W=[prods[0][1]])
                kb.op("pool", "tensor_tensor", out=mg[:, m, :], in0=prods[0][0][:], in1=prods[2][0][:], op=ALU.add,
                      R=[prods[0][1], prods[2][1]], W=[mg_b[m]])
            x2, x2_b = P.x1.next()
            for j in range(2):
                wt, wb = ws.get()
                w = wt[:, 0:KC * 512].rearrange("p (k g) -> p k g", k=KC)
                for mc in range(4):
                    m = j * 4 + mc
                    bk, bk_b = kb.bank()
                    kb.mm(bk[:, 0:T], [(w[:, k, mc * 128:(mc + 1) * 128], mg[:, k, :]) for k in range(KC)],
                          R=[wb], W=[bk_b], Rper=[[mg_b[k]] for k in range(KC)])
                    kb.op("dve", "tensor_tensor", out=x2[:, m, :], in0=bk[:, 0:T], in1=x1[:, m, :], op=ALU.add,
                          R=[bk_b, x1_b[m]], W=[x2_b[m]])
            xn, xn_b = P.xn.next()
            rmsnorm(kb, G, P, x2, x2_b, vb + 16, xn, xn_b)
            x3, x3_b = P.xin.next()
            hook = None
            if t + 1 < NT:
                pa = preA(t + 1)

                def hook(pa=pa):
                    nxt[0] = preB(pa)
            ffn(kb, G, P, ws, xn, xn_b, x2, x2_b, x3, x3_b, hook=hook)
            G.out_toks.append(kb.dma("pool", fm_view(xout, 0, KC, t0, T), x3[:], R=x3_b, W=[sbuf_of(S, xout_nm, t)]))
        kb.barrier()


NV = 32
CB_ONES, CB_CCSC, CB_IDENT, CB_BLK, CB_ROT, CB_HMF, CB_HMB, CB_AM, CB_END = 0, 128, 384, 512, 640, 768, 896, 1024, 1024 + 4 * 512


def host_consts():
    cb = np.zeros((128, CB_END), np.float32)
    cb[:, CB_ONES:CB_ONES + 128] = 1.0
    c = np.arange(128)
    ang = 2 * np.pi * np.outer(c, c) / 128.0
    cb[:, CB_CCSC:CB_CCSC + 128] = np.cos(ang) / np.sqrt(128.0)
    cb[:, CB_CCSC + 128:CB_CCSC + 256] = np.sin(ang) / np.sqrt(128.0)
    cb[:, CB_IDENT:CB_IDENT + 128] = np.eye(128)
    blk = np.zeros((128, 128))
    blk[:64, :64] = 1
    blk[64:, 64:] = 1
    cb[:, CB_BLK:CB_BLK + 128] = blk
    rot = np.zeros((128, 128))
    for h in range(2):
        for i in range(32):
            rot[h * 64 + 32 + i, h * 64 + i] = -1.0
            rot[h * 64 + i, h * 64 + 32 + i] = 1.0
    cb[:, CB_ROT:CB_ROT + 128] = rot
    sidx = np.arange(128)[:, None]
    tidx = np.arange(128)[None, :]
    same = (sidx // 64) == (tidx // 64)
    cb[:, CB_HMF:CB_HMF + 128] = (same & (sidx <= tidx))
    cb[:, CB_HMB:CB_HMB + 128] = (same & (sidx >= tidx))
    m1 = (tidx <= sidx).astype(np.float32)
    m2 = (tidx >= sidx).astype(np.float32)
    m1f = m1 * (sidx >= 64)
    m2l = m2 * (sidx < 64)
    for v, (a, b) in enumerate(((m1, m2), (m1f, m2), (m1, m2l), (m1f, m2l))):
        cb[:, CB_AM + v * 512:CB_AM + (v + 1) * 512] = (np.concatenate([a, a, b, b], axis=1) - 1.0) * 30000.0
    return cb.astype(NPBF)


def host_vecs(inputs):
    v = np.zeros((128, DEPTH * NV), np.float32)
    for l in range(DEPTH):
        b = l * NV
        for i, nm in enumerate(("ffn1_norm", "mix_norm", "ffn2_norm")):
            v[:, b + 8 * i:b + 8 * i + 8] = np.asarray(inputs[nm])[l].reshape(8, 128).T
        v[:, b + 24] = np.tile(np.asarray(inputs["q_norm"])[l], 2)
        v[:, b + 25] = np.tile(np.asarray(inputs["k_norm"])[l], 2)
        v[:, b + 26] = np.asarray(inputs["hgrn_out_norm"])[l]
    return v


def host_lbl(inputs):
    lg = np.asarray(inputs["hgrn_lb_logits"])
    return np.ascontiguousarray(lg.reshape(DEPTH * 2 * 4, 128).T)


def build(Lp, Ls, nlayers=DEPTH, dbg=None, phases=("p1", "p2", "p3", "p4", "p5")):
    dbg = dbg or {}
    kb = KB()
    nc = kb.nc
    G = Glob()
    G.NV = NV
    G.nlayers = nlayers
    G.out_toks = []
    wsrc = kb.dram("wsrc", [WTOTAL], F32, kind="ExternalInput")
    G.wbf = kb.dram("wbf", [WTOTAL], BF16)
    vecs_d = kb.dram("vecs", [128, DEPTH * NV], F32, kind="ExternalInput")
    cb_d = kb.dram("cbf", [128, CB_END], BF16, kind="ExternalInput")
    G.vecs = kb.sbuf("vecs_sb", [128, DEPTH * NV], F32)
    G.cb = kb.sbuf("cb_sb", [128, CB_END], BF16)
    G.ones = G.cb[:, CB_ONES:CB_ONES + 128]
    G.ccsc = G.cb[:, CB_CCSC:CB_CCSC + 256]
    G.ident = G.cb[:, CB_IDENT:CB_IDENT + 128]
    G.blk = G.cb[:, CB_BLK:CB_BLK + 128]
    G.rot = G.cb[:, CB_ROT:CB_ROT + 128]
    G.hmask = [G.cb[:, CB_HMF:CB_HMF + 128], G.cb[:, CB_HMB:CB_HMB + 128]]
    G.amask = [G.cb[:, CB_AM + v * 512:CB_AM + (v + 1) * 512] for v in range(4)]
    G.const_b = Buf("const")
    G.epsc = kb.sbuf("epsc", [128, 2], F32)
    G.epsD = G.epsc[:, 0:1]
    G.wcast_b = Buf("wcast")
    G.wslot_i = 0
    for i in range(8):
        kb.psum_banks.append((kb.psum(f"bank{i}", [128, 512], F32), Buf()))
    seqs = []
    if Lp:
        seqs.append(seq_scratch(kb, "p", Lp, dbg))
    if Ls:
        seqs.append(seq_scratch(kb, "s", Ls, dbg))
    kb.op("dve", "memset", G.epsc[:], EPS, W=[G.const_b])
    kb.dma("sp", G.vecs[:], vecs_d[:, :], W=[G.const_b])
    kb.dma("sp", G.cb[:], cb_d[:, :], W=[G.const_b])
    lbl_d = kb.dram("lbl", [128, DEPTH * 8], F32, kind="ExternalInput")
    G.lbe = kb.sbuf("lbe", [128, DEPTH * 8], F32)
    G.lb = kb.sbuf("lb", [128, DEPTH * 8], F32)
    G.oml = kb.sbuf("oml", [128, DEPTH * 8], F32)
    G.lbt = kb.sbuf("lbt", [128, 8], F32)
    G.rmask = kb.sbuf("rmask", [128, T], F32)
    lb_b = Buf()
    kb.dma("sp", G.lbe[:], lbl_d[:, :], W=[lb_b])
    kb.op("act", "activation", out=G.lbe[:], in_=G.lbe[:], func=AF.Exp, R=[lb_b], W=[lb_b])
    kb.op("dve", "tensor_copy", out=G.lbt[:], in_=G.lbe[:, 0:8], R=[lb_b], W=[lb_b])
    for l in range(1, DEPTH):
        kb.op("dve", "tensor_tensor", out=G.lbt[:], in0=G.lbt[:], in1=G.lbe[:, l * 8:(l + 1) * 8], op=ALU.add, R=[lb_b], W=[lb_b])
    kb.op("dve", "reciprocal", out=G.lbt[:], in_=G.lbt[:], R=[lb_b], W=[lb_b])
    kb.op("dve", "memset", G.lb[:, 0:8], 0.0, W=[lb_b])
    for l in range(1, DEPTH):
        kb.op("dve", "tensor_tensor", out=G.lbe[:, l * 8:(l + 1) * 8], in0=G.lbe[:, l * 8:(l + 1) * 8], in1=G.lbt[:], op=ALU.mult, R=[lb_b], W=[lb_b])
        kb.op("dve", "tensor_tensor", out=G.lb[:, l * 8:(l + 1) * 8], in0=G.lb[:, (l - 1) * 8:l * 8], in1=G.lbe[:, l * 8:(l + 1) * 8], op=ALU.add, R=[lb_b], W=[lb_b])
    kb.op("dve", "tensor_scalar", out=G.oml[:], in0=G.lb[:], scalar1=-1.0, scalar2=1.0, op0=ALU.mult, op1=ALU.add, R=[lb_b], W=[G.const_b])
    kb.op("dve", "memset", G.rmask[:], 1.0, W=[G.const_b])
    kb.op("dve", "memset", G.rmask[:].rearrange("p (c j) -> p c j", j=64)[:, :, 0:1], 0.0, W=[G.const_b])
    for i in range(WTOTAL // CAST_CH):
        kb.dma("pool", G.wbf[i * CAST_CH:(i + 1) * CAST_CH].rearrange("(p n) -> p n", p=128),
               wsrc[i * CAST_CH:(i + 1) * CAST_CH].rearrange("(p n) -> p n", p=128))
    kb.barrier()
    for l in range(nlayers):
        for S in seqs:
            if "p1" in phases:
                phase_p1(kb, G, S, l)
            if "p2" in phases:
                phase_p2(kb, G, S)
            if "p3" in phases:
                phase_p3(kb, G, S, l)
            if "p4" in phases:
                phase_p4(kb, G, S, l)
            if "p5" in phases:
                phase_p5(kb, G, S, l)
    kb.barrier()
    kb.es.close()
    return kb, seqs


def host_fnet_consts(L):
    N1 = L // 128
    fb = np.zeros((128, 4 * N1 + 256), np.float32)
    s1 = np.arange(N1)
    a1 = 2 * np.pi * np.outer(s1, s1) / N1
    fb[:N1, 0:N1] = np.cos(a1)
    fb[:N1, N1:2 * N1] = np.sin(a1)
    fb[:N1, 2 * N1:3 * N1] = -np.sin(a1)
    fb[:N1, 3 * N1:4 * N1] = np.cos(a1)
    s0 = np.arange(128)
    a2 = 2 * np.pi * np.outer(s0, s0) / 128.0
    fb[:, 4 * N1:4 * N1 + 128] = np.cos(a2) / np.sqrt(L)
    fb[:, 4 * N1 + 128:4 * N1 + 256] = -np.sin(a2) / np.sqrt(L)
    ff = np.zeros((128, 2 * N1), np.float32)
    at = 2 * np.pi * np.outer(s0, s1) / L
    ff[:, 0:N1] = np.cos(at)
    ff[:, N1:2 * N1] = np.sin(at)
    return fb.astype(NPBF), ff


def phase_p2(kb, G, S):
    nc = kb.nc
    L = S.L
    N1 = L // 128
    NT = L // T
    CH = 64
    cpb1 = 512 // (2 * N1)
    cpb2 = min(512 // N1, CH)
    with contextlib.ExitStack() as es:
        kb.nuid += 1
        uid = kb.nuid

        def sb(name, shape, dt):
            return es.enter_context(nc.sbuf_tensor(f"{name}_u{uid}", list(shape), dt))

        fb = sb("fn_fb", [128, 4 * N1 + 256], BF16)
        ff = sb("fn_ff", [128, 2 * N1], F32)
        cst_b = Buf()
        kb.dma("sp", fb[:], S.fnb[:, :], W=[cst_b])
        kb.dma("sp", ff[:], S.fnf[:, :], W=[cst_b])
        F1a = fb[0:N1, 0:2 * N1]
        F1b = fb[0:N1, 2 * N1:4 * N1]
        Fc2 = fb[:, 4 * N1:4 * N1 + 128]
        nFs2 = fb[:, 4 * N1 + 128:4 * N1 + 256]
        Tc = ff[:, 0:N1]
        Ts = ff[:, N1:2 * N1]
        zt = Rot([(sb(f"fn_zt{i}", [N1, 128, 256], BF16), Buf()) for i in range(1)])
        gp = sb("fn_gp", [128, CH, 2, N1], BF16)
        gp_b = Buf()
        yt = Rot([(sb(f"fn_yt{i}", [128, CH, N1], BF16), Buf()) for i in range(2)])
        tmp = Rot([(sb(f"fn_tmp{i}", [128, 4, 256], F32), Buf()) for i in range(3)])
        all_ucs = [sbuf_of(S, "ucs", t) for t in range(NT)]
        all_fn = [sbuf_of(S, "fn", t) for t in range(NT)]
        for g in range(4):
            z, z_b = zt.next()
            src = S.ucs[:, g * 256:(g + 1) * 256].rearrange("(s1 s0) c -> s1 s0 c", s0=128)
            kb.dma("sp", z[:], src, R=all_ucs, W=[z_b])
            for half in range(128 // CH):
                for c0 in range(0, CH, cpb1):
                    bk, bk_b = kb.bank()
                    groups = []
                    for j in range(cpb1):
                        ch = half * CH + c0 + j
                        groups.append((bk[:, j * 2 * N1:(j + 1) * 2 * N1],
                                       [(z[:, :, ch], F1a), (z[:, :, 128 + ch], F1b)]))
                    kb.mm_multi(groups, R=[z_b, cst_b], W=[bk_b])
                    bv = bk[:, 0:cpb1 * 2 * N1].rearrange("p (c s n) -> p c s n", c=cpb1, s=2)
                    tcb = Tc.unsqueeze(1).broadcast_to([128, cpb1, N1])
                    tsb = Ts.unsqueeze(1).broadcast_to([128, cpb1, N1])
                    tm, tm_b = tmp.next()
                    tv = tm[:].rearrange("p a b -> p (a b)")[:, 0:4 * cpb1 * N1].rearrange("p (a c n) -> p a c n", a=4, c=cpb1)
                    kb.op("dve", "tensor_tensor", out=tv[:, 0], in0=bv[:, :, 0, :], in1=tcb, op=ALU.mult, R=[bk_b, cst_b], W=[tm_b])
                    kb.op("dve", "tensor_tensor", out=tv[:, 1], in0=bv[:, :, 1, :], in1=tsb, op=ALU.mult, R=[bk_b, cst_b], W=[tm_b])
                    kb.op("dve", "tensor_tensor", out=tv[:, 2], in0=bv[:, :, 1, :], in1=tcb, op=ALU.mult, R=[bk_b, cst_b], W=[tm_b])
                    kb.op("dve", "tensor_tensor", out=tv[:, 3], in0=bv[:, :, 0, :], in1=tsb, op=ALU.mult, R=[bk_b, cst_b], W=[tm_b])
                    kb.op("pool", "tensor_tensor", out=gp[:, c0:c0 + cpb1, 0, :], in0=tv[:, 0], in1=tv[:, 1], op=ALU.subtract,
                          R=[tm_b], W=[gp_b])
                    kb.op("pool", "tensor_tensor", out=gp[:, c0:c0 + cpb1, 1, :], in0=tv[:, 2], in1=tv[:, 3], op=ALU.add,
                          R=[tm_b], W=[gp_b])
                y, y_b = yt.next()
                for i, c0 in enumerate(range(0, CH, cpb2)):
                    bk, bk_b = kb.bank()
                    kb.mm(bk[:, 0:cpb2 * N1], [(Fc2, gp[:, c0:c0 + cpb2, 0, :]), (nFs2, gp[:, c0:c0 + cpb2, 1, :])],
                          R=[gp_b, cst_b], W=[bk_b])
                    src_v = bk[:, 0:cpb2 * N1].rearrange("p (c n) -> p c n", c=cpb2)
                    if i % 2 == 0:
                        kb.op("act", "activation", out=y[:, c0:c0 + cpb2, :], in_=src_v, func=AF.Copy, R=[bk_b], W=[y_b])
                    else:
                        kb.op("dve", "tensor_copy", out=y[:, c0:c0 + cpb2, :], in_=src_v, R=[bk_b], W=[y_b])
                r0 = g * 128 + half * CH
                dst = S.fn[r0:r0 + CH, :].rearrange("c (k0 k1) -> k0 c k1", k1=N1)
                kb.dma("pool", dst, y[:], R=[y_b], W=all_fn)
        kb.barrier()


def phase_p3(kb, G, S, l):
    nc = kb.nc
    L = S.L
    NT = L // T
    H4 = 4
    with contextlib.ExitStack() as es:
        kb.nuid += 1
        uid = kb.nuid

        def sb(name, shape, dt):
            return es.enter_context(nc.sbuf_tensor(f"{name}_u{uid}", list(shape), dt))

        def rot(name, shape, dt, n):
            return Rot([(sb(f"{name}{i}", shape, dt), Buf()) for i in range(n)])

        zt = rot("h_z", [128, H4, T], F32, 1)
        qt = rot("h_q", [128, H4, T], BF16, 1)
        it = rot("h_i", [128, 4, 512], BF16, 2)
        gt = rot("h_g", [128, H4, T], BF16, 1)
        obin = rot("h_obin", [128, H4, T], F32, 1)
        Bs = (sb("h_Bs", [128, H4, T], F32), Buf())
        Blf = (sb("h_Blf", [128, H4, T], F32), Buf())
        Bkf = (sb("h_Bkf", [128, H4, T], F32), Buf())
        Bb = (sb("h_Bb", [128, H4, T], F32), Buf())
        Bg = (sb("h_Bg", [128, H4, T], F32), Buf())
        Be = (sb("h_Be", [128, H4, T], F32), Buf())
        Bgm = (sb("h_Bgm", [128, H4, T], F32), Buf())
        BE2 = (sb("h_BE2", [128, H4, T], F32), Buf())
        dec = rot("h_dec", [128, H4, 8], F32, 2)
        qd = rot("h_qd", [128, H4, T], BF16, 2)
        kd = rot("h_kd", [128, H4, T], BF16, 2)
        qe = rot("h_qe", [128, H4, T], BF16, 2)
        keT = rot("h_keT", [128, H4, T], BF16, 1)
        ketm = rot("h_ketm", [128, 4, 512], BF16, 2)
        atm = rot("h_atm", [128, 128], BF16, 6)
        ost = rot("h_ost", [128, H4, T], F32, 1)
        sq = (sb("h_sq", [128, H4, T], BF16), Buf())
        rs = (sb("h_rs", [128, H4, T], F32), Buf())
        sgt = (sb("h_sgt", [128, H4, T], F32), Buf())
        hout = rot("h_out", [128, H4, T], BF16, 1)
        St = [[sb(f"h_S{h}_{i}", [128, 128], F32) for i in range(2)] for h in range(H4)]
        Sb_ = [[sb(f"h_Sb{h}_{i}", [128, 128], BF16) for i in range(2)] for h in range(H4)]
        St_b = [[Buf(), Buf()] for h in range(H4)]
        Sb_b = [[Buf(), Buf()] for h in range(H4)]
        gain = G.vecs[:, l * NV + 26:l * NV + 27]

        def v4(t):
            return t[:].rearrange("p h (c j) -> p h c j", j=64)

        def hv(t, hh):
            return t[:, hh * 2:hh * 2 + 2, :]

        def pre_ops(d, t):
            t0 = t * T
            res = {}
            ops = []

            def o_load():
                z, z_b = zt.next()
                kb.dma("sp", z[:], fm_view(S.z, d * 512, H4, t0, T), R=[sbuf_of(S, "z", t)], W=[z_b])
                q, q_b = qt.next()
                kb.dma("sp", q[:], fm_view(S.qh, 0, H4, t0, T), R=[sbuf_of(S, "qh", t)], W=[q_b])
                iv, iv_b = it.next()
                kb.dma("sp", iv[:], S.ih[t0:t0 + T, :].rearrange("(b p) c -> p b c", p=128), R=[sbuf_of(S, "ih", t)], W=[iv_b])
                res.update(z=z, z_b=z_b, q=q, q_b=q_b, iv=iv, iv_b=iv_b)
            ops.append(o_load)

            def o1():
                kb.op("act", "activation", out=Bs[0][:], in_=res["z"][:], func=AF.Sigmoid, R=[res["z_b"]], W=[Bs[1]])
            ops.append(o1)

            def o2():
                for h in range(H4):
                    col = l * 8 + d * 4 + h
                    kb.op("dve", "tensor_scalar", out=Bs[0][:, h, :], in0=Bs[0][:, h, :], scalar1=G.oml[:, col:col + 1],
                          scalar2=G.lb[:, col:col + 1], op0=ALU.mult, op1=ALU.add, R=[G.const_b], W=[Bs[1]])
            ops.append(o2)

            def o3():
                kb.op("act", "activation", out=Blf[0][:], in_=Bs[0][:], func=AF.Ln, R=[Bs[1]], W=[Blf[1]])
                kb.op("pool", "tensor_scalar", out=Bkf[0][:], in0=Bs[0][:], scalar1=-1.0, scalar2=1.0, op0=ALU.mult, op1=ALU.add,
                      R=[Bs[1]], W=[Bkf[1]])
            ops.append(o3)

            def o4():
                for h in range(H4):
                    kb.op("dve", "tensor_tensor_scan", out=Bb[0][:, h, :], data0=G.rmask[:], data1=Blf[0][:, h, :], initial=0.0,
                          op0=ALU.mult, op1=ALU.add, R=[Blf[1], G.const_b], W=[Bb[1]])
            ops.append(o4)

            def o5():
                b4 = v4(Bb[0])
                bend = b4[:, :, :, 63:64]
                bend_bc = bend.broadcast_to([128, H4, 8, 64])
                dc, dc_b = dec.next()
                kb.op("act", "activation", out=dc[:].rearrange("p h (c o) -> p h c o", o=1), in_=bend, func=AF.Exp, R=[Bb[1]], W=[dc_b])
                res.update(dc=dc, dc_b=dc_b)
                if d == 0:
                    res["gbuf"] = Bb
                    kb.op("pool", "tensor_tensor", out=v4(Be[0]), in0=bend_bc, in1=b4, op=ALU.subtract, R=[Bb[1]], W=[Be[1]])
                else:
                    res["gbuf"] = Bg
                    kb.op("dve", "tensor_tensor", out=v4(Bg[0]), in0=bend_bc, in1=b4, op=ALU.subtract, R=[Bb[1]], W=[Bg[1]])
                    kb.op("dve", "tensor_tensor", out=Bg[0][:], in0=Bg[0][:], in1=Blf[0][:], op=ALU.add, R=[Blf[1]], W=[Bg[1]])
                    kb.op("pool", "tensor_tensor", out=Be[0][:], in0=Bb[0][:], in1=Blf[0][:], op=ALU.subtract, R=[Bb[1], Blf[1]], W=[Be[1]])
            ops.append(o5)

            def o6():
                gbuf = res["gbuf"]
                g4 = v4(gbuf[0])
                gmid_bc = g4[:, :, :, 32:33].broadcast_to([128, H4, 8, 64])
                kb.op("dve", "tensor_tensor", out=v4(Bgm[0]), in0=g4, in1=gmid_bc, op=ALU.subtract, R=[gbuf[1]], W=[Bgm[1]])
                kb.op("act", "activation", out=BE2[0][:], in_=Bgm[0][:], func=AF.Exp, scale=-1.0, R=[Bgm[1]], W=[BE2[1]])
                kb.op("act", "activation", out=Bgm[0][:], in_=Bgm[0][:], func=AF.Exp, R=[Bgm[1]], W=[Bgm[1]])
            ops.append(o6)

            def o7():
                qd_t, qd_b = qd.next()
                kd_t, kd_b = kd.next()
                q, q_b = res["q"], res["q_b"]
                kb.op("dve", "tensor_tensor", out=qd_t[:], in0=q[:], in1=Bgm[0][:], op=ALU.mult, R=[q_b, Bgm[1]], W=[qd_b])
                kb.op("dve", "tensor_tensor", out=hv(kd_t, 0), in0=hv(Bkf[0], 0), in1=hv(BE2[0], 0), op=ALU.mult, R=[Bkf[1], BE2[1]], W=[kd_b])
                kb.op("pool", "tensor_tensor", out=hv(kd_t, 1), in0=hv(Bkf[0], 1), in1=hv(BE2[0], 1), op=ALU.mult, R=[Bkf[1], BE2[1]], W=[kd_b])
                res.update(qd_t=qd_t, qd_b=qd_b, kd_t=kd_t, kd_b=kd_b)
            ops.append(o7)

            def o8():
                gbuf = res["gbuf"]
                kb.op("act", "activation", out=BE2[0][:], in_=gbuf[0][:], func=AF.Exp, R=[gbuf[1]], W=[BE2[1]])
                kb.op("act", "activation", out=Be[0][:], in_=Be[0][:], func=AF.Exp, R=[Be[1]], W=[Be[1]])
            ops.append(o8)

            def o9():
                qe_t, qe_b = qe.next()
                ke_t, ke_b = keT.next()
                q, q_b = res["q"], res["q_b"]
                kb.op("dve", "tensor_tensor", out=qe_t[:], in0=q[:], in1=BE2[0][:], op=ALU.mult, R=[q_b, BE2[1]], W=[qe_b])
                kb.op("dve", "tensor_tensor", out=hv(ke_t, 0), in0=hv(Bkf[0], 0), in1=hv(Be[0], 0), op=ALU.mult, R=[Bkf[1], Be[1]], W=[ke_b])
                kb.op("pool", "tensor_tensor", out=hv(ke_t, 1), in0=hv(Bkf[0], 1), in1=hv(Be[0], 1), op=ALU.mult, R=[Bkf[1], Be[1]], W=[ke_b])
                res.update(qe_t=qe_t, qe_b=qe_b, ke_t=ke_t, ke_b=ke_b)
            ops.append(o9)

            def o10():
                km, km_b = ketm.next()
                ke_t, ke_b = res["ke_t"], res["ke_b"]
                for tb in range(4):
                    bk, bk_b = kb.bank()
                    kb.mm_multi([(bk[:, h * 128:(h + 1) * 128], [(ke_t[:, h, tb * 128:(tb + 1) * 128], G.ident)]) for h in range(H4)],
                                R=[ke_b, G.const_b], W=[bk_b])
                    if tb % 2 == 0:
                        kb.op("act", "activation", out=km[:, tb, :], in_=bk[:, 0:512], func=AF.Copy, R=[bk_b], W=[km_b])
                    else:
                        kb.op("dve", "tensor_copy", out=km[:, tb, :], in_=bk[:, 0:512], R=[bk_b], W=[km_b])
                res.update(km=km, km_b=km_b)
            ops.append(o10)
            return ops, res

        def block_ops(d, t, res, cur):
            t0 = t * T
            ops = []
            st = {}

            def b_init():
                o, o_b = ost.next()
                st.update(o=o, o_b=o_b)
                if d == 0:
                    gg, gg_b = gt.next()
                    kb.dma("sp", gg[:], fm_view(S.gh, 0, H4, t0, T), R=[sbuf_of(S, "gh", t)], W=[gg_b])
                    obi, obi_b = obin.next()
                    kb.dma("sp", obi[:], fm_view(S.ob, 0, H4, t0, T), R=[sbuf_of(S, "ob", t)], W=[obi_b])
                    res.update(gg=gg, gg_b=gg_b, obi=obi, obi_b=obi_b)
            ops.append(b_init)
            blocks = [0, 1, 2, 3] if d == 0 else [3, 2, 1, 0]
            corder = (0, 1) if d == 0 else (1, 0)
            for tb in blocks:
                bst = {}

                def blkA(tb=tb, bst=bst):
                    tsl = slice(tb * 128, (tb + 1) * 128)
                    ats = []
                    for h in range(H4):
                        bkA, bkA_b = kb.bank()
                        kb.mm(bkA[:, 0:128], [(res["kd_t"][:, h, tsl], res["qd_t"][:, h, tsl])], R=[res["kd_b"], res["qd_b"]], W=[bkA_b])
                        a, a_b = atm.next()
                        kb.op("dve", "tensor_tensor", out=a[:], in0=bkA[:, 0:128], in1=G.hmask[d], op=ALU.mult,
                              R=[bkA_b, G.const_b], W=[a_b])
                        ats.append((a, a_b))
                    bst["ats"] = ats
                ops.append(blkA)
                for step, ci in enumerate(corder):
                    def blkS(tb=tb, bst=bst, ci=ci):
                        iv, iv_b = res["iv"], res["iv_b"]
                        km, km_b = res["km"], res["km_b"]
                        qe_t, qe_b = res["qe_t"], res["qe_b"]
                        dc, dc_b = res["dc"], res["dc_b"]
                        csl = slice(ci * 64, (ci + 1) * 64)
                        for h in range(H4):
                            a, a_b = bst["ats"][h]
                            bkO, bkO_b = kb.psum_banks[h]
                            c_ = cur[h]
                            kb.mm(bkO[:, ci * 64:(ci + 1) * 64],
                                  [(iv[:, tb, h * 128:(h + 1) * 128], a[:, csl]),
                                   (Sb_[h][c_][:], qe_t[:, h, tb * 128 + ci * 64:tb * 128 + (ci + 1) * 64])],
                                  R=[iv_b, a_b, Sb_b[h][c_], qe_b], W=[bkO_b])
                            bkS, bkS_b = kb.bank()
                            kb.mm(bkS[:, 0:128], [(km[csl, tb, h * 128:(h + 1) * 128], iv[csl, tb, h * 128:(h + 1) * 128])],
                                  R=[km_b, iv_b], W=[bkS_b])
                            n_ = 1 - c_
                            kb.op("dve", "scalar_tensor_tensor", out=St[h][n_][:], in0=St[h][c_][:],
                                  scalar=dc[:, h, tb * 2 + ci:tb * 2 + ci + 1], in1=bkS[:, 0:128], op0=ALU.mult, op1=ALU.add,
                                  R=[St_b[h][c_], dc_b, bkS_b], W=[St_b[h][n_]])
                            kb.op("act", "activation", out=Sb_[h][n_][:], in_=St[h][n_][:], func=AF.Copy, R=[St_b[h][n_]], W=[Sb_b[h][n_]])
                            cur[h] = n_
                    ops.append(blkS)

                def blkE(tb=tb):
                    o, o_b = st["o"], st["o_b"]
                    tsl = slice(tb * 128, (tb + 1) * 128)
                    for h in range(H4):
                        bkO, bkO_b = kb.psum_banks[h]
                        if d == 1:
                            kb.op("act", "activation", out=o[:, h, tsl], in_=bkO[:, 0:128], func=AF.Copy, R=[bkO_b], W=[o_b])
                        else:
                            obi, obi_b = res["obi"], res["obi_b"]
                            kb.op("dve", "tensor_tensor", out=o[:, h, tsl], in0=bkO[:, 0:128], in1=obi[:, h, tsl], op=ALU.add,
                                  R=[bkO_b, obi_b], W=[o_b])
                ops.append(blkE)

            def fin():
                o, o_b = st["o"], st["o_b"]
                if d == 1:
                    kb.dma("pool", fm_view(S.ob, 0, H4, t0, T), o[:], R=[o_b], W=[sbuf_of(S, "ob", t)])
                else:
                    gg, gg_b = res["gg"], res["gg_b"]
                    kb.op("pool", "tensor_tensor", out=sq[0][:], in0=o[:], in1=o[:], op=ALU.mult, R=[o_b], W=[sq[1]])
                    for h in range(H4):
                        bk, bk_b = kb.bank()
                        kb.mm(bk[:, 0:T], [(G.ones, sq[0][:, h, :])], R=[sq[1], G.const_b], W=[bk_b])
                        kb.op("act", "activation", out=rs[0][:, h, :], in_=bk[:, 0:T], func=AF.Ln, bias=G.epsD[:], scale=1.0 / 128,
                              R=[bk_b, G.const_b], W=[rs[1]])
                    kb.op("act", "activation", out=rs[0][:], in_=rs[0][:], func=AF.Exp, scale=-0.5, R=[rs[1]], W=[rs[1]])
                    kb.op("act", "activation", out=sgt[0][:], in_=gg[:], func=AF.Silu, R=[gg_b], W=[sgt[1]])
                    kb.op("pool", "tensor_tensor", out=rs[0][:], in0=rs[0][:], in1=o[:], op=ALU.mult, R=[o_b], W=[rs[1]])
                    ho, ho_b = hout.next()
                    kb.op("dve", "scalar_tensor_tensor", out=ho[:], in0=rs[0][:], scalar=gain, in1=sgt[0][:], op0=ALU.mult, op1=ALU.mult,
                          R=[rs[1], sgt[1], G.const_b], W=[ho_b])
                    kb.dma("pool", fm_view(S.hg, 0, H4, t0, T), ho[:], R=[ho_b], W=[sbuf_of(S, "hg", t)])
            ops.append(fin)
            return ops

        kb.bank_range = (4, 8)
        for d in (1, 0):
            cur = [0] * H4
            for h in range(H4):
                kb.op("dve", "memset", St[h][0][:], 0.0, W=[St_b[h][0]])
                kb.op("pool", "memset", Sb_[h][0][:], 0.0, W=[Sb_b[h][0]])
            tiles = list(range(NT))
            if d == 1:
                tiles = tiles[::-1]
            pops, pres = pre_ops(d, tiles[0])
            for f in pops:
                f()
            for i, t in enumerate(tiles):
                bops = block_ops(d, t, pres, cur)
                if i + 1 < len(tiles):
                    nops, nres = pre_ops(d, tiles[i + 1])
                else:
                    nops, nres = [], None
                nb = len(bops)
                per = (len(nops) + nb - 2) // max(1, nb - 1) if nops else 0
                import os
                if os.environ.get('P3_NOIL'):
                    per = 0
                pi = 0
                for bi, f in enumerate(bops):
                    f()
                    if bi < nb - 1:
                        for _ in range(per):
                            if pi < len(nops):
                                nops[pi]()
                                pi += 1
                while pi < len(nops):
                    nops[pi]()
                    pi += 1
                pres = nres
            kb.barrier()
        kb.bank_range = (0, 8)


AW = 2048
AGROUPS = ((1, 0), (4, 1), (16, 2))


def host_rope(L):
    half = HD // 2
    inv = (10000.0 ** (-np.arange(half, dtype=np.float32) / half)).astype(np.float32)
    pos = np.arange(L, dtype=np.float32)
    ang = (pos[None, :] * inv[:, None]).astype(np.float32)
    fi = (np.arange(128) % 64) % 32
    return np.stack([np.cos(ang)[fi], np.sin(ang)[fi]], axis=1).astype(NPBF)


def pipeline(stages_list, look=1):
    n = len(stages_list)
    for i in range(min(look, n)):
        stages_list[i][0]()
    for i in range(n):
        if i + look < n:
            stages_list[i + look][0]()
        stages_list[i][1]()


def pipeline_n(items, skew=1):
    n = len(items)
    if n == 0:
        return
    k = len(items[0])
    for step in range(n + (k - 1) * skew):
        for s_ in range(k):
            i = step - s_ * skew
            if 0 <= i < n:
                items[i][s_]()


def phase_p4(kb, G, S, l):
    nc = kb.nc
    L = S.L
    W = AW
    NW = L // W
    with contextlib.ExitStack() as es:
        kb.nuid += 1
        uid = kb.nuid

        def sb(name, shape, dt):
            return es.enter_context(nc.sbuf_tensor(f"{name}_u{uid}", list(shape), dt))

        NQC = W // 512
        KWID = [W + 128 * r for r, g in AGROUPS]
        KOFF = [0, KWID[0], KWID[0] + KWID[1]]
        qz = sb("a_qz", [128, 3, 2, W], BF16)
        qz_b = [[Buf() for c in range(NQC)] for g in range(3)]
        qraw = Rot([(sb(f"a_qraw{i}", [128, 512], BF16), Buf()) for i in range(3)])
        kn = sb("a_kcm", [128, sum(KWID)], BF16)
        kn_b = [[Buf() for c in range((KWID[g] + 511) // 512)] for g in range(3)]
        kraw = Rot([(sb(f"a_kraw{i}", [128, 512], BF16), Buf()) for i in range(3)])
        nblk = [r * (W // r // 128 + 1) for r, g in AGROUPS]
        goff = [0, nblk[0], nblk[0] + nblk[1]]
        vt = (sb("a_vt", [128, sum(nblk), 512], BF16), Buf())
        vt_gc = [[Buf() for c in range(r)] for r, g in AGROUPS]
        cs = (sb("a_cs", [128, 2, W + 2048], BF16), Buf())
        acc = (sb("a_acc", [128, 2, W], F32), Buf())
        rec = Rot([(sb(f"a_rec{i}", [128, 512], F32), Buf()) for i in range(2)])
        aout = Rot([(sb(f"a_out{i}", [128, W], BF16), Buf()) for i in range(1)])
        ND_ = 4
        sqt = Rot([(sb(f"a_sq{i}", [128, 512], BF16), Buf()) for i in range(ND_)])
        y1t = Rot([(sb(f"a_y1{i}", [128, 512], BF16), Buf()) for i in range(ND_)])
        y2t = Rot([(sb(f"a_y2{i}", [128, 512], BF16), Buf()) for i in range(ND_)])
        rst = Rot([(sb(f"a_rs{i}", [128, 512], F32), Buf()) for i in range(ND_)])
        ptt = Rot([(sb(f"a_pt{i}", [128, 512], BF16), Buf()) for i in range(4)])
        qgain = G.vecs[:, l * NV + 24:l * NV + 25]
        kgain = G.vecs[:, l * NV + 25:l * NV + 26]
        kb.op("pool", "memset", qz[:], 0.0, W=[b for bb in qz_b for b in bb])

        blkg = sb("a_blkg", [128, 2, 128], BF16)
        ginv = sb("a_ginv", [128, 2], F32)
        blkg_b = Buf()
        kb.op("dve", "tensor_tensor", out=ginv[:], in0=G.vecs[:, l * NV + 24:l * NV + 26], in1=G.vecs[:, l * NV + 24:l * NV + 26],
              op=ALU.mult, R=[G.const_b], W=[blkg_b])
        kb.op("dve", "reciprocal", out=ginv[:], in_=ginv[:], W=[blkg_b])
        for i in range(2):
            kb.op("dve", "tensor_scalar", out=blkg[:, i, :], in0=G.blk, scalar1=ginv[:, i:i + 1], scalar2=None, op0=ALU.mult,
                  R=[G.const_b], W=[blkg_b])

        def nrope_stages(x, xb, c0, n, which, outs=None, outb=None, pre=None, r=1):
            st = {}

            def S1():
                if pre is not None:
                    pre()
                s_, s_b = sqt.next()
                kb.op("pool", "tensor_tensor", out=s_[:, 0:n], in0=x, in1=x, op=ALU.mult, R=[xb], W=[s_b])
                y1, y1_b = y1t.next()
                y2, y2_b = y2t.next()
                kb.op("dve", "tensor_tensor", out=y1[:, 0:n], in0=x, in1=cs[0][:, 0, c0:c0 + n], op=ALU.mult, R=[xb, cs[1]], W=[y1_b])
                kb.op("pool", "tensor_tensor", out=y2[:, 0:n], in0=x, in1=cs[0][:, 1, c0:c0 + n], op=ALU.mult, R=[xb, cs[1]], W=[y2_b])
                st.update(s_=s_, s_b=s_b, y1=y1, y1_b=y1_b, y2=y2, y2_b=y2_b)

            def S2():
                b1, b1_b = kb.bank()
                kb.mm(b1[:, 0:n], [(blkg[:, which, :], st["s_"][:, 0:n])], R=[st["s_b"], blkg_b], W=[b1_b])
                b2, b2_b = kb.bank()
                kb.mm(b2[:, 0:n], [(G.ident, st["y1"][:, 0:n]), (G.rot, st["y2"][:, 0:n])], R=[st["y1_b"], st["y2_b"], G.const_b], W=[b2_b])
                st.update(b1=b1, b1_b=b1_b, b2=b2, b2_b=b2_b)

            def S3():
                r_, r_b = rst.next()
                kb.op("act", "activation", out=r_[:, 0:n], in_=st["b1"][:, 0:n], func=AF.Ln, bias=G.epsD[:], scale=1.0 / HD,
                      R=[st["b1_b"], G.const_b], W=[r_b])
                kb.op("act", "activation", out=r_[:, 0:n], in_=r_[:, 0:n], func=AF.Exp, scale=-0.5, R=[r_b], W=[r_b])
                st.update(r_=r_, r_b=r_b)

            def S4():
                b2, b2_b, r_, r_b = st["b2"], st["b2_b"], st["r_"], st["r_b"]
                for i, (dst, ps) in enumerate(outs):
                    kb.op("dve", "tensor_tensor", out=dst, in0=b2[ps, 0:n].rearrange("p (n r) -> p r n", r=r),
                          in1=r_[ps, 0:n].rearrange("p (n r) -> p r n", r=r), op=ALU.mult, R=[b2_b, r_b], W=[outb])
            return (S1, S2, S3, S4)

        for w in range(NW):
            w0 = w * W
            first_w = (w == 0)
            last_w = (w == NW - 1)
            lo = max(0, w0 - 1024)
            hi = min(L, w0 + W + 1024)
            kb.dma("sp", cs[0][:, :, lo - (w0 - 1024):hi - (w0 - 1024)], S.rope[:, :, lo:hi], W=[cs[1]])
            qk_t = [sbuf_of(S, "qk", t) for t in range(lo // T, hi // T)]
            v_t = [sbuf_of(S, "v", t) for t in range(lo // T, hi // T)]
            for gi, (r, g) in enumerate(AGROUPS):
                nbw = W // r // 128
                N = 128 * (nbw + 1)
                n_lo = 64 if first_w else 0
                n_hi = N - 64 if last_w else N
                rows = S.v[w0 - 64 * r + r * n_lo:w0 - 64 * r + r * n_hi, g * 512:(g + 1) * 512]
                rv = rows.rearrange("(n r) c -> r n c", r=r)
                for c in range(r):
                    b0 = goff[gi] + c * (nbw + 1)
                    n = n_lo
                    if n_lo > 0:
                        kb.op("pool", "memset", vt[0][0:64, b0, :], 0.0, W=[vt_gc[gi][c]])
                        kb.dma("sp", vt[0][64:128, b0, :], rv[c, 0:64, :], R=v_t, W=[vt_gc[gi][c]])
                        n = 128
                    m_lo = n // 128
                    m_hi = n_hi // 128
                    if m_hi > m_lo:
                        src = rv[c, m_lo * 128 - n_lo:m_hi * 128 - n_lo, :].rearrange("(m p) c -> p m c", p=128)
                        kb.dma("sp", vt[0][:, b0 + m_lo:b0 + m_hi, :], src, R=v_t, W=[vt_gc[gi][c]])
                    if n_hi % 128:
                        kb.op("pool", "memset", vt[0][64:128, b0 + nbw, :], 0.0, W=[vt_gc[gi][c]])
                        kb.dma("sp", vt[0][0:64, b0 + nbw, :], rv[c, m_hi * 128 - n_lo:m_hi * 128 - n_lo + 64, :], R=v_t, W=[vt_gc[gi][c]])
            for hp in range(4):
                kval = []
                for r, g in AGROUPS:
                    j_lo = 64 * r if first_w else 0
                    j_hi = W + 64 * r if last_w else W + 128 * r
                    kval.append((j_lo, j_hi))
                    kcv = kn[:, KOFF[g]:KOFF[g] + KWID[g]].rearrange("p (r n) -> p r n", r=r)
                    npc = KWID[g] // r
                    if j_lo > 0:
                        kb.op("pool", "memset", kcv[:, :, 0:64], 0.0, W=kn_b[g])
                    if j_hi < KWID[g]:
                        kb.op("pool", "memset", kcv[:, :, npc - 64:npc], 0.0, W=kn_b[g])
                stages = []
                for r, g in AGROUPS:
                    qcv = qz[:, g, :, :].rearrange("p s (r n) -> p s r n", r=r)
                    for ci, c0 in enumerate(range(0, W, 512)):
                        qx, qx_b = qraw.next()

                        def pre(qx=qx, qx_b=qx_b, g=g, c0=c0):
                            kb.dma("sp", qx[:], S.qk[g * 512 + hp * 128:g * 512 + hp * 128 + 128, w0 + c0:w0 + c0 + 512], R=qk_t, W=[qx_b])
                        n0 = c0 // r
                        stages.append(nrope_stages(qx[:], qx_b, 1024 + c0, 512, 0,
                                                   outs=[(qcv[0:64, 0, :, n0:n0 + 512 // r], slice(0, 64)),
                                                         (qcv[64:128, 1, :, n0:n0 + 512 // r], slice(64, 128))],
                                                   outb=qz_b[g][ci], pre=pre, r=r))
                    j_lo, j_hi = kval[g]
                    kcv = kn[:, KOFF[g]:KOFF[g] + KWID[g]].rearrange("p (r n) -> p r n", r=r)
                    row0 = 1536 + g * 512 + hp * 128
                    for ci in range(len(kn_b[g])):
                        c0 = max(j_lo, ci * 512)
                        c1 = min(j_hi, (ci + 1) * 512)
                        if c1 > c0:
                            kx, kx_b = kraw.next()
                            nn = c1 - c0

                            def prek(kx=kx, kx_b=kx_b, row0=row0, c0=c0, nn=nn, r=r):
                                kb.dma("sp", kx[:, 0:nn], S.qk[row0:row0 + 128, w0 - 64 * r + c0:w0 - 64 * r + c0 + nn], R=qk_t, W=[kx_b])
                            stages.append(nrope_stages(kx[:, 0:nn], kx_b, 1024 - 64 * r + c0, nn, 1,
                                                       outs=[(kcv[:, :, c0 // r:(c0 + nn) // r], slice(0, 128))],
                                                       outb=kn_b[g][ci], pre=prek, r=r))
                pipeline_n(stages, skew=1)
                units = []
                for gi, (r, g) in enumerate(AGROUPS):
                    nbw = W // r // 128
                    nb_total = L // r // 128
                    qv = qz[:, g, :, :].rearrange("p s (r n) -> p s r n", r=r)
                    kv = kn[:, KOFF[g]:KOFF[g] + KWID[g]].rearrange("p (r n) -> p r n", r=r)
                    accv = acc[0][:].rearrange("p s (n r) -> p s r n", r=r)
                    for c in range(r):
                        b0 = goff[gi] + c * (nbw + 1)
                        for bq in range(nbw):
                            mA = (w0 // r) // 128 + bq
                            variant = (1 if mA == 0 else 0) + (2 if mA + 1 == nb_total else 0)
                            st = {}
                            is_first = (c == 0 and bq == 0)
                            is_last = (c == r - 1 and bq == nbw - 1)

                            def A(st=st, g=g, qv=qv, kv=kv, c=c, bq=bq, variant=variant):
                                bk, bk_b = kb.bank()
                                qs = qv[:, :, c, 128 * bq:128 * bq + 128]
                                kb.mm_raw([(bk[:, 0:256], kv[:, c, 128 * bq:128 * bq + 128], qs, True, False),
                                           (bk[:, 256:512], kv[:, c, 128 * (bq + 1):128 * (bq + 1) + 128], qs, False, False),
                                           (bk[:, 0:512], G.ident, G.amask[variant], False, True)],
                                          R=qz_b[g] + kn_b[g] + [G.const_b], W=[bk_b])
                                pt, pt_b = ptt.next()
                                kb.op("act", "activation", out=pt[:], in_=bk[:, 0:512], func=AF.Exp, scale=float(HD) ** -0.5, R=[bk_b], W=[pt_b])
                                st.update(pt=pt, pt_b=pt_b)

                            def B(st=st, gi=gi, accv=accv, c=c, bq=bq, b0=b0, is_first=is_first, is_last=is_last, hp=hp):
                                pt, pt_b = st["pt"], st["pt_b"]
                                b2, b2_b = kb.bank()
                                vc = slice(hp * 128, (hp + 1) * 128)
                                kb.mm_multi([(b2[:, 0:256], [(vt[0][:, b0 + bq, vc], pt[:, 0:256]), (vt[0][:, b0 + bq + 1, vc], pt[:, 256:512])]),
                                             (b2[:, 256:512], [(G.ones, pt[:, 0:256]), (G.ones, pt[:, 256:512])])],
                                            R=[pt_b, vt_gc[gi][c], G.const_b], W=[b2_b])
                                tok = None
                                for h in range(2):
                                    hs = slice(h * 64, (h + 1) * 64)
                                    av = accv[hs, :, c, 128 * bq:128 * bq + 128]
                                    b2v = b2[hs, 0:512].rearrange("p (s h n) -> p s h n", s=2, h=2)[:, :, h, :]
                                    Wl = [acc[1]] if (is_first and h == 0) else []
                                    if gi == 0:
                                        tok = kb.op("dve", "tensor_copy", out=av, in_=b2v, R=[b2_b], W=Wl)
                                    else:
                                        tok = kb.op("dve", "tensor_tensor", out=av, in0=av, in1=b2v, op=ALU.add, R=[b2_b], W=Wl)
                                if is_last:
                                    acc[1].w = tok
                                    acc[1].r = {}
                            units.append((A, B))
                pipeline(units, look=2)
                ao, ao_b = aout.next()
                for c0 in range(0, W, 512):
                    rc, rc_b = rec.next()
                    kb.op("act", "activation", out=rc[:], in_=acc[0][:, 1, c0:c0 + 512], func=AF.Ln, R=[acc[1]], W=[rc_b])
                    kb.op("act", "activation", out=rc[:], in_=rc[:], func=AF.Exp, scale=-1.0, R=[rc_b], W=[rc_b])
                    kb.op("dve", "tensor_tensor", out=ao[:, c0:c0 + 512], in0=acc[0][:, 0, c0:c0 + 512], in1=rc[:], op=ALU.mult,
                          R=[acc[1], rc_b], W=[ao_b])
                kb.dma("pool", S.att[hp * 128:(hp + 1) * 128, w0:w0 + W], ao[:], R=[ao_b],
                       W=[sbuf_of(S, "att", t) for t in range(w0 // T, (w0 + W) // T)])
        kb.barrier()


LP = 16384
LS = 2048
NCORES = 8
_CACHE = {}


def kernel(**inputs):
    inputs = {k: np.asarray(v) for k, v in inputs.items()}
    xp = inputs["x_prompt"]
    xs = inputs["x_sample"]
    if "prog" not in _CACHE:
        _CACHE["prog"] = build(LP, LS)
    kb, seqs = _CACHE["prog"]
    wsrc = host_pack_weights(inputs)
    vecs = host_vecs(inputs)
    cbf = host_consts()
    lbl = host_lbl(inputs)
    pfb, pff = host_fnet_consts(LP)
    sfb, sff = host_fnet_consts(LS)
    prope = host_rope(LP)
    srope = host_rope(LS)
    zeros_p = np.zeros((D, LP), np.float32)
    in_maps = []
    for c in range(NCORES):
        px = np.ascontiguousarray(xp[c].T) if c < xp.shape[0] else zeros_p
        in_maps.append({
            "wsrc": wsrc, "vecs": vecs, "cbf": cbf, "lbl": lbl,
            "p_x": px, "s_x": np.ascontiguousarray(xs[c].T),
            "p_fnb": pfb, "p_fnf": pff, "s_fnb": sfb, "s_fnf": sff,
            "p_rope": prope, "s_rope": srope,
        })
    res = run_bass_kernel_spmd(kb.nc, in_maps, core_ids=list(range(NCORES)))
    yp = np.stack([np.asarray(res.results[c]["p_y"]).T for c in range(xp.shape[0])], axis=0).astype(np.float32)
    ys = np.stack([np.asarray(res.results[c]["s_y"]).T for c in range(NCORES)], axis=0).astype(np.float32)
    return (np.ascontiguousarray(yp), np.ascontiguousarray(ys))
```

```python
import contextlib
import numpy as np
import ml_dtypes
import concourse.bass as bass
import concourse.mybir as mybir
from concourse.bass_utils import run_bass_kernel_spmd

F32 = mybir.dt.float32
BF16 = mybir.dt.bfloat16
AF = mybir.ActivationFunctionType
ALU = mybir.AluOpType
NPBF = ml_dtypes.bfloat16

D = 1024
DFF = 2816
DEPTH = 2
KC = D // 128
FC = DFF // 128
EPS = 1e-6
T = 512
HD = 64
NQK = 3072
O_Q, O_K, O_V, O_ZF, O_ZB, O_QH, O_IH, O_GH, O_UF, O_GL = 0, 1536, 3072, 4608, 5120, 5632, 6144, 6656, 7168, 7680
IN_COLS = 10752
GW = 256
GWT = 512


class Buf:
    __slots__ = ("name", "w", "r")

    def __init__(self, name=""):
        self.name = name
        self.w = None
        self.r = {}


class Eng:
    def __init__(self, name, obj, sem):
        self.name, self.obj, self.sem = name, obj, sem
        self.cnt = 0
        self.waited = {}


class KB:
    def __init__(self):
        self.nc = bass.Bass("TRN2", target_bir_lowering=False)
        self.es = contextlib.ExitStack()
        nc = self.nc
        self.eng = {}
        for nm, obj in (("pe", nc.tensor), ("act", nc.scalar), ("dve", nc.vector),
                        ("pool", nc.gpsimd), ("sp", nc.sync)):
            sem = self.es.enter_context(nc.semaphore("s_" + nm))
            self.eng[nm] = Eng(nm, obj, sem)
        self.ND = 12
        self.dslots = {}
        self.dcnt = {}
        for q in ("sp", "pool", "act"):
            self.dslots[q] = [[self.es.enter_context(nc.semaphore(f"d_{q}{i}")), 0] for i in range(self.ND)]
            self.dcnt[q] = 0
        self.nuid = 0
        self.psum_banks = []
        self.psum_i = 0
        self.bank_range = (0, 8)

    def sbuf(self, name, shape, dtype):
        return self.es.enter_context(self.nc.sbuf_tensor(name, list(shape), dtype))

    def psum(self, name, shape, dtype):
        return self.es.enter_context(self.nc.psum_tensor(name, list(shape), dtype))

    def dram(self, name, shape, dtype, kind="Internal"):
        return self.nc.dram_tensor(name, list(shape), dtype, kind=kind).ap()

    def _wait(self, eng, toks):
        best = {}
        for sem, val in toks:
            k = id(sem)
            if k not in best or best[k][1] < val:
                best[k] = (sem, val)
        for k, (sem, val) in best.items():
            if eng.waited.get(k, 0) >= val:
                continue
            if sem is eng.sem and eng.name == "pe":
                continue
            eng.obj.wait_ge(sem, val)
            eng.waited[k] = val

    @staticmethod
    def _deps(R, W):
        toks = []
        for b in R:
            if b.w is not None:
                toks.append(b.w)
        for b in W:
            if b.w is not None:
                toks.append(b.w)
            toks.extend(b.r.values())
        return toks

    @staticmethod
    def _commit(tok, R, W):
        k = id(tok[0])
        for b in R:
            b.r[k] = tok
        for b in W:
            b.w = tok
            b.r = {}

    def op(self, E, method, *args, R=(), W=(), **kw):
        eng = self.eng[E]
        self._wait(eng, self._deps(R, W))
        ins = getattr(eng.obj, method)(*args, **kw)
        eng.cnt += 1
        ins.then_inc(eng.sem, 1)
        tok = (eng.sem, eng.cnt)
        self._commit(tok, R, W)
        return tok

    def mm_raw(self, items, R=(), W=()):
        eng = self.eng["pe"]
        self._wait(eng, self._deps(R, W))
        ins = None
        for (out, l, r, st, sp) in items:
            ins = eng.obj.matmul(out, lhsT=l, rhs=r, start=st, stop=sp)
        eng.cnt += 1
        ins.then_inc(eng.sem, 1)
        tok = (eng.sem, eng.cnt)
        self._commit(tok, R, W)
        return tok

    def mm(self, out, pairs, R=(), W=(), Rper=None, **kw):
        eng = self.eng["pe"]
        self._wait(eng, self._deps(R, W))
        n = len(pairs)
        ins = None
        for i, (l, r) in enumerate(pairs):
            if Rper is not None:
                self._wait(eng, self._deps(Rper[i], ()))
            ins = eng.obj.matmul(out, lhsT=l, rhs=r, start=(i == 0), stop=(i == n - 1), **kw)
        eng.cnt += 1
        ins.then_inc(eng.sem, 1)
        tok = (eng.sem, eng.cnt)
        allR = list(R)
        if Rper is not None:
            for rr in Rper:
                allR += list(rr)
        self._commit(tok, allR, W)
        return tok

    def mm_multi(self, groups, R=(), W=()):
        eng = self.eng["pe"]
        self._wait(eng, self._deps(R, W))
        ins = None
        for out, pairs in groups:
            n = len(pairs)
            for i, (l, r) in enumerate(pairs):
                ins = eng.obj.matmul(out, lhsT=l, rhs=r, start=(i == 0), stop=(i == n - 1))
        eng.cnt += 1
        ins.then_inc(eng.sem, 1)
        tok = (eng.sem, eng.cnt)
        self._commit(tok, R, W)
        return tok

    def dma(self, q, out, in_, R=(), W=(), **kw):
        eng = self.eng[q]
        toks = self._deps(R, W)
        slot = self.dslots[q][self.dcnt[q] % self.ND]
        self.dcnt[q] += 1
        if slot[1] > 0:
            toks.append((slot[0], slot[1]))
        self._wait(eng, toks)
        ins = eng.obj.dma_start(out=out, in_=in_, **kw)
        slot[1] += 16
        ins.then_inc(slot[0], 16)
        tok = (slot[0], slot[1])
        self._commit(tok, R, W)
        return tok

    def barrier(self):
        toks = [(e.sem, e.cnt) for e in self.eng.values() if e.cnt > 0]
        for q in self.dslots:
            toks += [(sl[0], sl[1]) for sl in self.dslots[q] if sl[1] > 0]
        for e in self.eng.values():
            self._wait(e, toks)

    def bank(self):
        lo, hi = self.bank_range
        b = self.psum_banks[lo + self.psum_i % (hi - lo)]
        self.psum_i += 1
        return b


class Rot:
    def __init__(self, items):
        self.items = items
        self.i = 0

    def next(self):
        it = self.items[self.i % len(self.items)]
        self.i += 1
        return it


def weight_plan():
    tiles = []
    groups = {}

    def add(gname, pieces):
        groups.setdefault(gname, []).append(len(tiles))
        tiles.append(pieces)

    for f in ("ffn1", "ffn2"):
        for mt in range(DFF // GW):
            add(f + "_gu", [(f + "_w_gate", D, mt * GW, GW), (f + "_w_up", D, mt * GW, GW)])
        for m in range(KC):
            add(f + "_down", [(f + "_w_down", DFF, m * 128, 128)])
    for c0 in list(range(O_Q, O_V, 512)) + list(range(O_ZF, O_IH, 512)) + [O_GH, O_UF]:
        add("in_fm", [("w_in", D, c0, 512)])
    for c0 in list(range(O_V, O_ZF, 512)) + [O_IH]:
        add("in_tm", [("w_in", D, c0, 512)])
    for m in range(KC):
        add("p5m", [("w_in", D, O_GL + br * D + m * 128, 128) for br in range(3)]
            + [(w, 512, m * 128, 128) for w in ("w_att_out", "w_hgrn_out", "w_fnet_out")])
    for c0 in (0, 512):
        add("w_out", [("w_out", D, c0, 512)])
    sizes = [sum(k * g for (_, k, _, g) in t) for t in tiles]
    offs = list(np.cumsum([0] + sizes[:-1]))
    return tiles, groups, [int(o) for o in offs], sizes, int(sum(sizes))


WTILES, WGROUPS, WOFFS, WSIZES, WLAYER = weight_plan()
IN_FM_COLS = list(range(O_Q, O_V, 512)) + list(range(O_ZF, O_IH, 512)) + [O_GH, O_UF]
IN_TM_COLS = list(range(O_V, O_ZF, 512)) + [O_IH]
WSLOT = max(s // 128 for s in WSIZES)
CAST_CH = 128 * 8192
WTOTAL = ((DEPTH * WLAYER + CAST_CH - 1) // CAST_CH) * CAST_CH


def host_pack_weights(inputs):
    flat = np.zeros(WTOTAL, np.float32)
    ws = {k: np.asarray(v) for k, v in inputs.items() if k.startswith("w_") or "_w_" in k}
    for l in range(DEPTH):
        for ti, pieces in enumerate(WTILES):
            blks = []
            for (wname, kdim, c0, g) in pieces:
                w = ws[wname][l]
                kc = kdim // 128
                blks.append(w[:, c0:c0 + g].reshape(kc, 128, g).transpose(1, 0, 2).reshape(128, kc * g))
            blk = np.concatenate(blks, axis=1)
            o = l * WLAYER + WOFFS[ti]
            flat[o:o + blk.size] = blk.reshape(-1)
    return flat


class WStream:
    def __init__(self, kb, G, order):
        self.kb, self.G, self.order = kb, G, order
        self.pos = 0
        self.issued = 0

    def _issue(self, i):
        G = self.G
        l, ti = self.order[i]
        t, b = G.wslots[G.wslot_i % len(G.wslots)]
        G.wslot_i += 1
        n = WSIZES[ti] // 128
        o = l * WLAYER + WOFFS[ti]
        src = G.wbf[o:o + WSIZES[ti]].rearrange("(p n) -> p n", p=128)
        self.kb.dma("sp", t[:, 0:n], src, R=[G.wcast_b], W=[b])
        return (t, b)

    def get(self):
        depth = len(self.G.wslots)
        if self.pos == 0:
            self.live = []
        while self.issued < min(self.pos + depth, len(self.order)):
            self.live.append(self._issue(self.issued))
            self.issued += 1
        r = self.live[self.pos]
        self.pos += 1
        return r


class Glob:
    pass


def seq_scratch(kb, name, L, dbg):
    S = Glob()
    S.L = L
    S.name = name

    def mk(nm, shape, dt):
        kind = "Internal"
        if nm in dbg.get("out", ()):
            kind = "ExternalOutput"
        if nm in dbg.get("in", ()):
            kind = "ExternalInput"
        return kb.dram(f"{name}_{nm}", shape, dt, kind=kind)

    S.x = [kb.dram(f"{name}_x", [D, L], F32, kind="ExternalInput")]
    for l in range(1, DEPTH):
        S.x.append(mk(f"xl{l}", [D, L], F32))
    S.y = kb.dram(f"{name}_y", [D, L], F32, kind="ExternalOutput")
    S.x1 = mk("x1", [D, L], F32)
    S.qk = mk("qk", [NQK, L], BF16)
    S.v = mk("v", [L, 1536], BF16)
    S.z = mk("z", [1024, L], F32)
    S.qh = mk("qh", [512, L], BF16)
    S.gh = mk("gh", [512, L], BF16)
    S.ih = mk("ih", [L, 512], BF16)
    S.ucs = mk("ucs", [L, 1024], BF16)
    S.att = mk("att", [512, L], BF16)
    S.hg = mk("hg", [512, L], BF16)
    S.fn = mk("fn", [512, L], BF16)
    S.ob = mk("ob", [512, L], F32)
    S.fnb = kb.dram(f"{name}_fnb", [128, 4 * (L // 128) + 256], BF16, kind="ExternalInput")
    S.fnf = kb.dram(f"{name}_fnf", [128, 2 * (L // 128)], F32, kind="ExternalInput")
    S.rope = kb.dram(f"{name}_rope", [128, 2, L], BF16, kind="ExternalInput")
    S.bufs = {}
    return S


def sbuf_of(S, nm, t):
    k = (nm, t)
    if k not in S.bufs:
        S.bufs[k] = Buf(f"{S.name}_{nm}_{t}")
    return S.bufs[k]


def rmsnorm_a(kb, G, P, src, src_b):
    sq, sq_b = P.sqr.next()
    for c in range(KC):
        e = "pool" if c % 3 == 2 else "dve"
        kb.op(e, "tensor_tensor", out=sq[:, c, :], in0=src[:, c, :], in1=src[:, c, :], op=ALU.mult, R=[src_b[c]], W=[sq_b[c]])
    return sq, sq_b


def rmsnorm_b(kb, G, P, sqh, src, src_b, gcol, dst, dst_b):
    sq, sq_b = sqh
    bk, bk_b = kb.bank()
    kb.mm(bk[:, 0:T], [(G.ones[:], sq[:, c, :]) for c in range(KC)], R=[G.const_b], W=[bk_b], Rper=[[sq_b[c]] for c in range(KC)])
    rs, rs_b = P.rstd.next()
    kb.op("act", "activation", out=rs[:], in_=bk[:, 0:T], func=AF.Sqrt, bias=G.epsD[:], scale=1.0 / D, R=[bk_b, G.const_b], W=[rs_b])
    kb.op("dve", "reciprocal", out=rs[:], in_=rs[:], R=[rs_b], W=[rs_b])
    for c in range(KC):
        kb.op("dve", "scalar_tensor_tensor", out=dst[:, c, :], in0=src[:, c, :], scalar=G.vecs[:, gcol + c:gcol + c + 1],
              in1=rs[:], op0=ALU.mult, op1=ALU.mult, R=[src_b[c], rs_b, G.const_b], W=[dst_b[c]])


def rmsnorm(kb, G, P, src, src_b, gcol, dst, dst_b):
    sqh = rmsnorm_a(kb, G, P, src, src_b)
    rmsnorm_b(kb, G, P, sqh, src, src_b, gcol, dst, dst_b)


def ffn(kb, G, P, ws, xn, xn_b, resid, resid_b, out, out_b, hook=None):
    for mt in range(DFF // GW):
        if mt == 5 and hook is not None:
            hook()
        wt, wb = ws.get()
        w = wt[:, 0:2 * KC * GW].rearrange("p (s k g) -> p s k g", s=2, k=KC)
        for j in range(GW // 128):
            m = mt * (GW // 128) + j
            bg, bg_b = kb.bank()
            bu, bu_b = kb.bank()
            kb.mm(bg[:, 0:T], [(w[:, 0, k, j * 128:(j + 1) * 128], xn[:, k, :]) for k in range(KC)], R=[wb], W=[bg_b],
                  Rper=[[xn_b[k]] for k in range(KC)])
            kb.mm(bu[:, 0:T], [(w[:, 1, k, j * 128:(j + 1) * 128], xn[:, k, :]) for k in range(KC)], R=[wb] + xn_b, W=[bu_b])
            sg, sg_b = P.sgt.next()
            kb.op("act", "activation", out=sg[:], in_=bg[:, 0:T], func=AF.Silu, R=[bg_b], W=[sg_b])
            kb.op("dve", "tensor_tensor", out=P.hid[:, m, :], in0=sg[:], in1=bu[:, 0:T], op=ALU.mult,
                  R=[sg_b, bu_b], W=[P.hid_b[m]])
    for m in range(KC):
        wt, wb = ws.get()
        w = wt[:, 0:FC * 128].rearrange("p (k g) -> p k g", k=FC)
        bk, bk_b = kb.bank()
        kb.mm(bk[:, 0:T], [(w[:, k, :], P.hid[:, k, :]) for k in range(FC)], R=[wb] + P.hid_b, W=[bk_b])
        kb.op("dve", "scalar_tensor_tensor", out=out[:, m, :], in0=bk[:, 0:T], scalar=0.5, in1=resid[:, m, :],
              op0=ALU.mult, op1=ALU.add, R=[bk_b, resid_b[m]], W=[out_b[m]])


def fm_view(ap2d, c0, nchunk, t0, n):
    return ap2d[c0:c0 + nchunk * 128, t0:t0 + n].rearrange("(c p) t -> p c t", p=128)


def alloc_dense(kb, es, nc):
    P = Glob()

    kb.nuid += 1
    uid = kb.nuid

    def sb(name, shape, dt):
        return es.enter_context(nc.sbuf_tensor(f"{name}_u{uid}", list(shape), dt))
    P.sb = sb

    P.xin = Rot([(sb(f"xin{i}", [128, KC, T], F32), [Buf() for _ in range(KC)]) for i in range(2)])
    P.sqr = Rot([(sb(f"sq{i}", [128, KC, T], BF16), [Buf() for _ in range(KC)]) for i in range(2)])
    P.rstd = Rot([(sb(f"rstd{i}", [128, T], F32), Buf()) for i in range(2)])
    P.xn = Rot([(sb(f"xn{i}", [128, KC, T], BF16), [Buf() for _ in range(KC)]) for i in range(3)])
    P.hid = sb("hid", [128, FC, T], BF16)
    P.hid_b = [Buf() for _ in range(FC)]
    P.sgt = Rot([(sb(f"sgt{i}", [128, T], F32), Buf()) for i in range(3)])
    P.x1 = Rot([(sb(f"x1_{i}", [128, KC, T], F32), [Buf() for _ in range(KC)]) for i in range(1)])
    P.stg = Rot([(sb(f"stg{i}", [128, 4, T], BF16), Buf()) for i in range(2)])
    P.stgf = Rot([(sb(f"stgf{i}", [128, 4, T], F32), Buf()) for i in range(1)])
    P.wslots = [(sb(f"wslot{i}", [128, WSLOT], BF16), Buf()) for i in range(5)]
    return P


def phase_p1(kb, G, S, l):
    nc = kb.nc
    L = S.L
    NT = L // T
    with contextlib.ExitStack() as es:
        P = alloc_dense(kb, es, nc)
        G.wslots = P.wslots
        order = []
        for t in range(NT):
            order += [(l, i) for i in WGROUPS["ffn1_gu"] + WGROUPS["ffn1_down"] + WGROUPS["in_fm"] + WGROUPS["in_tm"]]
        ws = WStream(kb, G, order)
        vb = l * G.NV
        def preA(t):
            xin, xin_b = P.xin.next()
            kb.dma("sp", xin[:], fm_view(S.x[l], 0, KC, t * T, T), R=[sbuf_of(S, f"x{l}", t)], W=xin_b)
            sqh = rmsnorm_a(kb, G, P, xin, xin_b)
            return (xin, xin_b, sqh)

        def preB(pa):
            xin, xin_b, sqh = pa
            xn, xn_b = P.xn.next()
            rmsnorm_b(kb, G, P, sqh, xin, xin_b, vb + 0, xn, xn_b)
            return (xin, xin_b, xn, xn_b)

        nxt = preB(preA(0))
        for t in range(NT):
            t0 = t * T
            xin, xin_b, xn, xn_b = nxt
            nxt_a = None
            x1, x1_b = P.x1.next()
            ffn(kb, G, P, ws, xn, xn_b, xin, xin_b, x1, x1_b)
            kb.dma("pool", fm_view(S.x1, 0, KC, t0, T), x1[:], R=x1_b, W=[sbuf_of(S, "x1", t)])
            hn, hn_b = P.xn.next()
            rmsnorm(kb, G, P, x1, x1_b, vb + 8, hn, hn_b)
            if t + 1 < NT:
                nxt_a = preA(t + 1)
            for j, c0 in enumerate(IN_FM_COLS):
                if j == 5 and nxt_a is not None:
                    nxt = preB(nxt_a)
                wt, wb = ws.get()
                w = wt[:, 0:KC * 512].rearrange("p (k g) -> p k g", k=KC)
                isz = (O_ZF <= c0 < O_QH)
                isu = (c0 == O_UF)
                st, st_b = (P.stgf if isz else P.stg).next()
                for mc in range(4):
                    bk, bk_b = kb.bank()
                    kb.mm(bk[:, 0:T], [(w[:, k, mc * 128:(mc + 1) * 128], hn[:, k, :]) for k in range(KC)],
                          R=[wb], W=[bk_b], Rper=[[hn_b[k]] for k in range(KC)])
                    if c0 < O_V:
                        gcol = G.vecs[:, vb + 24:vb + 25] if c0 < O_K else G.vecs[:, vb + 25:vb + 26]
                        if mc % 2 == 0:
                            kb.op("act", "activation", out=st[:, mc, :], in_=bk[:, 0:T], func=AF.Copy, scale=gcol, R=[bk_b, G.const_b], W=[st_b])
                        else:
                            kb.op("dve", "tensor_scalar", out=st[:, mc, :], in0=bk[:, 0:T], scalar1=gcol, scalar2=None, op0=ALU.mult,
                                  R=[bk_b, G.const_b], W=[st_b])
                    elif mc % 2 == 0:
                        kb.op("act", "activation", out=st[:, mc, :], in_=bk[:, 0:T], func=AF.Copy, R=[bk_b], W=[st_b])
                    else:
                        kb.op("dve", "tensor_copy", out=st[:, mc, :], in_=bk[:, 0:T], R=[bk_b], W=[st_b])
                if c0 < O_V:
                    kb.dma("pool", fm_view(S.qk, c0, 4, t0, T), st[:], R=[st_b], W=[sbuf_of(S, "qk", t)])
                elif isz:
                    kb.dma("pool", fm_view(S.z, c0 - O_ZF, 4, t0, T), st[:], R=[st_b], W=[sbuf_of(S, "z", t)])
                elif c0 == O_QH:
                    kb.dma("pool", fm_view(S.qh, 0, 4, t0, T), st[:], R=[st_b], W=[sbuf_of(S, "qh", t)])
                elif c0 == O_GH:
                    kb.dma("pool", fm_view(S.gh, 0, 4, t0, T), st[:], R=[st_b], W=[sbuf_of(S, "gh", t)])
                elif isu:
                    for tb in range(4):
                        st2, st2_b = P.stg.next()
                        for g in range(4):
                            bk, bk_b = kb.bank()
                            kb.mm(bk[:, 0:256], [(st[:, g, tb * 128:(tb + 1) * 128], G.ccsc[:])],
                                  R=[st_b, G.const_b], W=[bk_b])
                            e = "act" if g % 2 == 0 else "dve"
                            if e == "act":
                                kb.op("act", "activation", out=st2[:, g, 0:256], in_=bk[:, 0:256], func=AF.Copy, R=[bk_b], W=[st2_b])
                            else:
                                kb.op("dve", "tensor_copy", out=st2[:, g, 0:256], in_=bk[:, 0:256], R=[bk_b], W=[st2_b])
                        dst = S.ucs[t0 + tb * 128:t0 + (tb + 1) * 128, :].rearrange("p (g c) -> p g c", g=4)
                        kb.dma("pool", dst, st2[:, :, 0:256], R=[st2_b], W=[sbuf_of(S, "ucs", t)])
            for j, c0 in enumerate(IN_TM_COLS):
                wt, wb = ws.get()
                w = wt[:, 0:KC * 512].rearrange("p (k g) -> p k g", k=KC)
                st, st_b = P.stg.next()
                for tb in range(4):
                    bk, bk_b = kb.bank()
                    kb.mm(bk[:, 0:512], [(hn[:, k, tb * 128:(tb + 1) * 128], w[:, k, :]) for k in range(KC)],
                          R=[wb] + hn_b, W=[bk_b])
                    if tb % 2 == 0:
                        kb.op("act", "activation", out=st[:, tb, :], in_=bk[:, 0:512], func=AF.Copy, R=[bk_b], W=[st_b])
                    else:
                        kb.op("dve", "tensor_copy", out=st[:, tb, :], in_=bk[:, 0:512], R=[bk_b], W=[st_b])
                if c0 < O_ZF:
                    dst = S.v[t0:t0 + T, c0 - O_V:c0 - O_V + 512].rearrange("(b p) c -> p b c", p=128)
                    kb.dma("pool", dst, st[:], R=[st_b], W=[sbuf_of(S, "v", t)])
                else:
                    dst = S.ih[t0:t0 + T, :].rearrange("(b p) c -> p b c", p=128)
                    kb.dma("pool", dst, st[:], R=[st_b], W=[sbuf_of(S, "ih", t)])
        kb.barrier()


def phase_p5(kb, G, S, l):
    nc = kb.nc
    L = S.L
    NT = L // T
    last = (l + 1 >= G.nlayers)
    xout = S.y if last else S.x[l + 1]
    xout_nm = "y" if last else f"x{l + 1}"
    with contextlib.ExitStack() as es:
        P = alloc_dense(kb, es, nc)
        G.wslots = P.wslots
        P.br = [(P.sb(f"br{i}", [128, 4, T], BF16), Buf()) for i in range(3)]
        order = []
        for t in range(NT):
            order += [(l, i) for i in WGROUPS["p5m"] + WGROUPS["w_out"] + WGROUPS["ffn2_gu"] + WGROUPS["ffn2_down"]]
        ws = WStream(kb, G, order)
        vb = l * G.NV
        def preA(t):
            x1, x1_b = P.xin.next()
            kb.dma("sp", x1[:], fm_view(S.x1, 0, KC, t * T, T), R=[sbuf_of(S, "x1", t)], W=x1_b)
            for i, (nm, src) in enumerate((("att", S.att), ("hg", S.hg), ("fn", S.fn))):
                kb.dma("sp", P.br[i][0][:], fm_view(src, 0, 4, t * T, T), R=[sbuf_of(S, nm, t)], W=[P.br[i][1]])
            sqh = rmsnorm_a(kb, G, P, x1, x1_b)
            return (x1, x1_b, sqh)

        def preB(pa):
            x1, x1_b, sqh = pa
            hn, hn_b = P.xn.next()
            rmsnorm_b(kb, G, P, sqh, x1, x1_b, vb + 8, hn, hn_b)
            return (x1, x1_b, hn, hn_b)

        nxt = [preB(preA(0))]
        for t in range(NT):
            t0 = t * T
            x1, x1_b, hn, hn_b = nxt[0]
            mg, mg_b = P.xn.next()
            for m in range(KC):
                wt, wb = ws.get()
                wg = wt[:, 0:3 * KC * 128].rearrange("p (b k g) -> p b k g", b=3, k=KC)
                wo = wt[:, 3 * KC * 128:3 * KC * 128 + 3 * 4 * 128].rearrange("p (b k g) -> p b k g", b=3, k=4)
                prods = []
                for br in range(3):
                    bg, bg_b = kb.bank()
                    bp, bp_b = kb.bank()
                    kb.mm(bg[:, 0:T], [(wg[:, br, k, :], hn[:, k, :]) for k in range(KC)], R=[wb], W=[bg_b], Rper=[[hn_b[k]] for k in range(KC)])
                    kb.mm(bp[:, 0:T], [(wo[:, br, k, :], P.br[br][0][:, k, :]) for k in range(4)],
                          R=[wb, P.br[br][1]], W=[bp_b])
                    sg, sg_b = P.sgt.next()
                    kb.op("act", "activation", out=sg[:], in_=bg[:, 0:T], func=AF.Sigmoid, R=[bg_b], W=[sg_b])
                    kb.op("dve", "tensor_tensor", out=sg[:], in0=sg[:], in1=bp[:, 0:T], op=ALU.mult,
                          R=[sg_b, bp_b], W=[sg_b])
                    prods.append((sg, sg_b))
                kb.op("pool", "tensor_tensor", out=prods[0][0][:], in0=prods[0][0][:], in1=prods[1][0][:], op=ALU.add,
                      R=[prods[1][1]], W=[prods[0][1]])
                kb.op("pool", "tensor_tensor", out=mg[:, m, :], in0=prods[0][0][:], in1=prods[2][0][:], op=ALU.add,
                      R=[prods[0][1], prods[2][1]], W=[mg_b[m]])
            x2, x2_b = P.x1.next()
            for j in range(2):
                wt, wb = ws.get()
                w = wt[:, 0:KC * 512].rearrange("p (k g) -> p k g", k=KC)
                for mc in range(4):
                    m = j * 4 + mc
                    bk, bk_b = kb.bank()
                    kb.mm(bk[:, 0:T], [(w[:, k, mc * 128:(mc + 1) * 128], mg[:, k, :]) for k in range(KC)],
                          R=[wb], W=[bk_b], Rper=[[mg_b[k]] for k in range(KC)])
                    kb.op("dve", "tensor_tensor", out=x2[:, m, :], in0=bk[:, 0:T], in1=x1[:, m, :], op=ALU.add,
                          R=[bk_b, x1_b[m]], W=[x2_b[m]])
            xn, xn_b = P.xn.next()
            rmsnorm(kb, G, P, x2, x2_b, vb + 16, xn, xn_b)
            x3, x3_b = P.xin.next()
            hook = None
            if t + 1 < NT:
                pa = preA(t + 1)

                def hook(pa=pa):
                    nxt[0] = preB(pa)
            ffn(kb, G, P, ws, xn, xn_b, x2, x2_b, x3, x3_b, hook=hook)
            G.out_toks.append(kb.dma("pool", fm_view(xout, 0, KC, t0, T), x3[:], R=x3_b, W=[sbuf_of(S, xout_nm, t)]))
        kb.barrier()


NV = 32
CB_ONES, CB_CCSC, CB_IDENT, CB_BLK, CB_ROT, CB_HMF, CB_HMB, CB_AM, CB_END = 0, 128, 384, 512, 640, 768, 896, 1024, 1024 + 4 * 512


def host_consts():
    cb = np.zeros((128, CB_END), np.float32)
    cb[:, CB_ONES:CB_ONES + 128] = 1.0
    c = np.arange(128)
    ang = 2 * np.pi * np.outer(c, c) / 128.0
    cb[:, CB_CCSC:CB_CCSC + 128] = np.cos(ang) / np.sqrt(128.0)
    cb[:, CB_CCSC + 128:CB_CCSC + 256] = np.sin(ang) / np.sqrt(128.0)
    cb[:, CB_IDENT:CB_IDENT + 128] = np.eye(128)
    blk = np.zeros((128, 128))
    blk[:64, :64] = 1
    blk[64:, 64:] = 1
    cb[:, CB_BLK:CB_BLK + 128] = blk
    rot = np.zeros((128, 128))
    for h in range(2):
        for i in range(32):
            rot[h * 64 + 32 + i, h * 64 + i] = -1.0
            rot[h * 64 + i, h * 64 + 32 + i] = 1.0
    cb[:, CB_ROT:CB_ROT + 128] = rot
    sidx = np.arange(128)[:, None]
    tidx = np.arange(128)[None, :]
    same = (sidx // 64) == (tidx // 64)
    cb[:, CB_HMF:CB_HMF + 128] = (same & (sidx <= tidx))
    cb[:, CB_HMB:CB_HMB + 128] = (same & (sidx >= tidx))
    m1 = (tidx <= sidx).astype(np.float32)
    m2 = (tidx >= sidx).astype(np.float32)
    m1f = m1 * (sidx >= 64)
    m2l = m2 * (sidx < 64)
    for v, (a, b) in enumerate(((m1, m2), (m1f, m2), (m1, m2l), (m1f, m2l))):
        cb[:, CB_AM + v * 512:CB_AM + (v + 1) * 512] = (np.concatenate([a, a, b, b], axis=1) - 1.0) * 30000.0
    return cb.astype(NPBF)


def host_vecs(inputs):
    v = np.zeros((128, DEPTH * NV), np.float32)
    for l in range(DEPTH):
        b = l * NV
        for i, nm in enumerate(("ffn1_norm", "mix_norm", "ffn2_norm")):
            v[:, b + 8 * i:b + 8 * i + 8] = np.asarray(inputs[nm])[l].reshape(8, 128).T
        v[:, b + 24] = np.tile(np.asarray(inputs["q_norm"])[l], 2)
        v[:, b + 25] = np.tile(np.asarray(inputs["k_norm"])[l], 2)
        v[:, b + 26] = np.asarray(inputs["hgrn_out_norm"])[l]
    return v


def host_lbl(inputs):
    lg = np.asarray(inputs["hgrn_lb_logits"])
    return np.ascontiguousarray(lg.reshape(DEPTH * 2 * 4, 128).T)


def build(Lp, Ls, nlayers=DEPTH, dbg=None, phases=("p1", "p2", "p3", "p4", "p5")):
    dbg = dbg or {}
    kb = KB()
    nc = kb.nc
    G = Glob()
    G.NV = NV
    G.nlayers = nlayers
    G.out_toks = []
    wsrc = kb.dram("wsrc", [WTOTAL], F32, kind="ExternalInput")
    G.wbf = kb.dram("wbf", [WTOTAL], BF16)
    vecs_d = kb.dram("vecs", [128, DEPTH * NV], F32, kind="ExternalInput")
    cb_d = kb.dram("cbf", [128, CB_END], BF16, kind="ExternalInput")
    G.vecs = kb.sbuf("vecs_sb", [128, DEPTH * NV], F32)
    G.cb = kb.sbuf("cb_sb", [128, CB_END], BF16)
    G.ones = G.cb[:, CB_ONES:CB_ONES + 128]
    G.ccsc = G.cb[:, CB_CCSC:CB_CCSC + 256]
    G.ident = G.cb[:, CB_IDENT:CB_IDENT + 128]
    G.blk = G.cb[:, CB_BLK:CB_BLK + 128]
    G.rot = G.cb[:, CB_ROT:CB_ROT + 128]
    G.hmask = [G.cb[:, CB_HMF:CB_HMF + 128], G.cb[:, CB_HMB:CB_HMB + 128]]
    G.amask = [G.cb[:, CB_AM + v * 512:CB_AM + (v + 1) * 512] for v in range(4)]
    G.const_b = Buf("const")
    G.epsc = kb.sbuf("epsc", [128, 2], F32)
    G.epsD = G.epsc[:, 0:1]
    G.wcast_b = Buf("wcast")
    G.wslot_i = 0
    for i in range(8):
        kb.psum_banks.append((kb.psum(f"bank{i}", [128, 512], F32), Buf()))
    seqs = []
    if Lp:
        seqs.append(seq_scratch(kb, "p", Lp, dbg))
    if Ls:
        seqs.append(seq_scratch(kb, "s", Ls, dbg))
    kb.op("dve", "memset", G.epsc[:], EPS, W=[G.const_b])
    kb.dma("sp", G.vecs[:], vecs_d[:, :], W=[G.const_b])
    kb.dma("sp", G.cb[:], cb_d[:, :], W=[G.const_b])
    lbl_d = kb.dram("lbl", [128, DEPTH * 8], F32, kind="ExternalInput")
    G.lbe = kb.sbuf("lbe", [128, DEPTH * 8], F32)
    G.lb = kb.sbuf("lb", [128, DEPTH * 8], F32)
    G.oml = kb.sbuf("oml", [128, DEPTH * 8], F32)
    G.lbt = kb.sbuf("lbt", [128, 8], F32)
    G.rmask = kb.sbuf("rmask", [128, T], F32)
    lb_b = Buf()
    kb.dma("sp", G.lbe[:], lbl_d[:, :], W=[lb_b])
    kb.op("act", "activation", out=G.lbe[:], in_=G.lbe[:], func=AF.Exp, R=[lb_b], W=[lb_b])
    kb.op("dve", "tensor_copy", out=G.lbt[:], in_=G.lbe[:, 0:8], R=[lb_b], W=[lb_b])
    for l in range(1, DEPTH):
        kb.op("dve", "tensor_tensor", out=G.lbt[:], in0=G.lbt[:], in1=G.lbe[:, l * 8:(l + 1) * 8], op=ALU.add, R=[lb_b], W=[lb_b])
    kb.op("dve", "reciprocal", out=G.lbt[:], in_=G.lbt[:], R=[lb_b], W=[lb_b])
    kb.op("dve", "memset", G.lb[:, 0:8], 0.0, W=[lb_b])
    for l in range(1, DEPTH):
        kb.op("dve", "tensor_tensor", out=G.lbe[:, l * 8:(l + 1) * 8], in0=G.lbe[:, l * 8:(l + 1) * 8], in1=G.lbt[:], op=ALU.mult, R=[lb_b], W=[lb_b])
        kb.op("dve", "tensor_tensor", out=G.lb[:, l * 8:(l + 1) * 8], in0=G.lb[:, (l - 1) * 8:l * 8], in1=G.lbe[:, l * 8:(l + 1) * 8], op=ALU.add, R=[lb_b], W=[lb_b])
    kb.op("dve", "tensor_scalar", out=G.oml[:], in0=G.lb[:], scalar1=-1.0, scalar2=1.0, op0=ALU.mult, op1=ALU.add, R=[lb_b], W=[G.const_b])
    kb.op("dve", "memset", G.rmask[:], 1.0, W=[G.const_b])
    kb.op("dve", "memset", G.rmask[:].rearrange("p (c j) -> p c j", j=64)[:, :, 0:1], 0.0, W=[G.const_b])
    for i in range(WTOTAL // CAST_CH):
        kb.dma("pool", G.wbf[i * CAST_CH:(i + 1) * CAST_CH].rearrange("(p n) -> p n", p=128),
               wsrc[i * CAST_CH:(i + 1) * CAST_CH].rearrange("(p n) -> p n", p=128))
    kb.barrier()
    for l in range(nlayers):
        for S in seqs:
            if "p1" in phases:
                phase_p1(kb, G, S, l)
            if "p2" in phases:
                phase_p2(kb, G, S)
            if "p3" in phases:
                phase_p3(kb, G, S, l)
            if "p4" in phases:
                phase_p4(kb, G, S, l)
            if "p5" in phases:
                phase_p5(kb, G, S, l)
    kb.barrier()
    kb.es.close()
    return kb, seqs


def host_fnet_consts(L):
    N1 = L // 128
    fb = np.zeros((128, 4 * N1 + 256), np.float32)
    s1 = np.arange(N1)
    a1 = 2 * np.pi * np.outer(s1, s1) / N1
    fb[:N1, 0:N1] = np.cos(a1)
    fb[:N1, N1:2 * N1] = np.sin(a1)
    fb[:N1, 2 * N1:3 * N1] = -np.sin(a1)
    fb[:N1, 3 * N1:4 * N1] = np.cos(a1)
    s0 = np.arange(128)
    a2 = 2 * np.pi * np.outer(s0, s0) / 128.0
    fb[:, 4 * N1:4 * N1 + 128] = np.cos(a2) / np.sqrt(L)
    fb[:, 4 * N1 + 128:4 * N1 + 256] = -np.sin(a2) / np.sqrt(L)
    ff = np.zeros((128, 2 * N1), np.float32)
    at = 2 * np.pi * np.outer(s0, s1) / L
    ff[:, 0:N1] = np.cos(at)
    ff[:, N1:2 * N1] = np.sin(at)
    return fb.astype(NPBF), ff


def phase_p2(kb, G, S):
    nc = kb.nc
    L = S.L
    N1 = L // 128
    NT = L // T
    CH = 64
    cpb1 = 512 // (2 * N1)
    cpb2 = min(512 // N1, CH)
    with contextlib.ExitStack() as es:
        kb.nuid += 1
        uid = kb.nuid

        def sb(name, shape, dt):
            return es.enter_context(nc.sbuf_tensor(f"{name}_u{uid}", list(shape), dt))

        fb = sb("fn_fb", [128, 4 * N1 + 256], BF16)
        ff = sb("fn_ff", [128, 2 * N1], F32)
        cst_b = Buf()
        kb.dma("sp", fb[:], S.fnb[:, :], W=[cst_b])
        kb.dma("sp", ff[:], S.fnf[:, :], W=[cst_b])
        F1a = fb[0:N1, 0:2 * N1]
        F1b = fb[0:N1, 2 * N1:4 * N1]
        Fc2 = fb[:, 4 * N1:4 * N1 + 128]
        nFs2 = fb[:, 4 * N1 + 128:4 * N1 + 256]
        Tc = ff[:, 0:N1]
        Ts = ff[:, N1:2 * N1]
        zt = Rot([(sb(f"fn_zt{i}", [N1, 128, 256], BF16), Buf()) for i in range(1)])
        gp = sb("fn_gp", [128, CH, 2, N1], BF16)
        gp_b = Buf()
        yt = Rot([(sb(f"fn_yt{i}", [128, CH, N1], BF16), Buf()) for i in range(2)])
        tmp = Rot([(sb(f"fn_tmp{i}", [128, 4, 256], F32), Buf()) for i in range(3)])
        all_ucs = [sbuf_of(S, "ucs", t) for t in range(NT)]
        all_fn = [sbuf_of(S, "fn", t) for t in range(NT)]
        for g in range(4):
            z, z_b = zt.next()
            src = S.ucs[:, g * 256:(g + 1) * 256].rearrange("(s1 s0) c -> s1 s0 c", s0=128)
            kb.dma("sp", z[:], src, R=all_ucs, W=[z_b])
            for half in range(128 // CH):
                for c0 in range(0, CH, cpb1):
                    bk, bk_b = kb.bank()
                    groups = []
                    for j in range(cpb1):
                        ch = half * CH + c0 + j
                        groups.append((bk[:, j * 2 * N1:(j + 1) * 2 * N1],
                                       [(z[:, :, ch], F1a), (z[:, :, 128 + ch], F1b)]))
                    kb.mm_multi(groups, R=[z_b, cst_b], W=[bk_b])
                    bv = bk[:, 0:cpb1 * 2 * N1].rearrange("p (c s n) -> p c s n", c=cpb1, s=2)
                    tcb = Tc.unsqueeze(1).broadcast_to([128, cpb1, N1])
                    tsb = Ts.unsqueeze(1).broadcast_to([128, cpb1, N1])
                    tm, tm_b = tmp.next()
                    tv = tm[:].rearrange("p a b -> p (a b)")[:, 0:4 * cpb1 * N1].rearrange("p (a c n) -> p a c n", a=4, c=cpb1)
                    kb.op("dve", "tensor_tensor", out=tv[:, 0], in0=bv[:, :, 0, :], in1=tcb, op=ALU.mult, R=[bk_b, cst_b], W=[tm_b])
                    kb.op("dve", "tensor_tensor", out=tv[:, 1], in0=bv[:, :, 1, :], in1=tsb, op=ALU.mult, R=[bk_b, cst_b], W=[tm_b])
                    kb.op("dve", "tensor_tensor", out=tv[:, 2], in0=bv[:, :, 1, :], in1=tcb, op=ALU.mult, R=[bk_b, cst_b], W=[tm_b])
                    kb.op("dve", "tensor_tensor", out=tv[:, 3], in0=bv[:, :, 0, :], in1=tsb, op=ALU.mult, R=[bk_b, cst_b], W=[tm_b])
                    kb.op("pool", "tensor_tensor", out=gp[:, c0:c0 + cpb1, 0, :], in0=tv[:, 0], in1=tv[:, 1], op=ALU.subtract,
                          R=[tm_b], W=[gp_b])
                    kb.op("pool", "tensor_tensor", out=gp[:, c0:c0 + cpb1, 1, :], in0=tv[:, 2], in1=tv[:, 3], op=ALU.add,
                          R=[tm_b], W=[gp_b])
                y, y_b = yt.next()
                for i, c0 in enumerate(range(0, CH, cpb2)):
                    bk, bk_b = kb.bank()
                    kb.mm(bk[:, 0:cpb2 * N1], [(Fc2, gp[:, c0:c0 + cpb2, 0, :]), (nFs2, gp[:, c0:c0 + cpb2, 1, :])],
                          R=[gp_b, cst_b], W=[bk_b])
                    src_v = bk[:, 0:cpb2 * N1].rearrange("p (c n) -> p c n", c=cpb2)
                    if i % 2 == 0:
                        kb.op("act", "activation", out=y[:, c0:c0 + cpb2, :], in_=src_v, func=AF.Copy, R=[bk_b], W=[y_b])
                    else:
                        kb.op("dve", "tensor_copy", out=y[:, c0:c0 + cpb2, :], in_=src_v, R=[bk_b], W=[y_b])
                r0 = g * 128 + half * CH
                dst = S.fn[r0:r0 + CH, :].rearrange("c (k0 k1) -> k0 c k1", k1=N1)
                kb.dma("pool", dst, y[:], R=[y_b], W=all_fn)
        kb.barrier()


def phase_p3(kb, G, S, l):
    nc = kb.nc
    L = S.L
    NT = L // T
    H4 = 4
    with contextlib.ExitStack() as es:
        kb.nuid += 1
        uid = kb.nuid

        def sb(name, shape, dt):
            return es.enter_context(nc.sbuf_tensor(f"{name}_u{uid}", list(shape), dt))

        def rot(name, shape, dt, n):
            return Rot([(sb(f"{name}{i}", shape, dt), Buf()) for i in range(n)])

        zt = rot("h_z", [128, H4, T], F32, 1)
        qt = rot("h_q", [128, H4, T], BF16, 1)
        it = rot("h_i", [128, 4, 512], BF16, 2)
        gt = rot("h_g", [128, H4, T], BF16, 1)
        obin = rot("h_obin", [128, H4, T], F32, 1)
        Bs = (sb("h_Bs", [128, H4, T], F32), Buf())
        Blf = (sb("h_Blf", [128, H4, T], F32), Buf())
        Bkf = (sb("h_Bkf", [128, H4, T], F32), Buf())
        Bb = (sb("h_Bb", [128, H4, T], F32), Buf())
        Bg = (sb("h_Bg", [128, H4, T], F32), Buf())
        Be = (sb("h_Be", [128, H4, T], F32), Buf())
        Bgm = (sb("h_Bgm", [128, H4, T], F32), Buf())
        BE2 = (sb("h_BE2", [128, H4, T], F32), Buf())
        Bs_h, Blf_h, Bkf_h, Bb_h, Bg_h, Be_h, Bgm_h, BE2_h = [[Buf() for _ in range(H4)] for _ in range(8)]
        dec = rot("h_dec", [128, H4, 8], F32, 2)
        qd = rot("h_qd", [128, H4, T], BF16, 2)
        kd = rot("h_kd", [128, H4, T], BF16, 2)
        qe = rot("h_qe", [128, H4, T], BF16, 2)
        keT = rot("h_keT", [128, H4, T], BF16, 1)
        ketm = rot("h_ketm", [128, 4, 512], BF16, 2)
        atm = rot("h_atm", [128, 128], BF16, 6)
        ost = rot("h_ost", [128, H4, T], F32, 1)
        sq = (sb("h_sq", [128, H4, T], BF16), Buf())
        rs = (sb("h_rs", [128, H4, T], F32), Buf())
        sgt = (sb("h_sgt", [128, H4, T], F32), Buf())
        hout = rot("h_out", [128, H4, T], BF16, 1)
        St = [[sb(f"h_S{h}_{i}", [128, 128], F32) for i in range(2)] for h in range(H4)]
        Sb_ = [[sb(f"h_Sb{h}_{i}", [128, 128], BF16) for i in range(2)] for h in range(H4)]
        St_b = [[Buf(), Buf()] for h in range(H4)]
        Sb_b = [[Buf(), Buf()] for h in range(H4)]
        gain = G.vecs[:, l * NV + 26:l * NV + 27]

        def v4(t):
            return t[:].rearrange("p h (c j) -> p h c j", j=64)

        def hv(t, hh):
            return t[:, hh * 2:hh * 2 + 2, :]

        def pre_ops(d, t):
            t0 = t * T
            res = {}
            ops = []

            def o_load():
                z, z_b = zt.next()
                kb.dma("sp", z[:], fm_view(S.z, d * 512, H4, t0, T), R=[sbuf_of(S, "z", t)], W=[z_b])
                q, q_b = qt.next()
                kb.dma("sp", q[:], fm_view(S.qh, 0, H4, t0, T), R=[sbuf_of(S, "qh", t)], W=[q_b])
                iv, iv_b = it.next()
                kb.dma("sp", iv[:], S.ih[t0:t0 + T, :].rearrange("(b p) c -> p b c", p=128), R=[sbuf_of(S, "ih", t)], W=[iv_b])
                res.update(z=z, z_b=z_b, q=q, q_b=q_b, iv=iv, iv_b=iv_b)
            ops.append(o_load)

            def H(t_, h):
                return t_[:, h, :]

            def H4v(t_, h):
                return t_[:, h, :].rearrange("p (c j) -> p c j", j=64)

            def mk(stage):
                for h in range(H4):
                    ops.append(lambda h=h: stage(h))

            def s1(h):
                kb.op("act", "activation", out=H(Bs[0], h), in_=H(res["z"], h), func=AF.Sigmoid, R=[res["z_b"]], W=[Bs_h[h]])
                col = l * 8 + d * 4 + h
                kb.op("dve", "tensor_scalar", out=H(Bs[0], h), in0=H(Bs[0], h), scalar1=G.oml[:, col:col + 1],
                      scalar2=G.lb[:, col:col + 1], op0=ALU.mult, op1=ALU.add, R=[G.const_b], W=[Bs_h[h]])
            mk(s1)

            def s2(h):
                kb.op("act", "activation", out=H(Blf[0], h), in_=H(Bs[0], h), func=AF.Ln, R=[Bs_h[h]], W=[Blf_h[h]])
                e = "pool" if h % 2 else "dve"
                kb.op(e, "tensor_scalar", out=H(Bkf[0], h), in0=H(Bs[0], h), scalar1=-1.0, scalar2=1.0, op0=ALU.mult, op1=ALU.add,
                      R=[Bs_h[h]], W=[Bkf_h[h]])
            mk(s2)

            def s3(h):
                kb.op("dve", "tensor_tensor_scan", out=H(Bb[0], h), data0=G.rmask[:], data1=H(Blf[0], h), initial=0.0,
                      op0=ALU.mult, op1=ALU.add, R=[Blf_h[h], G.const_b], W=[Bb_h[h]])
            mk(s3)

            def s4a():
                dc, dc_b = dec.next()
                res.update(dc=dc, dc_b=dc_b)
                res["gbuf"], res["gbuf_h"] = (Bb, Bb_h) if d == 0 else (Bg, Bg_h)
            ops.append(s4a)

            def s4(h):
                b4 = H4v(Bb[0], h)
                bend = b4[:, :, 63:64]
                bend_bc = bend.broadcast_to([128, 8, 64])
                dc, dc_b = res["dc"], res["dc_b"]
                kb.op("act", "activation", out=dc[:, h, :].rearrange("p (c o) -> p c o", o=1), in_=bend, func=AF.Exp, R=[Bb_h[h]], W=[dc_b])
                if d == 0:
                    kb.op("pool", "tensor_tensor", out=H4v(Be[0], h), in0=bend_bc, in1=b4, op=ALU.subtract, R=[Bb_h[h]], W=[Be_h[h]])
                else:
                    kb.op("dve", "tensor_tensor", out=H4v(Bg[0], h), in0=bend_bc, in1=b4, op=ALU.subtract, R=[Bb_h[h]], W=[Bg_h[h]])
                    kb.op("dve", "tensor_tensor", out=H(Bg[0], h), in0=H(Bg[0], h), in1=H(Blf[0], h), op=ALU.add, R=[Blf_h[h]], W=[Bg_h[h]])
                    kb.op("pool", "tensor_tensor", out=H(Be[0], h), in0=H(Bb[0], h), in1=H(Blf[0], h), op=ALU.subtract,
                          R=[Bb_h[h], Blf_h[h]], W=[Be_h[h]])
            mk(s4)

            def s5(h):
                gbuf, gbuf_h = res["gbuf"], res["gbuf_h"]
                g4 = H4v(gbuf[0], h)
                gmid_bc = g4[:, :, 32:33].broadcast_to([128, 8, 64])
                kb.op("dve", "tensor_tensor", out=H4v(Bgm[0], h), in0=g4, in1=gmid_bc, op=ALU.subtract, R=[gbuf_h[h]], W=[Bgm_h[h]])
                kb.op("act", "activation", out=H(BE2[0], h), in_=H(Bgm[0], h), func=AF.Exp, scale=-1.0, R=[Bgm_h[h]], W=[BE2_h[h]])
                kb.op("act", "activation", out=H(Bgm[0], h), in_=H(Bgm[0], h), func=AF.Exp, R=[Bgm_h[h]], W=[Bgm_h[h]])
            mk(s5)

            def s6a():
                qd_t, qd_b = qd.next()
                kd_t, kd_b = kd.next()
                qe_t, qe_b = qe.next()
                ke_t, ke_b = keT.next()
                res.update(qd_t=qd_t, qd_b=qd_b, kd_t=kd_t, kd_b=kd_b, qe_t=qe_t, qe_b=qe_b, ke_t=ke_t, ke_b=ke_b)
            ops.append(s6a)

            def s6(h):
                q, q_b = res["q"], res["q_b"]
                kb.op("dve", "tensor_tensor", out=H(res["qd_t"], h), in0=H(q, h), in1=H(Bgm[0], h), op=ALU.mult, R=[q_b, Bgm_h[h]], W=[res["qd_b"]])
                e = "pool" if h >= 2 else "dve"
                kb.op(e, "tensor_tensor", out=H(res["kd_t"], h), in0=H(Bkf[0], h), in1=H(BE2[0], h), op=ALU.mult, R=[Bkf_h[h], BE2_h[h]], W=[res["kd_b"]])
            mk(s6)

            def s7(h):
                gbuf, gbuf_h = res["gbuf"], res["gbuf_h"]
                kb.op("act", "activation", out=H(BE2[0], h), in_=H(gbuf[0], h), func=AF.Exp, R=[gbuf_h[h]], W=[BE2_h[h]])
                kb.op("act", "activation", out=H(Be[0], h), in_=H(Be[0], h), func=AF.Exp, R=[Be_h[h]], W=[Be_h[h]])
            mk(s7)

            def s8(h):
                q, q_b = res["q"], res["q_b"]
                kb.op("dve", "tensor_tensor", out=H(res["qe_t"], h), in0=H(q, h), in1=H(BE2[0], h), op=ALU.mult, R=[q_b, BE2_h[h]], W=[res["qe_b"]])
                e = "pool" if h >= 2 else "dve"
                kb.op(e, "tensor_tensor", out=H(res["ke_t"], h), in0=H(Bkf[0], h), in1=H(Be[0], h), op=ALU.mult, R=[Bkf_h[h], Be_h[h]], W=[res["ke_b"]])
            mk(s8)

            def o10():
                km, km_b = ketm.next()
                ke_t, ke_b = res["ke_t"], res["ke_b"]
                for tb in range(4):
                    bk, bk_b = kb.bank()
                    kb.mm_multi([(bk[:, h * 128:(h + 1) * 128], [(ke_t[:, h, tb * 128:(tb + 1) * 128], G.ident)]) for h in range(H4)],
                                R=[ke_b, G.const_b], W=[bk_b])
                    if tb % 2 == 0:
                        kb.op("act", "activation", out=km[:, tb, :], in_=bk[:, 0:512], func=AF.Copy, R=[bk_b], W=[km_b])
                    else:
                        kb.op("dve", "tensor_copy", out=km[:, tb, :], in_=bk[:, 0:512], R=[bk_b], W=[km_b])
                res.update(km=km, km_b=km_b)
            ops.append(o10)
            return ops, res

        def block_ops(d, t, res, cur):
            t0 = t * T
            ops = []
            st = {}

            def b_init():
                o, o_b = ost.next()
                st.update(o=o, o_b=o_b)
                if d == 0:
                    gg, gg_b = gt.next()
                    kb.dma("sp", gg[:], fm_view(S.gh, 0, H4, t0, T), R=[sbuf_of(S, "gh", t)], W=[gg_b])
                    obi, obi_b = obin.next()
                    kb.dma("sp", obi[:], fm_view(S.ob, 0, H4, t0, T), R=[sbuf_of(S, "ob", t)], W=[obi_b])
                    res.update(gg=gg, gg_b=gg_b, obi=obi, obi_b=obi_b)
            ops.append(b_init)
            blocks = [0, 1, 2, 3] if d == 0 else [3, 2, 1, 0]
            corder = (0, 1) if d == 0 else (1, 0)
            for tb in blocks:
                bst = {}

                def blkA(tb=tb, bst=bst):
                    tsl = slice(tb * 128, (tb + 1) * 128)
                    ats = []
                    for h in range(H4):
                        bkA, bkA_b = kb.bank()
                        kb.mm(bkA[:, 0:128], [(res["kd_t"][:, h, tsl], res["qd_t"][:, h, tsl])], R=[res["kd_b"], res["qd_b"]], W=[bkA_b])
                        a, a_b = atm.next()
                        kb.op("dve", "tensor_tensor", out=a[:], in0=bkA[:, 0:128], in1=G.hmask[d], op=ALU.mult,
                              R=[bkA_b, G.const_b], W=[a_b])
                        ats.append((a, a_b))
                    bst["ats"] = ats
                ops.append(blkA)
                for step, ci in enumerate(corder):
                    def blkS(tb=tb, bst=bst, ci=ci):
                        iv, iv_b = res["iv"], res["iv_b"]
                        km, km_b = res["km"], res["km_b"]
                        qe_t, qe_b = res["qe_t"], res["qe_b"]
                        dc, dc_b = res["dc"], res["dc_b"]
                        csl = slice(ci * 64, (ci + 1) * 64)
                        for h in range(H4):
                            a, a_b = bst["ats"][h]
                            bkO, bkO_b = kb.psum_banks[h]
                            c_ = cur[h]
                            kb.mm(bkO[:, ci * 64:(ci + 1) * 64],
                                  [(iv[:, tb, h * 128:(h + 1) * 128], a[:, csl]),
                                   (Sb_[h][c_][:], qe_t[:, h, tb * 128 + ci * 64:tb * 128 + (ci + 1) * 64])],
                                  R=[iv_b, a_b, Sb_b[h][c_], qe_b], W=[bkO_b])
                            bkS, bkS_b = kb.bank()
                            kb.mm(bkS[:, 0:128], [(km[csl, tb, h * 128:(h + 1) * 128], iv[csl, tb, h * 128:(h + 1) * 128])],
                                  R=[km_b, iv_b], W=[bkS_b])
                            n_ = 1 - c_
                            kb.op("dve", "scalar_tensor_tensor", out=St[h][n_][:], in0=St[h][c_][:],
                                  scalar=dc[:, h, tb * 2 + ci:tb * 2 + ci + 1], in1=bkS[:, 0:128], op0=ALU.mult, op1=ALU.add,
                                  R=[St_b[h][c_], dc_b, bkS_b], W=[St_b[h][n_]])
                            kb.op("act", "activation", out=Sb_[h][n_][:], in_=St[h][n_][:], func=AF.Copy, R=[St_b[h][n_]], W=[Sb_b[h][n_]])
                            cur[h] = n_
                    ops.append(blkS)

                def blkE(tb=tb):
                    o, o_b = st["o"], st["o_b"]
                    tsl = slice(tb * 128, (tb + 1) * 128)
                    for h in range(H4):
                        bkO, bkO_b = kb.psum_banks[h]
                        if d == 1:
                            kb.op("act", "activation", out=o[:, h, tsl], in_=bkO[:, 0:128], func=AF.Copy, R=[bkO_b], W=[o_b])
                        else:
                            obi, obi_b = res["obi"], res["obi_b"]
                            kb.op("dve", "tensor_tensor", out=o[:, h, tsl], in0=bkO[:, 0:128], in1=obi[:, h, tsl], op=ALU.add,
                                  R=[bkO_b, obi_b], W=[o_b])
                ops.append(blkE)

            def fin():
                o, o_b = st["o"], st["o_b"]
                if d == 1:
                    kb.dma("pool", fm_view(S.ob, 0, H4, t0, T), o[:], R=[o_b], W=[sbuf_of(S, "ob", t)])
                else:
                    gg, gg_b = res["gg"], res["gg_b"]
                    kb.op("pool", "tensor_tensor", out=sq[0][:], in0=o[:], in1=o[:], op=ALU.mult, R=[o_b], W=[sq[1]])
                    for h in range(H4):
                        bk, bk_b = kb.bank()
                        kb.mm(bk[:, 0:T], [(G.ones, sq[0][:, h, :])], R=[sq[1], G.const_b], W=[bk_b])
                        kb.op("act", "activation", out=rs[0][:, h, :], in_=bk[:, 0:T], func=AF.Ln, bias=G.epsD[:], scale=1.0 / 128,
                              R=[bk_b, G.const_b], W=[rs[1]])
                    kb.op("act", "activation", out=rs[0][:], in_=rs[0][:], func=AF.Exp, scale=-0.5, R=[rs[1]], W=[rs[1]])
                    kb.op("act", "activation", out=sgt[0][:], in_=gg[:], func=AF.Silu, R=[gg_b], W=[sgt[1]])
                    kb.op("pool", "tensor_tensor", out=rs[0][:], in0=rs[0][:], in1=o[:], op=ALU.mult, R=[o_b], W=[rs[1]])
                    ho, ho_b = hout.next()
                    kb.op("dve", "scalar_tensor_tensor", out=ho[:], in0=rs[0][:], scalar=gain, in1=sgt[0][:], op0=ALU.mult, op1=ALU.mult,
                          R=[rs[1], sgt[1], G.const_b], W=[ho_b])
                    kb.dma("pool", fm_view(S.hg, 0, H4, t0, T), ho[:], R=[ho_b], W=[sbuf_of(S, "hg", t)])
            ops.append(fin)
            return ops

        kb.bank_range = (4, 8)
        for d in (1, 0):
            cur = [0] * H4
            for h in range(H4):
                kb.op("dve", "memset", St[h][0][:], 0.0, W=[St_b[h][0]])
                kb.op("pool", "memset", Sb_[h][0][:], 0.0, W=[Sb_b[h][0]])
            tiles = list(range(NT))
            if d == 1:
                tiles = tiles[::-1]
            pops, pres = pre_ops(d, tiles[0])
            for f in pops:
                f()
            for i, t in enumerate(tiles):
                bops = block_ops(d, t, pres, cur)
                if i + 1 < len(tiles):
                    nops, nres = pre_ops(d, tiles[i + 1])
                else:
                    nops, nres = [], None
                nb = len(bops)
                per = (len(nops) + nb - 2) // max(1, nb - 1) if nops else 0
                import os
                if os.environ.get('P3_NOIL'):
                    per = 0
                pi = 0
                for bi, f in enumerate(bops):
                    f()
                    if bi < nb - 1:
                        for _ in range(per):
                            if pi < len(nops):
                                nops[pi]()
                                pi += 1
                while pi < len(nops):
                    nops[pi]()
                    pi += 1
                pres = nres
            kb.barrier()
        kb.bank_range = (0, 8)


AW = 2048
AGROUPS = ((1, 0), (4, 1), (16, 2))


def host_rope(L):
    half = HD // 2
    inv = (10000.0 ** (-np.arange(half, dtype=np.float32) / half)).astype(np.float32)
    pos = np.arange(L, dtype=np.float32)
    ang = (pos[None, :] * inv[:, None]).astype(np.float32)
    fi = (np.arange(128) % 64) % 32
    return np.stack([np.cos(ang)[fi], np.sin(ang)[fi]], axis=1).astype(NPBF)


def pipeline(stages_list, look=1):
    n = len(stages_list)
    for i in range(min(look, n)):
        stages_list[i][0]()
    for i in range(n):
        if i + look < n:
            stages_list[i + look][0]()
        stages_list[i][1]()


def pipeline_n(items, skew=1):
    n = len(items)
    if n == 0:
        return
    k = len(items[0])
    for step in range(n + (k - 1) * skew):
        for s_ in range(k):
            i = step - s_ * skew
            if 0 <= i < n:
                items[i][s_]()


def phase_p4(kb, G, S, l):
    nc = kb.nc
    L = S.L
    W = AW
    NW = L // W
    with contextlib.ExitStack() as es:
        kb.nuid += 1
        uid = kb.nuid

        def sb(name, shape, dt):
            return es.enter_context(nc.sbuf_tensor(f"{name}_u{uid}", list(shape), dt))

        NQC = W // 512
        KWID = [W + 128 * r for r, g in AGROUPS]
        KOFF = [0, KWID[0], KWID[0] + KWID[1]]
        qz = sb("a_qz", [128, 3, 2, W], BF16)
        qz_b = [[Buf() for c in range(NQC)] for g in range(3)]
        qraw = Rot([(sb(f"a_qraw{i}", [128, 512], BF16), Buf()) for i in range(3)])
        kn = sb("a_kn", [128, sum(KWID)], BF16)
        kn_b = [[Buf() for c in range((KWID[g] + 511) // 512)] for g in range(3)]
        nblk = [r * (W // r // 128 + 1) for r, g in AGROUPS]
        goff = [0, nblk[0], nblk[0] + nblk[1]]
        vt = (sb("a_vt", [128, sum(nblk), 512], BF16), Buf())
        vt_gc = [[Buf() for c in range(r)] for r, g in AGROUPS]
        cs = (sb("a_cs", [128, 2, W + 2048], BF16), Buf())
        acc = (sb("a_acc", [128, 2, W], F32), Buf())
        rec = Rot([(sb(f"a_rec{i}", [128, 512], F32), Buf()) for i in range(2)])
        aout = Rot([(sb(f"a_out{i}", [128, W], BF16), Buf()) for i in range(1)])
        ND_ = 5
        sqt = Rot([(sb(f"a_sq{i}", [128, 512], BF16), Buf()) for i in range(ND_)])
        y1t = Rot([(sb(f"a_y1{i}", [128, 512], BF16), Buf()) for i in range(ND_)])
        y2t = Rot([(sb(f"a_y2{i}", [128, 512], BF16), Buf()) for i in range(ND_)])
        rst = Rot([(sb(f"a_rs{i}", [128, 512], F32), Buf()) for i in range(ND_)])
        ptt = Rot([(sb(f"a_pt{i}", [128, 512], BF16), Buf()) for i in range(4)])
        qgain = G.vecs[:, l * NV + 24:l * NV + 25]
        kgain = G.vecs[:, l * NV + 25:l * NV + 26]
        kb.op("pool", "memset", qz[:], 0.0, W=[b for bb in qz_b for b in bb])

        blkg = sb("a_blkg", [128, 2, 128], BF16)
        ginv = sb("a_ginv", [128, 2], F32)
        blkg_b = Buf()
        kb.op("dve", "tensor_tensor", out=ginv[:], in0=G.vecs[:, l * NV + 24:l * NV + 26], in1=G.vecs[:, l * NV + 24:l * NV + 26],
              op=ALU.mult, R=[G.const_b], W=[blkg_b])
        kb.op("dve", "reciprocal", out=ginv[:], in_=ginv[:], W=[blkg_b])
        for i in range(2):
            kb.op("dve", "tensor_scalar", out=blkg[:, i, :], in0=G.blk, scalar1=ginv[:, i:i + 1], scalar2=None, op0=ALU.mult,
                  R=[G.const_b], W=[blkg_b])

        def nrope_stages(x, xb, c0, n, which, outs=None, outb=None, pre=None):
            st = {}

            def S1():
                if pre is not None:
                    pre()
                s_, s_b = sqt.next()
                kb.op("pool", "tensor_tensor", out=s_[:, 0:n], in0=x, in1=x, op=ALU.mult, R=[xb], W=[s_b])
                y1, y1_b = y1t.next()
                y2, y2_b = y2t.next()
                kb.op("dve", "tensor_tensor", out=y1[:, 0:n], in0=x, in1=cs[0][:, 0, c0:c0 + n], op=ALU.mult, R=[xb, cs[1]], W=[y1_b])
                kb.op("pool", "tensor_tensor", out=y2[:, 0:n], in0=x, in1=cs[0][:, 1, c0:c0 + n], op=ALU.mult, R=[xb, cs[1]], W=[y2_b])
                st.update(s_=s_, s_b=s_b, y1=y1, y1_b=y1_b, y2=y2, y2_b=y2_b)

            def S2():
                b1, b1_b = kb.bank()
                kb.mm(b1[:, 0:n], [(blkg[:, which, :], st["s_"][:, 0:n])], R=[st["s_b"], blkg_b], W=[b1_b])
                b2, b2_b = kb.bank()
                kb.mm(b2[:, 0:n], [(G.ident, st["y1"][:, 0:n]), (G.rot, st["y2"][:, 0:n])], R=[st["y1_b"], st["y2_b"], G.const_b], W=[b2_b])
                st.update(b1=b1, b1_b=b1_b, b2=b2, b2_b=b2_b)

            def S3():
                r_, r_b = rst.next()
                kb.op("act", "activation", out=r_[:, 0:n], in_=st["b1"][:, 0:n], func=AF.Ln, bias=G.epsD[:], scale=1.0 / HD,
                      R=[st["b1_b"], G.const_b], W=[r_b])
                kb.op("act", "activation", out=r_[:, 0:n], in_=r_[:, 0:n], func=AF.Exp, scale=-0.5, R=[r_b], W=[r_b])
                st.update(r_=r_, r_b=r_b)

            def S4():
                b2, b2_b, r_, r_b = st["b2"], st["b2_b"], st["r_"], st["r_b"]
                if outs is None:
                    kb.op("dve", "tensor_tensor", out=x, in0=b2[:, 0:n], in1=r_[:, 0:n], op=ALU.mult, R=[b2_b, r_b], W=[xb])
                else:
                    for i, (dst, ps) in enumerate(outs):
                        kb.op("dve", "tensor_tensor", out=dst, in0=b2[ps, 0:n], in1=r_[ps, 0:n], op=ALU.mult,
                              R=[b2_b, r_b], W=[outb])
            return (S1, S2, S3, S4)

        for w in range(NW):
            w0 = w * W
            first_w = (w == 0)
            last_w = (w == NW - 1)
            lo = max(0, w0 - 1024)
            hi = min(L, w0 + W + 1024)
            kb.dma("sp", cs[0][:, :, lo - (w0 - 1024):hi - (w0 - 1024)], S.rope[:, :, lo:hi], W=[cs[1]])
            qk_t = [sbuf_of(S, "qk", t) for t in range(lo // T, hi // T)]
            v_t = [sbuf_of(S, "v", t) for t in range(lo // T, hi // T)]
            for gi, (r, g) in enumerate(AGROUPS):
                nbw = W // r // 128
                N = 128 * (nbw + 1)
                n_lo = 64 if first_w else 0
                n_hi = N - 64 if last_w else N
                rows = S.v[w0 - 64 * r + r * n_lo:w0 - 64 * r + r * n_hi, g * 512:(g + 1) * 512]
                rv = rows.rearrange("(n r) c -> r n c", r=r)
                for c in range(r):
                    b0 = goff[gi] + c * (nbw + 1)
                    n = n_lo
                    if n_lo > 0:
                        kb.op("pool", "memset", vt[0][0:64, b0, :], 0.0, W=[vt_gc[gi][c]])
                        kb.dma("sp", vt[0][64:128, b0, :], rv[c, 0:64, :], R=v_t, W=[vt_gc[gi][c]])
                        n = 128
                    m_lo = n // 128
                    m_hi = n_hi // 128
                    if m_hi > m_lo:
                        src = rv[c, m_lo * 128 - n_lo:m_hi * 128 - n_lo, :].rearrange("(m p) c -> p m c", p=128)
                        kb.dma("sp", vt[0][:, b0 + m_lo:b0 + m_hi, :], src, R=v_t, W=[vt_gc[gi][c]])
                    if n_hi % 128:
                        kb.op("pool", "memset", vt[0][64:128, b0 + nbw, :], 0.0, W=[vt_gc[gi][c]])
                        kb.dma("sp", vt[0][0:64, b0 + nbw, :], rv[c, m_hi * 128 - n_lo:m_hi * 128 - n_lo + 64, :], R=v_t, W=[vt_gc[gi][c]])
            for hp in range(4):
                kval = []
                for r, g in AGROUPS:
                    j_lo = 64 * r if first_w else 0
                    j_hi = W + 64 * r if last_w else W + 128 * r
                    kval.append((j_lo, j_hi))
                    row0 = 1536 + g * 512 + hp * 128
                    ko = KOFF[g]
                    if j_lo > 0:
                        kb.op("pool", "memset", kn[:, ko:ko + j_lo], 0.0, W=kn_b[g])
                    if j_hi < KWID[g]:
                        kb.op("pool", "memset", kn[:, ko + j_hi:ko + KWID[g]], 0.0, W=kn_b[g])
                    kb.dma("sp", kn[:, ko + j_lo:ko + j_hi], S.qk[row0:row0 + 128, w0 - 64 * r + j_lo:w0 - 64 * r + j_hi],
                           R=qk_t, W=kn_b[g])
                stages = []
                for r, g in AGROUPS:
                    for ci, c0 in enumerate(range(0, W, 512)):
                        qx, qx_b = qraw.next()

                        def pre(qx=qx, qx_b=qx_b, g=g, c0=c0):
                            kb.dma("sp", qx[:], S.qk[g * 512 + hp * 128:g * 512 + hp * 128 + 128, w0 + c0:w0 + c0 + 512], R=qk_t, W=[qx_b])
                        stages.append(nrope_stages(qx[:], qx_b, 1024 + c0, 512, 0,
                                                   outs=[(qz[0:64, g, 0, c0:c0 + 512], slice(0, 64)), (qz[64:128, g, 1, c0:c0 + 512], slice(64, 128))],
                                                   outb=qz_b[g][ci], pre=pre))
                    j_lo, j_hi = kval[g]
                    ko = KOFF[g]
                    for ci in range(len(kn_b[g])):
                        c0 = max(j_lo, ci * 512)
                        c1 = min(j_hi, (ci + 1) * 512)
                        if c1 > c0:
                            stages.append(nrope_stages(kn[:, ko + c0:ko + c1], kn_b[g][ci], 1024 - 64 * r + c0, c1 - c0, 1))
                pipeline_n(stages, skew=1)
                units = []
                for gi, (r, g) in enumerate(AGROUPS):
                    nbw = W // r // 128
                    nb_total = L // r // 128
                    qv = qz[:, g, :, :].rearrange("p s (n r) -> p s r n", r=r)
                    kv = kn[:, KOFF[g]:KOFF[g] + KWID[g]].rearrange("p (n r) -> p r n", r=r)
                    accv = acc[0][:].rearrange("p s (n r) -> p s r n", r=r)
                    for c in range(r):
                        b0 = goff[gi] + c * (nbw + 1)
                        for bq in range(nbw):
                            mA = (w0 // r) // 128 + bq
                            variant = (1 if mA == 0 else 0) + (2 if mA + 1 == nb_total else 0)
                            st = {}
                            is_first = (c == 0 and bq == 0)
                            is_last = (c == r - 1 and bq == nbw - 1)

                            def A(st=st, g=g, qv=qv, kv=kv, c=c, bq=bq, variant=variant):
                                bk, bk_b = kb.bank()
                                qs = qv[:, :, c, 128 * bq:128 * bq + 128]
                                kb.mm_raw([(bk[:, 0:256], kv[:, c, 128 * bq:128 * bq + 128], qs, True, False),
                                           (bk[:, 256:512], kv[:, c, 128 * (bq + 1):128 * (bq + 1) + 128], qs, False, False),
                                           (bk[:, 0:512], G.ident, G.amask[variant], False, True)],
                                          R=qz_b[g] + kn_b[g] + [G.const_b], W=[bk_b])
                                pt, pt_b = ptt.next()
                                kb.op("act", "activation", out=pt[:], in_=bk[:, 0:512], func=AF.Exp, scale=float(HD) ** -0.5, R=[bk_b], W=[pt_b])
                                st.update(pt=pt, pt_b=pt_b)

                            def B(st=st, gi=gi, accv=accv, c=c, bq=bq, b0=b0, is_first=is_first, is_last=is_last, hp=hp):
                                pt, pt_b = st["pt"], st["pt_b"]
                                b2, b2_b = kb.bank()
                                vc = slice(hp * 128, (hp + 1) * 128)
                                kb.mm_multi([(b2[:, 0:256], [(vt[0][:, b0 + bq, vc], pt[:, 0:256]), (vt[0][:, b0 + bq + 1, vc], pt[:, 256:512])]),
                                             (b2[:, 256:512], [(G.ones, pt[:, 0:256]), (G.ones, pt[:, 256:512])])],
                                            R=[pt_b, vt_gc[gi][c], G.const_b], W=[b2_b])
                                tok = None
                                for h in range(2):
                                    hs = slice(h * 64, (h + 1) * 64)
                                    av = accv[hs, :, c, 128 * bq:128 * bq + 128]
                                    b2v = b2[hs, 0:512].rearrange("p (s h n) -> p s h n", s=2, h=2)[:, :, h, :]
                                    Wl = [acc[1]] if (is_first and h == 0) else []
                                    if gi == 0:
                                        tok = kb.op("dve", "tensor_copy", out=av, in_=b2v, R=[b2_b], W=Wl)
                                    else:
                                        tok = kb.op("dve", "tensor_tensor", out=av, in0=av, in1=b2v, op=ALU.add, R=[b2_b], W=Wl)
                                if is_last:
                                    acc[1].w = tok
                                    acc[1].r = {}
                            units.append((A, B))
                pipeline(units, look=2)
                ao, ao_b = aout.next()
                for c0 in range(0, W, 512):
                    rc, rc_b = rec.next()
                    kb.op("act", "activation", out=rc[:], in_=acc[0][:, 1, c0:c0 + 512], func=AF.Ln, R=[acc[1]], W=[rc_b])
                    kb.op("act", "activation", out=rc[:], in_=rc[:], func=AF.Exp, scale=-1.0, R=[rc_b], W=[rc_b])
                    kb.op("dve", "tensor_tensor", out=ao[:, c0:c0 + 512], in0=acc[0][:, 0, c0:c0 + 512], in1=rc[:], op=ALU.mult,
                          R=[acc[1], rc_b], W=[ao_b])
                kb.dma("pool", S.att[hp * 128:(hp + 1) * 128, w0:w0 + W], ao[:], R=[ao_b],
                       W=[sbuf_of(S, "att", t) for t in range(w0 // T, (w0 + W) // T)])
        kb.barrier()


LP = 16384
LS = 2048
NCORES = 8
_CACHE = {}


def kernel(**inputs):
    inputs = {k: np.asarray(v) for k, v in inputs.items()}
    xp = inputs["x_prompt"]
    xs = inputs["x_sample"]
    if "prog" not in _CACHE:
        _CACHE["prog"] = build(LP, LS)
    kb, seqs = _CACHE["prog"]
    wsrc = host_pack_weights(inputs)
    vecs = host_vecs(inputs)
    cbf = host_consts()
    lbl = host_lbl(inputs)
    pfb, pff = host_fnet_consts(LP)
    sfb, sff = host_fnet_consts(LS)
    prope = host_rope(LP)
    srope = host_rope(LS)
    zeros_p = np.zeros((D, LP), np.float32)
    in_maps = []
    for c in range(NCORES):
        px = np.ascontiguousarray(xp[c].T) if c < xp.shape[0] else zeros_p
        in_maps.append({
            "wsrc": wsrc, "vecs": vecs, "cbf": cbf, "lbl": lbl,
            "p_x": px, "s_x": np.ascontiguousarray(xs[c].T),
            "p_fnb": pfb, "p_fnf": pff, "s_fnb": sfb, "s_fnf": sff,
            "p_rope": prope, "s_rope": srope,
        })
    res = run_bass_kernel_spmd(kb.nc, in_maps, core_ids=list(range(NCORES)))
    yp = np.stack([np.asarray(res.results[c]["p_y"]).T for c in range(xp.shape[0])], axis=0).astype(np.float32)
    ys = np.stack([np.asarray(res.results[c]["s_y"]).T for c in range(NCORES)], axis=0).astype(np.float32)
    return (np.ascontiguousarray(yp), np.ascontiguousarray(ys))
```

```python
import contextlib
import numpy as np
import ml_dtypes
import concourse.bass as bass
import concourse.mybir as mybir
from concourse.bass_utils import run_bass_kernel_spmd

F32 = mybir.dt.float32
BF16 = mybir.dt.bfloat16
AF = mybir.ActivationFunctionType
ALU = mybir.AluOpType
NPBF = ml_dtypes.bfloat16

D = 1024
DFF = 2816
DEPTH = 2
KC = D // 128
FC = DFF // 128
EPS = 1e-6
T = 512
HD = 64
NQK = 3072
O_Q, O_K, O_V, O_ZF, O_ZB, O_QH, O_IH, O_GH, O_UF, O_GL = 0, 1536, 3072, 4608, 5120, 5632, 6144, 6656, 7168, 7680
IN_COLS = 10752
GW = 256
GWT = 512


class Buf:
    __slots__ = ("name", "w", "r")

    def __init__(self, name=""):
        self.name = name
        self.w = None
        self.r = {}


class Eng:
    def __init__(self, name, obj, sem):
        self.name, self.obj, self.sem = name, obj, sem
        self.cnt = 0
        self.waited = {}


class KB:
    def __init__(self):
        self.nc = bass.Bass("TRN2", target_bir_lowering=False)
        self.es = contextlib.ExitStack()
        nc = self.nc
        self.eng = {}
        for nm, obj in (("pe", nc.tensor), ("act", nc.scalar), ("dve", nc.vector),
                        ("pool", nc.gpsimd), ("sp", nc.sync)):
            sem = self.es.enter_context(nc.semaphore("s_" + nm))
            self.eng[nm] = Eng(nm, obj, sem)
        self.ND = 12
        self.dslots = {}
        self.dcnt = {}
        for q in ("sp", "pool", "act"):
            self.dslots[q] = [[self.es.enter_context(nc.semaphore(f"d_{q}{i}")), 0] for i in range(self.ND)]
            self.dcnt[q] = 0
        self.nuid = 0
        self.psum_banks = []
        self.psum_i = 0
        self.bank_range = (0, 8)

    def sbuf(self, name, shape, dtype):
        return self.es.enter_context(self.nc.sbuf_tensor(name, list(shape), dtype))

    def psum(self, name, shape, dtype):
        return self.es.enter_context(self.nc.psum_tensor(name, list(shape), dtype))

    def dram(self, name, shape, dtype, kind="Internal"):
        return self.nc.dram_tensor(name, list(shape), dtype, kind=kind).ap()

    def _wait(self, eng, toks):
        best = {}
        for sem, val in toks:
            k = id(sem)
            if k not in best or best[k][1] < val:
                best[k] = (sem, val)
        for k, (sem, val) in best.items():
            if eng.waited.get(k, 0) >= val:
                continue
            if sem is eng.sem and eng.name == "pe":
                continue
            eng.obj.wait_ge(sem, val)
            eng.waited[k] = val

    @staticmethod
    def _deps(R, W):
        toks = []
        for b in R:
            if b.w is not None:
                toks.append(b.w)
        for b in W:
            if b.w is not None:
                toks.append(b.w)
            toks.extend(b.r.values())
        return toks

    @staticmethod
    def _commit(tok, R, W):
        k = id(tok[0])
        for b in R:
            b.r[k] = tok
        for b in W:
            b.w = tok
            b.r = {}

    def op(self, E, method, *args, R=(), W=(), **kw):
        eng = self.eng[E]
        self._wait(eng, self._deps(R, W))
        ins = getattr(eng.obj, method)(*args, **kw)
        eng.cnt += 1
        ins.then_inc(eng.sem, 1)
        tok = (eng.sem, eng.cnt)
        self._commit(tok, R, W)
        return tok

    def mm_raw(self, items, R=(), W=()):
        eng = self.eng["pe"]
        self._wait(eng, self._deps(R, W))
        ins = None
        for (out, l, r, st, sp) in items:
            ins = eng.obj.matmul(out, lhsT=l, rhs=r, start=st, stop=sp)
        eng.cnt += 1
        ins.then_inc(eng.sem, 1)
        tok = (eng.sem, eng.cnt)
        self._commit(tok, R, W)
        return tok

    def mm(self, out, pairs, R=(), W=(), Rper=None, **kw):
        eng = self.eng["pe"]
        self._wait(eng, self._deps(R, W))
        n = len(pairs)
        ins = None
        for i, (l, r) in enumerate(pairs):
            if Rper is not None:
                self._wait(eng, self._deps(Rper[i], ()))
            ins = eng.obj.matmul(out, lhsT=l, rhs=r, start=(i == 0), stop=(i == n - 1), **kw)
        eng.cnt += 1
        ins.then_inc(eng.sem, 1)
        tok = (eng.sem, eng.cnt)
        allR = list(R)
        if Rper is not None:
            for rr in Rper:
                allR += list(rr)
        self._commit(tok, allR, W)
        return tok

    def mm_multi(self, groups, R=(), W=()):
        eng = self.eng["pe"]
        self._wait(eng, self._deps(R, W))
        ins = None
        for out, pairs in groups:
            n = len(pairs)
            for i, (l, r) in enumerate(pairs):
                ins = eng.obj.matmul(out, lhsT=l, rhs=r, start=(i == 0), stop=(i == n - 1))
        eng.cnt += 1
        ins.then_inc(eng.sem, 1)
        tok = (eng.sem, eng.cnt)
        self._commit(tok, R, W)
        return tok

    def dma(self, q, out, in_, R=(), W=(), **kw):
        eng = self.eng[q]
        toks = self._deps(R, W)
        slot = self.dslots[q][self.dcnt[q] % self.ND]
        self.dcnt[q] += 1
        if slot[1] > 0:
            toks.append((slot[0], slot[1]))
        self._wait(eng, toks)
        ins = eng.obj.dma_start(out=out, in_=in_, **kw)
        slot[1] += 16
        ins.then_inc(slot[0], 16)
        tok = (slot[0], slot[1])
        self._commit(tok, R, W)
        return tok

    def barrier(self):
        toks = [(e.sem, e.cnt) for e in self.eng.values() if e.cnt > 0]
        for q in self.dslots:
            toks += [(sl[0], sl[1]) for sl in self.dslots[q] if sl[1] > 0]
        for e in self.eng.values():
            self._wait(e, toks)

    def bank(self):
        lo, hi = self.bank_range
        b = self.psum_banks[lo + self.psum_i % (hi - lo)]
        self.psum_i += 1
        return b


class Rot:
    def __init__(self, items):
        self.items = items
        self.i = 0

    def next(self):
        it = self.items[self.i % len(self.items)]
        self.i += 1
        return it


def weight_plan():
    tiles = []
    groups = {}

    def add(gname, pieces):
        groups.setdefault(gname, []).append(len(tiles))
        tiles.append(pieces)

    for f in ("ffn1", "ffn2"):
        for mt in range(DFF // GW):
            add(f + "_gu", [(f + "_w_gate", D, mt * GW, GW), (f + "_w_up", D, mt * GW, GW)])
        for m in range(KC):
            add(f + "_down", [(f + "_w_down", DFF, m * 128, 128)])
    for c0 in list(range(O_Q, O_V, 512)) + list(range(O_ZF, O_IH, 512)) + [O_GH, O_UF]:
        add("in_fm", [("w_in", D, c0, 512)])
    for c0 in list(range(O_V, O_ZF, 512)) + [O_IH]:
        add("in_tm", [("w_in", D, c0, 512)])
    for m in range(KC):
        add("p5m", [("w_in", D, O_GL + br * D + m * 128, 128) for br in range(3)]
            + [(w, 512, m * 128, 128) for w in ("w_att_out", "w_hgrn_out", "w_fnet_out")])
    for c0 in (0, 512):
        add("w_out", [("w_out", D, c0, 512)])
    sizes = [sum(k * g for (_, k, _, g) in t) for t in tiles]
    offs = list(np.cumsum([0] + sizes[:-1]))
    return tiles, groups, [int(o) for o in offs], sizes, int(sum(sizes))


WTILES, WGROUPS, WOFFS, WSIZES, WLAYER = weight_plan()
IN_FM_COLS = list(range(O_Q, O_V, 512)) + list(range(O_ZF, O_IH, 512)) + [O_GH, O_UF]
IN_TM_COLS = list(range(O_V, O_ZF, 512)) + [O_IH]
WSLOT = max(s // 128 for s in WSIZES)
CAST_CH = 128 * 8192
WTOTAL = ((DEPTH * WLAYER + CAST_CH - 1) // CAST_CH) * CAST_CH


def host_pack_weights(inputs):
    flat = np.zeros(WTOTAL, np.float32)
    ws = {k: np.asarray(v) for k, v in inputs.items() if k.startswith("w_") or "_w_" in k}
    for l in range(DEPTH):
        for ti, pieces in enumerate(WTILES):
            blks = []
            for (wname, kdim, c0, g) in pieces:
                w = ws[wname][l]
                kc = kdim // 128
                blks.append(w[:, c0:c0 + g].reshape(kc, 128, g).transpose(1, 0, 2).reshape(128, kc * g))
            blk = np.concatenate(blks, axis=1)
            o = l * WLAYER + WOFFS[ti]
            flat[o:o + blk.size] = blk.reshape(-1)
    return flat


class WStream:
    def __init__(self, kb, G, order):
        self.kb, self.G, self.order = kb, G, order
        self.pos = 0
        self.issued = 0

    def _issue(self, i):
        G = self.G
        l, ti = self.order[i]
        t, b = G.wslots[G.wslot_i % len(G.wslots)]
        G.wslot_i += 1
        n = WSIZES[ti] // 128
        o = l * WLAYER + WOFFS[ti]
        src = G.wbf[o:o + WSIZES[ti]].rearrange("(p n) -> p n", p=128)
        self.kb.dma("sp", t[:, 0:n], src, R=[G.wcast_b], W=[b])
        return (t, b)

    def get(self):
        depth = len(self.G.wslots)
        if self.pos == 0:
            self.live = []
        while self.issued < min(self.pos + depth, len(self.order)):
            self.live.append(self._issue(self.issued))
            self.issued += 1
        r = self.live[self.pos]
        self.pos += 1
        return r


class Glob:
    pass


def seq_scratch(kb, name, L, dbg):
    S = Glob()
    S.L = L
    S.name = name

    def mk(nm, shape, dt):
        kind = "Internal"
        if nm in dbg.get("out", ()):
            kind = "ExternalOutput"
        if nm in dbg.get("in", ()):
            kind = "ExternalInput"
        return kb.dram(f"{name}_{nm}", shape, dt, kind=kind)

    S.x = [kb.dram(f"{name}_x", [D, L], F32, kind="ExternalInput")]
    for l in range(1, DEPTH):
        S.x.append(mk(f"xl{l}", [D, L], F32))
    S.y = kb.dram(f"{name}_y", [D, L], F32, kind="ExternalOutput")
    S.x1 = mk("x1", [D, L], F32)
    S.qk = mk("qk", [NQK, L], BF16)
    S.v = mk("v", [L, 1536], BF16)
    S.z = mk("z", [1024, L], F32)
    S.qh = mk("qh", [512, L], BF16)
    S.gh = mk("gh", [512, L], BF16)
    S.ih = mk("ih", [L, 512], BF16)
    S.ucs = mk("ucs", [L, 1024], BF16)
    S.att = mk("att", [512, L], BF16)
    S.hg = mk("hg", [512, L], BF16)
    S.fn = mk("fn", [512, L], BF16)
    S.ob = mk("ob", [512, L], F32)
    S.fnb = kb.dram(f"{name}_fnb", [128, 4 * (L // 128) + 256], BF16, kind="ExternalInput")
    S.fnf = kb.dram(f"{name}_fnf", [128, 2 * (L // 128)], F32, kind="ExternalInput")
    S.rope = kb.dram(f"{name}_rope", [128, 2, L], BF16, kind="ExternalInput")
    S.bufs = {}
    return S


def sbuf_of(S, nm, t):
    k = (nm, t)
    if k not in S.bufs:
        S.bufs[k] = Buf(f"{S.name}_{nm}_{t}")
    return S.bufs[k]


def rmsnorm_a(kb, G, P, src, src_b):
    sq, sq_b = P.sqr.next()
    for c in range(KC):
        e = "pool" if c % 3 == 2 else "dve"
        kb.op(e, "tensor_tensor", out=sq[:, c, :], in0=src[:, c, :], in1=src[:, c, :], op=ALU.mult, R=[src_b[c]], W=[sq_b[c]])
    return sq, sq_b


def rmsnorm_b(kb, G, P, sqh, src, src_b, gcol, dst, dst_b):
    sq, sq_b = sqh
    bk, bk_b = kb.bank()
    kb.mm(bk[:, 0:T], [(G.ones[:], sq[:, c, :]) for c in range(KC)], R=[G.const_b], W=[bk_b], Rper=[[sq_b[c]] for c in range(KC)])
    rs, rs_b = P.rstd.next()
    kb.op("act", "activation", out=rs[:], in_=bk[:, 0:T], func=AF.Sqrt, bias=G.epsD[:], scale=1.0 / D, R=[bk_b, G.const_b], W=[rs_b])
    kb.op("dve", "reciprocal", out=rs[:], in_=rs[:], R=[rs_b], W=[rs_b])
    for c in range(KC):
        kb.op("dve", "scalar_tensor_tensor", out=dst[:, c, :], in0=src[:, c, :], scalar=G.vecs[:, gcol + c:gcol + c + 1],
              in1=rs[:], op0=ALU.mult, op1=ALU.mult, R=[src_b[c], rs_b, G.const_b], W=[dst_b[c]])


def rmsnorm(kb, G, P, src, src_b, gcol, dst, dst_b):
    sqh = rmsnorm_a(kb, G, P, src, src_b)
    rmsnorm_b(kb, G, P, sqh, src, src_b, gcol, dst, dst_b)


def ffn(kb, G, P, ws, xn, xn_b, resid, resid_b, out, out_b, hook=None):
    for mt in range(DFF // GW):
        if mt == 5 and hook is not None:
            hook()
        wt, wb = ws.get()
        w = wt[:, 0:2 * KC * GW].rearrange("p (s k g) -> p s k g", s=2, k=KC)
        for j in range(GW // 128):
            m = mt * (GW // 128) + j
            bg, bg_b = kb.bank()
            bu, bu_b = kb.bank()
            kb.mm(bg[:, 0:T], [(w[:, 0, k, j * 128:(j + 1) * 128], xn[:, k, :]) for k in range(KC)], R=[wb], W=[bg_b],
                  Rper=[[xn_b[k]] for k in range(KC)])
            kb.mm(bu[:, 0:T], [(w[:, 1, k, j * 128:(j + 1) * 128], xn[:, k, :]) for k in range(KC)], R=[wb] + xn_b, W=[bu_b])
            sg, sg_b = P.sgt.next()
            kb.op("act", "activation", out=sg[:], in_=bg[:, 0:T], func=AF.Silu, R=[bg_b], W=[sg_b])
            kb.op("dve", "tensor_tensor", out=P.hid[:, m, :], in0=sg[:], in1=bu[:, 0:T], op=ALU.mult,
                  R=[sg_b, bu_b], W=[P.hid_b[m]])
    for m in range(KC):
        wt, wb = ws.get()
        w = wt[:, 0:FC * 128].rearrange("p (k g) -> p k g", k=FC)
        bk, bk_b = kb.bank()
        kb.mm(bk[:, 0:T], [(w[:, k, :], P.hid[:, k, :]) for k in range(FC)], R=[wb] + P.hid_b, W=[bk_b])
        kb.op("dve", "scalar_tensor_tensor", out=out[:, m, :], in0=bk[:, 0:T], scalar=0.5, in1=resid[:, m, :],
              op0=ALU.mult, op1=ALU.add, R=[bk_b, resid_b[m]], W=[out_b[m]])


def fm_view(ap2d, c0, nchunk, t0, n):
    return ap2d[c0:c0 + nchunk * 128, t0:t0 + n].rearrange("(c p) t -> p c t", p=128)


def alloc_dense(kb, es, nc):
    P = Glob()

    kb.nuid += 1
    uid = kb.nuid

    def sb(name, shape, dt):
        return es.enter_context(nc.sbuf_tensor(f"{name}_u{uid}", list(shape), dt))
    P.sb = sb

    P.xin = Rot([(sb(f"xin{i}", [128, KC, T], F32), [Buf() for _ in range(KC)]) for i in range(2)])
    P.sqr = Rot([(sb(f"sq{i}", [128, KC, T], BF16), [Buf() for _ in range(KC)]) for i in range(2)])
    P.rstd = Rot([(sb(f"rstd{i}", [128, T], F32), Buf()) for i in range(2)])
    P.xn = Rot([(sb(f"xn{i}", [128, KC, T], BF16), [Buf() for _ in range(KC)]) for i in range(3)])
    P.hid = sb("hid", [128, FC, T], BF16)
    P.hid_b = [Buf() for _ in range(FC)]
    P.sgt = Rot([(sb(f"sgt{i}", [128, T], F32), Buf()) for i in range(3)])
    P.x1 = Rot([(sb(f"x1_{i}", [128, KC, T], F32), [Buf() for _ in range(KC)]) for i in range(1)])
    P.stg = Rot([(sb(f"stg{i}", [128, 4, T], BF16), Buf()) for i in range(2)])
    P.stgf = Rot([(sb(f"stgf{i}", [128, 4, T], F32), Buf()) for i in range(1)])
    P.wslots = [(sb(f"wslot{i}", [128, WSLOT], BF16), Buf()) for i in range(5)]
    return P


def phase_p1(kb, G, S, l):
    nc = kb.nc
    L = S.L
    NT = L // T
    with contextlib.ExitStack() as es:
        P = alloc_dense(kb, es, nc)
        G.wslots = P.wslots
        order = []
        for t in range(NT):
            order += [(l, i) for i in WGROUPS["ffn1_gu"] + WGROUPS["ffn1_down"] + WGROUPS["in_fm"] + WGROUPS["in_tm"]]
        ws = WStream(kb, G, order)
        vb = l * G.NV
        def preA(t):
            xin, xin_b = P.xin.next()
            kb.dma("sp", xin[:], fm_view(S.x[l], 0, KC, t * T, T), R=[sbuf_of(S, f"x{l}", t)], W=xin_b)
            sqh = rmsnorm_a(kb, G, P, xin, xin_b)
            return (xin, xin_b, sqh)

        def preB(pa):
            xin, xin_b, sqh = pa
            xn, xn_b = P.xn.next()
            rmsnorm_b(kb, G, P, sqh, xin, xin_b, vb + 0, xn, xn_b)
            return (xin, xin_b, xn, xn_b)

        nxt = preB(preA(0))
        for t in range(NT):
            t0 = t * T
            xin, xin_b, xn, xn_b = nxt
            nxt_a = None
            x1, x1_b = P.x1.next()
            ffn(kb, G, P, ws, xn, xn_b, xin, xin_b, x1, x1_b)
            kb.dma("pool", fm_view(S.x1, 0, KC, t0, T), x1[:], R=x1_b, W=[sbuf_of(S, "x1", t)])
            hn, hn_b = P.xn.next()
            rmsnorm(kb, G, P, x1, x1_b, vb + 8, hn, hn_b)
            if t + 1 < NT:
                nxt_a = preA(t + 1)
            for j, c0 in enumerate(IN_FM_COLS):
                if j == 5 and nxt_a is not None:
                    nxt = preB(nxt_a)
                wt, wb = ws.get()
                w = wt[:, 0:KC * 512].rearrange("p (k g) -> p k g", k=KC)
                isz = (O_ZF <= c0 < O_QH)
                isu = (c0 == O_UF)
                st, st_b = (P.stgf if isz else P.stg).next()
                for mc in range(4):
                    bk, bk_b = kb.bank()
                    kb.mm(bk[:, 0:T], [(w[:, k, mc * 128:(mc + 1) * 128], hn[:, k, :]) for k in range(KC)],
                          R=[wb], W=[bk_b], Rper=[[hn_b[k]] for k in range(KC)])
                    if c0 < O_V:
                        gcol = G.vecs[:, vb + 24:vb + 25] if c0 < O_K else G.vecs[:, vb + 25:vb + 26]
                        if mc % 2 == 0:
                            kb.op("act", "activation", out=st[:, mc, :], in_=bk[:, 0:T], func=AF.Copy, scale=gcol, R=[bk_b, G.const_b], W=[st_b])
                        else:
                            kb.op("dve", "tensor_scalar", out=st[:, mc, :], in0=bk[:, 0:T], scalar1=gcol, scalar2=None, op0=ALU.mult,
                                  R=[bk_b, G.const_b], W=[st_b])
                    elif mc % 2 == 0:
                        kb.op("act", "activation", out=st[:, mc, :], in_=bk[:, 0:T], func=AF.Copy, R=[bk_b], W=[st_b])
                    else:
                        kb.op("dve", "tensor_copy", out=st[:, mc, :], in_=bk[:, 0:T], R=[bk_b], W=[st_b])
                if c0 < O_V:
                    kb.dma("pool", fm_view(S.qk, c0, 4, t0, T), st[:], R=[st_b], W=[sbuf_of(S, "qk", t)])
                elif isz:
                    kb.dma("pool", fm_view(S.z, c0 - O_ZF, 4, t0, T), st[:], R=[st_b], W=[sbuf_of(S, "z", t)])
                elif c0 == O_QH:
                    kb.dma("pool", fm_view(S.qh, 0, 4, t0, T), st[:], R=[st_b], W=[sbuf_of(S, "qh", t)])
                elif c0 == O_GH:
                    kb.dma("pool", fm_view(S.gh, 0, 4, t0, T), st[:], R=[st_b], W=[sbuf_of(S, "gh", t)])
                elif isu:
                    for tb in range(4):
                        st2, st2_b = P.stg.next()
                        for g in range(4):
                            bk, bk_b = kb.bank()
                            kb.mm(bk[:, 0:256], [(st[:, g, tb * 128:(tb + 1) * 128], G.ccsc[:])],
                                  R=[st_b, G.const_b], W=[bk_b])
                            e = "act" if g % 2 == 0 else "dve"
                            if e == "act":
                                kb.op("act", "activation", out=st2[:, g, 0:256], in_=bk[:, 0:256], func=AF.Copy, R=[bk_b], W=[st2_b])
                            else:
                                kb.op("dve", "tensor_copy", out=st2[:, g, 0:256], in_=bk[:, 0:256], R=[bk_b], W=[st2_b])
                        dst = S.ucs[t0 + tb * 128:t0 + (tb + 1) * 128, :].rearrange("p (g c) -> p g c", g=4)
                        kb.dma("pool", dst, st2[:, :, 0:256], R=[st2_b], W=[sbuf_of(S, "ucs", t)])
            for j, c0 in enumerate(IN_TM_COLS):
                wt, wb = ws.get()
                w = wt[:, 0:KC * 512].rearrange("p (k g) -> p k g", k=KC)
                st, st_b = P.stg.next()
                for tb in range(4):
                    bk, bk_b = kb.bank()
                    kb.mm(bk[:, 0:512], [(hn[:, k, tb * 128:(tb + 1) * 128], w[:, k, :]) for k in range(KC)],
                          R=[wb] + hn_b, W=[bk_b])
                    if tb % 2 == 0:
                        kb.op("act", "activation", out=st[:, tb, :], in_=bk[:, 0:512], func=AF.Copy, R=[bk_b], W=[st_b])
                    else:
                        kb.op("dve", "tensor_copy", out=st[:, tb, :], in_=bk[:, 0:512], R=[bk_b], W=[st_b])
                if c0 < O_ZF:
                    dst = S.v[t0:t0 + T, c0 - O_V:c0 - O_V + 512].rearrange("(b p) c -> p b c", p=128)
                    kb.dma("pool", dst, st[:], R=[st_b], W=[sbuf_of(S, "v", t)])
                else:
                    dst = S.ih[t0:t0 + T, :].rearrange("(b p) c -> p b c", p=128)
                    kb.dma("pool", dst, st[:], R=[st_b], W=[sbuf_of(S, "ih", t)])
        kb.barrier()


def phase_p5(kb, G, S, l):
    nc = kb.nc
    L = S.L
    NT = L // T
    last = (l + 1 >= G.nlayers)
    xout = S.y if last else S.x[l + 1]
    xout_nm = "y" if last else f"x{l + 1}"
    with contextlib.ExitStack() as es:
        P = alloc_dense(kb, es, nc)
        G.wslots = P.wslots
        P.br = [(P.sb(f"br{i}", [128, 4, T], BF16), Buf()) for i in range(3)]
        order = []
        for t in range(NT):
            order += [(l, i) for i in WGROUPS["p5m"] + WGROUPS["w_out"] + WGROUPS["ffn2_gu"] + WGROUPS["ffn2_down"]]
        ws = WStream(kb, G, order)
        vb = l * G.NV
        def preA(t):
            x1, x1_b = P.xin.next()
            kb.dma("sp", x1[:], fm_view(S.x1, 0, KC, t * T, T), R=[sbuf_of(S, "x1", t)], W=x1_b)
            for i, (nm, src) in enumerate((("att", S.att), ("hg", S.hg), ("fn", S.fn))):
                kb.dma("sp", P.br[i][0][:], fm_view(src, 0, 4, t * T, T), R=[sbuf_of(S, nm, t)], W=[P.br[i][1]])
            sqh = rmsnorm_a(kb, G, P, x1, x1_b)
            return (x1, x1_b, sqh)

        def preB(pa):
            x1, x1_b, sqh = pa
            hn, hn_b = P.xn.next()
            rmsnorm_b(kb, G, P, sqh, x1, x1_b, vb + 8, hn, hn_b)
            return (x1, x1_b, hn, hn_b)

        nxt = [preB(preA(0))]
        for t in range(NT):
            t0 = t * T
            x1, x1_b, hn, hn_b = nxt[0]
            mg, mg_b = P.xn.next()
            for m in range(KC):
                wt, wb = ws.get()
                wg = wt[:, 0:3 * KC * 128].rearrange("p (b k g) -> p b k g", b=3, k=KC)
                wo = wt[:, 3 * KC * 128:3 * KC * 128 + 3 * 4 * 128].rearrange("p (b k g) -> p b k g", b=3, k=4)
                prods = []
                for br in range(3):
                    bg, bg_b = kb.bank()
                    bp, bp_b = kb.bank()
                    kb.mm(bg[:, 0:T], [(wg[:, br, k, :], hn[:, k, :]) for k in range(KC)], R=[wb], W=[bg_b], Rper=[[hn_b[k]] for k in range(KC)])
                    kb.mm(bp[:, 0:T], [(wo[:, br, k, :], P.br[br][0][:, k, :]) for k in range(4)],
                          R=[wb, P.br[br][1]], W=[bp_b])
                    sg, sg_b = P.sgt.next()
                    kb.op("act", "activation", out=sg[:], in_=bg[:, 0:T], func=AF.Sigmoid, R=[bg_b], W=[sg_b])
                    kb.op("dve", "tensor_tensor", out=sg[:], in0=sg[:], in1=bp[:, 0:T], op=ALU.mult,
                          R=[sg_b, bp_b], W=[sg_b])
                    prods.append((sg, sg_b))
                kb.op("pool", "tensor_tensor", out=prods[0][0][:], in0=prods[0][0][:], in1=prods[1][0][:], op=ALU.add,
                      R=[prods[1][1]], W=[prods[0][1]])
                kb.op("pool", "tensor_tensor", out=mg[:, m, :], in0=prods[0][0][:], in1=prods[2][0][:], op=ALU.add,
                      R=[prods[0][1], prods[2][1]], W=[mg_b[m]])
            x2, x2_b = P.x1.next()
            for j in range(2):
                wt, wb = ws.get()
                w = wt[:, 0:KC * 512].rearrange("p (k g) -> p k g", k=KC)
                for mc in range(4):
                    m = j * 4 + mc
                    bk, bk_b = kb.bank()
                    kb.mm(bk[:, 0:T], [(w[:, k, mc * 128:(mc + 1) * 128], mg[:, k, :]) for k in range(KC)],
                          R=[wb], W=[bk_b], Rper=[[mg_b[k]] for k in range(KC)])
                    kb.op("dve", "tensor_tensor", out=x2[:, m, :], in0=bk[:, 0:T], in1=x1[:, m, :], op=ALU.add,
                          R=[bk_b, x1_b[m]], W=[x2_b[m]])
            xn, xn_b = P.xn.next()
            rmsnorm(kb, G, P, x2, x2_b, vb + 16, xn, xn_b)
            x3, x3_b = P.xin.next()
            hook = None
            if t + 1 < NT:
                pa = preA(t + 1)

                def hook(pa=pa):
                    nxt[0] = preB(pa)
            ffn(kb, G, P, ws, xn, xn_b, x2, x2_b, x3, x3_b, hook=hook)
            G.out_toks.append(kb.dma("pool", fm_view(xout, 0, KC, t0, T), x3[:], R=x3_b, W=[sbuf_of(S, xout_nm, t)]))
        kb.barrier()


NV = 32
CB_ONES, CB_CCSC, CB_IDENT, CB_BLK, CB_ROT, CB_HMF, CB_HMB, CB_AM, CB_END = 0, 128, 384, 512, 640, 768, 896, 1024, 1024 + 4 * 512


def host_consts():
    cb = np.zeros((128, CB_END), np.float32)
    cb[:, CB_ONES:CB_ONES + 128] = 1.0
    c = np.arange(128)
    ang = 2 * np.pi * np.outer(c, c) / 128.0
    cb[:, CB_CCSC:CB_CCSC + 128] = np.cos(ang) / np.sqrt(128.0)
    cb[:, CB_CCSC + 128:CB_CCSC + 256] = np.sin(ang) / np.sqrt(128.0)
    cb[:, CB_IDENT:CB_IDENT + 128] = np.eye(128)
    blk = np.zeros((128, 128))
    blk[:64, :64] = 1
    blk[64:, 64:] = 1
    cb[:, CB_BLK:CB_BLK + 128] = blk
    rot = np.zeros((128, 128))
    for h in range(2):
        for i in range(32):
            rot[h * 64 + 32 + i, h * 64 + i] = -1.0
            rot[h * 64 + i, h * 64 + 32 + i] = 1.0
    cb[:, CB_ROT:CB_ROT + 128] = rot
    sidx = np.arange(128)[:, None]
    tidx = np.arange(128)[None, :]
    same = (sidx // 64) == (tidx // 64)
    cb[:, CB_HMF:CB_HMF + 128] = (same & (sidx <= tidx))
    cb[:, CB_HMB:CB_HMB + 128] = (same & (sidx >= tidx))
    m1 = (tidx <= sidx).astype(np.float32)
    m2 = (tidx >= sidx).astype(np.float32)
    m1f = m1 * (sidx >= 64)
    m2l = m2 * (sidx < 64)
    for v, (a, b) in enumerate(((m1, m2), (m1f, m2), (m1, m2l), (m1f, m2l))):
        cb[:, CB_AM + v * 512:CB_AM + (v + 1) * 512] = (np.concatenate([a, a, b, b], axis=1) - 1.0) * 30000.0
    return cb.astype(NPBF)


def host_vecs(inputs):
    v = np.zeros((128, DEPTH * NV), np.float32)
    for l in range(DEPTH):
        b = l * NV
        for i, nm in enumerate(("ffn1_norm", "mix_norm", "ffn2_norm")):
            v[:, b + 8 * i:b + 8 * i + 8] = np.asarray(inputs[nm])[l].reshape(8, 128).T
        v[:, b + 24] = np.tile(np.asarray(inputs["q_norm"])[l], 2)
        v[:, b + 25] = np.tile(np.asarray(inputs["k_norm"])[l], 2)
        v[:, b + 26] = np.asarray(inputs["hgrn_out_norm"])[l]
    return v


def host_lbl(inputs):
    lg = np.asarray(inputs["hgrn_lb_logits"])
    return np.ascontiguousarray(lg.reshape(DEPTH * 2 * 4, 128).T)


def build(Lp, Ls, nlayers=DEPTH, dbg=None, phases=("p1", "p2", "p3", "p4", "p5")):
    dbg = dbg or {}
    kb = KB()
    nc = kb.nc
    G = Glob()
    G.NV = NV
    G.nlayers = nlayers
    G.out_toks = []
    wsrc = kb.dram("wsrc", [WTOTAL], F32, kind="ExternalInput")
    G.wbf = kb.dram("wbf", [WTOTAL], BF16)
    vecs_d = kb.dram("vecs", [128, DEPTH * NV], F32, kind="ExternalInput")
    cb_d = kb.dram("cbf", [128, CB_END], BF16, kind="ExternalInput")
    G.vecs = kb.sbuf("vecs_sb", [128, DEPTH * NV], F32)
    G.cb = kb.sbuf("cb_sb", [128, CB_END], BF16)
    G.ones = G.cb[:, CB_ONES:CB_ONES + 128]
    G.ccsc = G.cb[:, CB_CCSC:CB_CCSC + 256]
    G.ident = G.cb[:, CB_IDENT:CB_IDENT + 128]
    G.blk = G.cb[:, CB_BLK:CB_BLK + 128]
    G.rot = G.cb[:, CB_ROT:CB_ROT + 128]
    G.hmask = [G.cb[:, CB_HMF:CB_HMF + 128], G.cb[:, CB_HMB:CB_HMB + 128]]
    G.amask = [G.cb[:, CB_AM + v * 512:CB_AM + (v + 1) * 512] for v in range(4)]
    G.const_b = Buf("const")
    G.epsc = kb.sbuf("epsc", [128, 2], F32)
    G.epsD = G.epsc[:, 0:1]
    G.wcast_b = Buf("wcast")
    G.wslot_i = 0
    for i in range(8):
        kb.psum_banks.append((kb.psum(f"bank{i}", [128, 512], F32), Buf()))
    seqs = []
    if Lp:
        seqs.append(seq_scratch(kb, "p", Lp, dbg))
    if Ls:
        seqs.append(seq_scratch(kb, "s", Ls, dbg))
    kb.op("dve", "memset", G.epsc[:], EPS, W=[G.const_b])
    kb.dma("sp", G.vecs[:], vecs_d[:, :], W=[G.const_b])
    kb.dma("sp", G.cb[:], cb_d[:, :], W=[G.const_b])
    lbl_d = kb.dram("lbl", [128, DEPTH * 8], F32, kind="ExternalInput")
    G.lbe = kb.sbuf("lbe", [128, DEPTH * 8], F32)
    G.lb = kb.sbuf("lb", [128, DEPTH * 8], F32)
    G.oml = kb.sbuf("oml", [128, DEPTH * 8], F32)
    G.lbt = kb.sbuf("lbt", [128, 8], F32)
    G.rmask = kb.sbuf("rmask", [128, T], F32)
    lb_b = Buf()
    kb.dma("sp", G.lbe[:], lbl_d[:, :], W=[lb_b])
    kb.op("act", "activation", out=G.lbe[:], in_=G.lbe[:], func=AF.Exp, R=[lb_b], W=[lb_b])
    kb.op("dve", "tensor_copy", out=G.lbt[:], in_=G.lbe[:, 0:8], R=[lb_b], W=[lb_b])
    for l in range(1, DEPTH):
        kb.op("dve", "tensor_tensor", out=G.lbt[:], in0=G.lbt[:], in1=G.lbe[:, l * 8:(l + 1) * 8], op=ALU.add, R=[lb_b], W=[lb_b])
    kb.op("dve", "reciprocal", out=G.lbt[:], in_=G.lbt[:], R=[lb_b], W=[lb_b])
    kb.op("dve", "memset", G.lb[:, 0:8], 0.0, W=[lb_b])
    for l in range(1, DEPTH):
        kb.op("dve", "tensor_tensor", out=G.lbe[:, l * 8:(l + 1) * 8], in0=G.lbe[:, l * 8:(l + 1) * 8], in1=G.lbt[:], op=ALU.mult, R=[lb_b], W=[lb_b])
        kb.op("dve", "tensor_tensor", out=G.lb[:, l * 8:(l + 1) * 8], in0=G.lb[:, (l - 1) * 8:l * 8], in1=G.lbe[:, l * 8:(l + 1) * 8], op=ALU.add, R=[lb_b], W=[lb_b])
    kb.op("dve", "tensor_scalar", out=G.oml[:], in0=G.lb[:], scalar1=-1.0, scalar2=1.0, op0=ALU.mult, op1=ALU.add, R=[lb_b], W=[G.const_b])
    kb.op("dve", "memset", G.rmask[:], 1.0, W=[G.const_b])
    kb.op("dve", "memset", G.rmask[:].rearrange("p (c j) -> p c j", j=64)[:, :, 0:1], 0.0, W=[G.const_b])
    for i in range(WTOTAL // CAST_CH):
        kb.dma("pool", G.wbf[i * CAST_CH:(i + 1) * CAST_CH].rearrange("(p n) -> p n", p=128),
               wsrc[i * CAST_CH:(i + 1) * CAST_CH].rearrange("(p n) -> p n", p=128))
    kb.barrier()
    for l in range(nlayers):
        for S in seqs:
            if "p1" in phases:
                phase_p1(kb, G, S, l)
            if "p2" in phases:
                phase_p2(kb, G, S)
            if "p3" in phases:
                phase_p3(kb, G, S, l)
            if "p4" in phases:
                phase_p4(kb, G, S, l)
            if "p5" in phases:
                phase_p5(kb, G, S, l)
    kb.barrier()
    kb.es.close()
    return kb, seqs


def host_fnet_consts(L):
    N1 = L // 128
    fb = np.zeros((128, 4 * N1 + 256), np.float32)
    s1 = np.arange(N1)
    a1 = 2 * np.pi * np.outer(s1, s1) / N1
    fb[:N1, 0:N1] = np.cos(a1)
    fb[:N1, N1:2 * N1] = np.sin(a1)
    fb[:N1, 2 * N1:3 * N1] = -np.sin(a1)
    fb[:N1, 3 * N1:4 * N1] = np.cos(a1)
    s0 = np.arange(128)
    a2 = 2 * np.pi * np.outer(s0, s0) / 128.0
    fb[:, 4 * N1:4 * N1 + 128] = np.cos(a2) / np.sqrt(L)
    fb[:, 4 * N1 + 128:4 * N1 + 256] = -np.sin(a2) / np.sqrt(L)
    ff = np.zeros((128, 2 * N1), np.float32)
    at = 2 * np.pi * np.outer(s0, s1) / L
    ff[:, 0:N1] = np.cos(at)
    ff[:, N1:2 * N1] = np.sin(at)
    return fb.astype(NPBF), ff


def phase_p2(kb, G, S):
    nc = kb.nc
    L = S.L
    N1 = L // 128
    NT = L // T
    CH = 64
    cpb1 = 512 // (2 * N1)
    cpb2 = min(512 // N1, CH)
    with contextlib.ExitStack() as es:
        kb.nuid += 1
        uid = kb.nuid

        def sb(name, shape, dt):
            return es.enter_context(nc.sbuf_tensor(f"{name}_u{uid}", list(shape), dt))

        fb = sb("fn_fb", [128, 4 * N1 + 256], BF16)
        ff = sb("fn_ff", [128, 2 * N1], F32)
        cst_b = Buf()
        kb.dma("sp", fb[:], S.fnb[:, :], W=[cst_b])
        kb.dma("sp", ff[:], S.fnf[:, :], W=[cst_b])
        F1a = fb[0:N1, 0:2 * N1]
        F1b = fb[0:N1, 2 * N1:4 * N1]
        Fc2 = fb[:, 4 * N1:4 * N1 + 128]
        nFs2 = fb[:, 4 * N1 + 128:4 * N1 + 256]
        Tc = ff[:, 0:N1]
        Ts = ff[:, N1:2 * N1]
        zt = Rot([(sb(f"fn_zt{i}", [N1, 128, 256], BF16), Buf()) for i in range(1)])
        gp = sb("fn_gp", [128, CH, 2, N1], BF16)
        gp_b = Buf()
        yt = Rot([(sb(f"fn_yt{i}", [128, CH, N1], BF16), Buf()) for i in range(2)])
        tmp = Rot([(sb(f"fn_tmp{i}", [128, 4, 256], F32), Buf()) for i in range(3)])
        all_ucs = [sbuf_of(S, "ucs", t) for t in range(NT)]
        all_fn = [sbuf_of(S, "fn", t) for t in range(NT)]
        for g in range(4):
            z, z_b = zt.next()
            src = S.ucs[:, g * 256:(g + 1) * 256].rearrange("(s1 s0) c -> s1 s0 c", s0=128)
            kb.dma("sp", z[:], src, R=all_ucs, W=[z_b])
            for half in range(128 // CH):
                for c0 in range(0, CH, cpb1):
                    bk, bk_b = kb.bank()
                    groups = []
                    for j in range(cpb1):
                        ch = half * CH + c0 + j
                        groups.append((bk[:, j * 2 * N1:(j + 1) * 2 * N1],
                                       [(z[:, :, ch], F1a), (z[:, :, 128 + ch], F1b)]))
                    kb.mm_multi(groups, R=[z_b, cst_b], W=[bk_b])
                    bv = bk[:, 0:cpb1 * 2 * N1].rearrange("p (c s n) -> p c s n", c=cpb1, s=2)
                    tcb = Tc.unsqueeze(1).broadcast_to([128, cpb1, N1])
                    tsb = Ts.unsqueeze(1).broadcast_to([128, cpb1, N1])
                    tm, tm_b = tmp.next()
                    tv = tm[:].rearrange("p a b -> p (a b)")[:, 0:4 * cpb1 * N1].rearrange("p (a c n) -> p a c n", a=4, c=cpb1)
                    kb.op("dve", "tensor_tensor", out=tv[:, 0], in0=bv[:, :, 0, :], in1=tcb, op=ALU.mult, R=[bk_b, cst_b], W=[tm_b])
                    kb.op("dve", "tensor_tensor", out=tv[:, 1], in0=bv[:, :, 1, :], in1=tsb, op=ALU.mult, R=[bk_b, cst_b], W=[tm_b])
                    kb.op("dve", "tensor_tensor", out=tv[:, 2], in0=bv[:, :, 1, :], in1=tcb, op=ALU.mult, R=[bk_b, cst_b], W=[tm_b])
                    kb.op("dve", "tensor_tensor", out=tv[:, 3], in0=bv[:, :, 0, :], in1=tsb, op=ALU.mult, R=[bk_b, cst_b], W=[tm_b])
                    kb.op("pool", "tensor_tensor", out=gp[:, c0:c0 + cpb1, 0, :], in0=tv[:, 0], in1=tv[:, 1], op=ALU.subtract,
                          R=[tm_b], W=[gp_b])
                    kb.op("pool", "tensor_tensor", out=gp[:, c0:c0 + cpb1, 1, :], in0=tv[:, 2], in1=tv[:, 3], op=ALU.add,
                          R=[tm_b], W=[gp_b])
                y, y_b = yt.next()
                for i, c0 in enumerate(range(0, CH, cpb2)):
                    bk, bk_b = kb.bank()
                    kb.mm(bk[:, 0:cpb2 * N1], [(Fc2, gp[:, c0:c0 + cpb2, 0, :]), (nFs2, gp[:, c0:c0 + cpb2, 1, :])],
                          R=[gp_b, cst_b], W=[bk_b])
                    src_v = bk[:, 0:cpb2 * N1].rearrange("p (c n) -> p c n", c=cpb2)
                    if i % 2 == 0:
                        kb.op("act", "activation", out=y[:, c0:c0 + cpb2, :], in_=src_v, func=AF.Copy, R=[bk_b], W=[y_b])
                    else:
                        kb.op("dve", "tensor_copy", out=y[:, c0:c0 + cpb2, :], in_=src_v, R=[bk_b], W=[y_b])
                r0 = g * 128 + half * CH
                dst = S.fn[r0:r0 + CH, :].rearrange("c (k0 k1) -> k0 c k1", k1=N1)
                kb.dma("pool", dst, y[:], R=[y_b], W=all_fn)
        kb.barrier()


def phase_p3(kb, G, S, l):
    nc = kb.nc
    L = S.L
    NT = L // T
    H4 = 4
    with contextlib.ExitStack() as es:
        kb.nuid += 1
        uid = kb.nuid

        def sb(name, shape, dt):
            return es.enter_context(nc.sbuf_tensor(f"{name}_u{uid}", list(shape), dt))

        def rot(name, shape, dt, n):
            return Rot([(sb(f"{name}{i}", shape, dt), Buf()) for i in range(n)])

        zt = rot("h_z", [128, H4, T], F32, 1)
        qt = rot("h_q", [128, H4, T], BF16, 1)
        it = rot("h_i", [128, 4, 512], BF16, 2)
        gt = rot("h_g", [128, H4, T], BF16, 1)
        obin = rot("h_obin", [128, H4, T], F32, 1)
        Bs = (sb("h_Bs", [128, H4, T], F32), Buf())
        Blf = (sb("h_Blf", [128, H4, T], F32), Buf())
        Bkf = (sb("h_Bkf", [128, H4, T], F32), Buf())
        Bb = (sb("h_Bb", [128, H4, T], F32), Buf())
        Bg = (sb("h_Bg", [128, H4, T], F32), Buf())
        Be = (sb("h_Be", [128, H4, T], F32), Buf())
        Bgm = (sb("h_Bgm", [128, H4, T], F32), Buf())
        BE2 = (sb("h_BE2", [128, H4, T], F32), Buf())
        Bs_h, Blf_h, Bkf_h, Bb_h, Bg_h, Be_h, Bgm_h, BE2_h = [[Buf() for _ in range(H4)] for _ in range(8)]
        dec = rot("h_dec", [128, H4, 8], F32, 2)
        qd = rot("h_qd", [128, H4, T], BF16, 2)
        kd = rot("h_kd", [128, H4, T], BF16, 2)
        qe = rot("h_qe", [128, H4, T], BF16, 2)
        keT = rot("h_keT", [128, H4, T], BF16, 1)
        ketm = rot("h_ketm", [128, 4, 512], BF16, 2)
        atm = rot("h_atm", [128, 128], BF16, 6)
        ost = rot("h_ost", [128, H4, T], F32, 1)
        sq = (sb("h_sq", [128, H4, T], BF16), Buf())
        rs = (sb("h_rs", [128, H4, T], F32), Buf())
        sgt = (sb("h_sgt", [128, H4, T], F32), Buf())
        hout = rot("h_out", [128, H4, T], BF16, 1)
        St = [[sb(f"h_S{h}_{i}", [128, 128], F32) for i in range(2)] for h in range(H4)]
        Sb_ = [[sb(f"h_Sb{h}_{i}", [128, 128], BF16) for i in range(2)] for h in range(H4)]
        St_b = [[Buf(), Buf()] for h in range(H4)]
        Sb_b = [[Buf(), Buf()] for h in range(H4)]
        gain = G.vecs[:, l * NV + 26:l * NV + 27]

        def v4(t):
            return t[:].rearrange("p h (c j) -> p h c j", j=64)

        def hv(t, hh):
            return t[:, hh * 2:hh * 2 + 2, :]

        def pre_ops(d, t):
            t0 = t * T
            res = {}
            ops = []

            def o_load():
                z, z_b = zt.next()
                kb.dma("sp", z[:], fm_view(S.z, d * 512, H4, t0, T), R=[sbuf_of(S, "z", t)], W=[z_b])
                q, q_b = qt.next()
                kb.dma("sp", q[:], fm_view(S.qh, 0, H4, t0, T), R=[sbuf_of(S, "qh", t)], W=[q_b])
                iv, iv_b = it.next()
                kb.dma("sp", iv[:], S.ih[t0:t0 + T, :].rearrange("(b p) c -> p b c", p=128), R=[sbuf_of(S, "ih", t)], W=[iv_b])
                res.update(z=z, z_b=z_b, q=q, q_b=q_b, iv=iv, iv_b=iv_b)
            ops.append(o_load)

            def H(t_, h):
                return t_[:, h, :]

            def H4v(t_, h):
                return t_[:, h, :].rearrange("p (c j) -> p c j", j=64)

            def mk(stage):
                for h in range(H4):
                    ops.append(lambda h=h: stage(h))

            def s1(h):
                kb.op("act", "activation", out=H(Bs[0], h), in_=H(res["z"], h), func=AF.Sigmoid, R=[res["z_b"]], W=[Bs_h[h]])
                col = l * 8 + d * 4 + h
                kb.op("dve", "tensor_scalar", out=H(Bs[0], h), in0=H(Bs[0], h), scalar1=G.oml[:, col:col + 1],
                      scalar2=G.lb[:, col:col + 1], op0=ALU.mult, op1=ALU.add, R=[G.const_b], W=[Bs_h[h]])
            mk(s1)

            def s2(h):
                kb.op("act", "activation", out=H(Blf[0], h), in_=H(Bs[0], h), func=AF.Ln, R=[Bs_h[h]], W=[Blf_h[h]])
                e = "pool" if h % 2 else "dve"
                kb.op(e, "tensor_scalar", out=H(Bkf[0], h), in0=H(Bs[0], h), scalar1=-1.0, scalar2=1.0, op0=ALU.mult, op1=ALU.add,
                      R=[Bs_h[h]], W=[Bkf_h[h]])
            mk(s2)

            def s3(h):
                kb.op("dve", "tensor_tensor_scan", out=H(Bb[0], h), data0=G.rmask[:], data1=H(Blf[0], h), initial=0.0,
                      op0=ALU.mult, op1=ALU.add, R=[Blf_h[h], G.const_b], W=[Bb_h[h]])
            mk(s3)

            def s4a():
                dc, dc_b = dec.next()
                res.update(dc=dc, dc_b=dc_b)
                res["gbuf"], res["gbuf_h"] = (Bb, Bb_h) if d == 0 else (Bg, Bg_h)
            ops.append(s4a)

            def s4(h):
                b4 = H4v(Bb[0], h)
                bend = b4[:, :, 63:64]
                bend_bc = bend.broadcast_to([128, 8, 64])
                dc, dc_b = res["dc"], res["dc_b"]
                kb.op("act", "activation", out=dc[:, h, :].rearrange("p (c o) -> p c o", o=1), in_=bend, func=AF.Exp, R=[Bb_h[h]], W=[dc_b])
                if d == 0:
                    kb.op("pool", "tensor_tensor", out=H4v(Be[0], h), in0=bend_bc, in1=b4, op=ALU.subtract, R=[Bb_h[h]], W=[Be_h[h]])
                else:
                    kb.op("dve", "tensor_tensor", out=H4v(Bg[0], h), in0=bend_bc, in1=b4, op=ALU.subtract, R=[Bb_h[h]], W=[Bg_h[h]])
                    kb.op("dve", "tensor_tensor", out=H(Bg[0], h), in0=H(Bg[0], h), in1=H(Blf[0], h), op=ALU.add, R=[Blf_h[h]], W=[Bg_h[h]])
                    kb.op("pool", "tensor_tensor", out=H(Be[0], h), in0=H(Bb[0], h), in1=H(Blf[0], h), op=ALU.subtract,
                          R=[Bb_h[h], Blf_h[h]], W=[Be_h[h]])
            mk(s4)

            def s5(h):
                gbuf, gbuf_h = res["gbuf"], res["gbuf_h"]
                g4 = H4v(gbuf[0], h)
                gmid_bc = g4[:, :, 32:33].broadcast_to([128, 8, 64])
                kb.op("dve", "tensor_tensor", out=H4v(Bgm[0], h), in0=g4, in1=gmid_bc, op=ALU.subtract, R=[gbuf_h[h]], W=[Bgm_h[h]])
                kb.op("act", "activation", out=H(BE2[0], h), in_=H(Bgm[0], h), func=AF.Exp, scale=-1.0, R=[Bgm_h[h]], W=[BE2_h[h]])
                kb.op("act", "activation", out=H(Bgm[0], h), in_=H(Bgm[0], h), func=AF.Exp, R=[Bgm_h[h]], W=[Bgm_h[h]])
            mk(s5)

            def s6a():
                qd_t, qd_b = qd.next()
                kd_t, kd_b = kd.next()
                qe_t, qe_b = qe.next()
                ke_t, ke_b = keT.next()
                res.update(qd_t=qd_t, qd_b=qd_b, kd_t=kd_t, kd_b=kd_b, qe_t=qe_t, qe_b=qe_b, ke_t=ke_t, ke_b=ke_b)
            ops.append(s6a)

            def s6(h):
                q, q_b = res["q"], res["q_b"]
                kb.op("dve", "tensor_tensor", out=H(res["qd_t"], h), in0=H(q, h), in1=H(Bgm[0], h), op=ALU.mult, R=[q_b, Bgm_h[h]], W=[res["qd_b"]])
                e = "pool" if h >= 2 else "dve"
                kb.op(e, "tensor_tensor", out=H(res["kd_t"], h), in0=H(Bkf[0], h), in1=H(BE2[0], h), op=ALU.mult, R=[Bkf_h[h], BE2_h[h]], W=[res["kd_b"]])
            mk(s6)

            def s7(h):
                gbuf, gbuf_h = res["gbuf"], res["gbuf_h"]
                kb.op("act", "activation", out=H(BE2[0], h), in_=H(gbuf[0], h), func=AF.Exp, R=[gbuf_h[h]], W=[BE2_h[h]])
                kb.op("act", "activation", out=H(Be[0], h), in_=H(Be[0], h), func=AF.Exp, R=[Be_h[h]], W=[Be_h[h]])
            mk(s7)

            def s8(h):
                q, q_b = res["q"], res["q_b"]
                kb.op("dve", "tensor_tensor", out=H(res["qe_t"], h), in0=H(q, h), in1=H(BE2[0], h), op=ALU.mult, R=[q_b, BE2_h[h]], W=[res["qe_b"]])
                e = "pool" if h >= 2 else "dve"
                kb.op(e, "tensor_tensor", out=H(res["ke_t"], h), in0=H(Bkf[0], h), in1=H(Be[0], h), op=ALU.mult, R=[Bkf_h[h], Be_h[h]], W=[res["ke_b"]])
            mk(s8)

            def o10():
                km, km_b = ketm.next()
                ke_t, ke_b = res["ke_t"], res["ke_b"]
                for tb in range(4):
                    bk, bk_b = kb.bank()
                    kb.mm_multi([(bk[:, h * 128:(h + 1) * 128], [(ke_t[:, h, tb * 128:(tb + 1) * 128], G.ident)]) for h in range(H4)],
                                R=[ke_b, G.const_b], W=[bk_b])
                    if tb % 2 == 0:
                        kb.op("act", "activation", out=km[:, tb, :], in_=bk[:, 0:512], func=AF.Copy, R=[bk_b], W=[km_b])
                    else:
                        kb.op("dve", "tensor_copy", out=km[:, tb, :], in_=bk[:, 0:512], R=[bk_b], W=[km_b])
                res.update(km=km, km_b=km_b)
            ops.append(o10)
            return ops, res

        def block_ops(d, t, res, cur):
            t0 = t * T
            ops = []
            st = {}

            def b_init():
                o, o_b = ost.next()
                st.update(o=o, o_b=o_b)
                if d == 0:
                    gg, gg_b = gt.next()
                    kb.dma("sp", gg[:], fm_view(S.gh, 0, H4, t0, T), R=[sbuf_of(S, "gh", t)], W=[gg_b])
                    obi, obi_b = obin.next()
                    kb.dma("sp", obi[:], fm_view(S.ob, 0, H4, t0, T), R=[sbuf_of(S, "ob", t)], W=[obi_b])
                    res.update(gg=gg, gg_b=gg_b, obi=obi, obi_b=obi_b)
            ops.append(b_init)
            blocks = [0, 1, 2, 3] if d == 0 else [3, 2, 1, 0]
            corder = (0, 1) if d == 0 else (1, 0)
            for tb in blocks:
                bst = {}

                def blkA(tb=tb, bst=bst):
                    tsl = slice(tb * 128, (tb + 1) * 128)
                    ats = []
                    for h in range(H4):
                        bkA, bkA_b = kb.bank()
                        kb.mm(bkA[:, 0:128], [(res["kd_t"][:, h, tsl], res["qd_t"][:, h, tsl])], R=[res["kd_b"], res["qd_b"]], W=[bkA_b])
                        a, a_b = atm.next()
                        kb.op("dve", "tensor_tensor", out=a[:], in0=bkA[:, 0:128], in1=G.hmask[d], op=ALU.mult,
                              R=[bkA_b, G.const_b], W=[a_b])
                        ats.append((a, a_b))
                    bst["ats"] = ats
                ops.append(blkA)
                for step, ci in enumerate(corder):
                    def blkS(tb=tb, bst=bst, ci=ci):
                        iv, iv_b = res["iv"], res["iv_b"]
                        km, km_b = res["km"], res["km_b"]
                        qe_t, qe_b = res["qe_t"], res["qe_b"]
                        dc, dc_b = res["dc"], res["dc_b"]
                        csl = slice(ci * 64, (ci + 1) * 64)
                        for h in range(H4):
                            a, a_b = bst["ats"][h]
                            bkO, bkO_b = kb.psum_banks[h]
                            c_ = cur[h]
                            kb.mm(bkO[:, ci * 64:(ci + 1) * 64],
                                  [(iv[:, tb, h * 128:(h + 1) * 128], a[:, csl]),
                                   (Sb_[h][c_][:], qe_t[:, h, tb * 128 + ci * 64:tb * 128 + (ci + 1) * 64])],
                                  R=[iv_b, a_b, Sb_b[h][c_], qe_b], W=[bkO_b])
                            bkS, bkS_b = kb.bank()
                            kb.mm(bkS[:, 0:128], [(km[csl, tb, h * 128:(h + 1) * 128], iv[csl, tb, h * 128:(h + 1) * 128])],
                                  R=[km_b, iv_b], W=[bkS_b])
                            n_ = 1 - c_
                            kb.op("dve", "scalar_tensor_tensor", out=St[h][n_][:], in0=St[h][c_][:],
                                  scalar=dc[:, h, tb * 2 + ci:tb * 2 + ci + 1], in1=bkS[:, 0:128], op0=ALU.mult, op1=ALU.add,
                                  R=[St_b[h][c_], dc_b, bkS_b], W=[St_b[h][n_]])
                            kb.op("act", "activation", out=Sb_[h][n_][:], in_=St[h][n_][:], func=AF.Copy, R=[St_b[h][n_]], W=[Sb_b[h][n_]])
                            cur[h] = n_
                    ops.append(blkS)

                def blkE(tb=tb):
                    o, o_b = st["o"], st["o_b"]
                    tsl = slice(tb * 128, (tb + 1) * 128)
                    for h in range(H4):
                        bkO, bkO_b = kb.psum_banks[h]
                        if d == 1:
                            kb.op("act", "activation", out=o[:, h, tsl], in_=bkO[:, 0:128], func=AF.Copy, R=[bkO_b], W=[o_b])
                        else:
                            obi, obi_b = res["obi"], res["obi_b"]
                            kb.op("dve", "tensor_tensor", out=o[:, h, tsl], in0=bkO[:, 0:128], in1=obi[:, h, tsl], op=ALU.add,
                                  R=[bkO_b, obi_b], W=[o_b])
                ops.append(blkE)

            def fin():
                o, o_b = st["o"], st["o_b"]
                if d == 1:
                    kb.dma("pool", fm_view(S.ob, 0, H4, t0, T), o[:], R=[o_b], W=[sbuf_of(S, "ob", t)])
                else:
                    gg, gg_b = res["gg"], res["gg_b"]
                    kb.op("pool", "tensor_tensor", out=sq[0][:], in0=o[:], in1=o[:], op=ALU.mult, R=[o_b], W=[sq[1]])
                    for h in range(H4):
                        bk, bk_b = kb.bank()
                        kb.mm(bk[:, 0:T], [(G.ones, sq[0][:, h, :])], R=[sq[1], G.const_b], W=[bk_b])
                        kb.op("act", "activation", out=rs[0][:, h, :], in_=bk[:, 0:T], func=AF.Ln, bias=G.epsD[:], scale=1.0 / 128,
                              R=[bk_b, G.const_b], W=[rs[1]])
                    kb.op("act", "activation", out=rs[0][:], in_=rs[0][:], func=AF.Exp, scale=-0.5, R=[rs[1]], W=[rs[1]])
                    kb.op("act", "activation", out=sgt[0][:], in_=gg[:], func=AF.Silu, R=[gg_b], W=[sgt[1]])
                    kb.op("pool", "tensor_tensor", out=rs[0][:], in0=rs[0][:], in1=o[:], op=ALU.mult, R=[o_b], W=[rs[1]])
                    ho, ho_b = hout.next()
                    kb.op("dve", "scalar_tensor_tensor", out=ho[:], in0=rs[0][:], scalar=gain, in1=sgt[0][:], op0=ALU.mult, op1=ALU.mult,
                          R=[rs[1], sgt[1], G.const_b], W=[ho_b])
                    kb.dma("pool", fm_view(S.hg, 0, H4, t0, T), ho[:], R=[ho_b], W=[sbuf_of(S, "hg", t)])
            ops.append(fin)
            return ops

        kb.bank_range = (4, 8)
        for d in (1, 0):
            cur = [0] * H4
            for h in range(H4):
                kb.op("dve", "memset", St[h][0][:], 0.0, W=[St_b[h][0]])
                kb.op("pool", "memset", Sb_[h][0][:], 0.0, W=[Sb_b[h][0]])
            tiles = list(range(NT))
            if d == 1:
                tiles = tiles[::-1]
            pops, pres = pre_ops(d, tiles[0])
            for f in pops:
                f()
            for i, t in enumerate(tiles):
                bops = block_ops(d, t, pres, cur)
                if i + 1 < len(tiles):
                    nops, nres = pre_ops(d, tiles[i + 1])
                else:
                    nops, nres = [], None
                nb = len(bops)
                per = (len(nops) + nb - 2) // max(1, nb - 1) if nops else 0
                import os
                if os.environ.get('P3_NOIL'):
                    per = 0
                pi = 0
                for bi, f in enumerate(bops):
                    f()
                    if bi < nb - 1:
                        for _ in range(per):
                            if pi < len(nops):
                                nops[pi]()
                                pi += 1
                while pi < len(nops):
                    nops[pi]()
                    pi += 1
                pres = nres
            kb.barrier()
        kb.bank_range = (0, 8)


AW = 2048
AGROUPS = ((1, 0), (4, 1), (16, 2))


def host_rope(L):
    half = HD // 2
    inv = (10000.0 ** (-np.arange(half, dtype=np.float32) / half)).astype(np.float32)
    pos = np.arange(L, dtype=np.float32)
    ang = (pos[None, :] * inv[:, None]).astype(np.float32)
    fi = (np.arange(128) % 64) % 32
    return np.stack([np.cos(ang)[fi], np.sin(ang)[fi]], axis=1).astype(NPBF)


def pipeline(stages_list, look=1):
    n = len(stages_list)
    for i in range(min(look, n)):
        stages_list[i][0]()
    for i in range(n):
        if i + look < n:
            stages_list[i + look][0]()
        stages_list[i][1]()


def pipeline_n(items, skew=1):
    n = len(items)
    if n == 0:
        return
    k = len(items[0])
    for step in range(n + (k - 1) * skew):
        for s_ in range(k):
            i = step - s_ * skew
            if 0 <= i < n:
                items[i][s_]()


def phase_p4(kb, G, S, l):
    nc = kb.nc
    L = S.L
    W = AW
    NW = L // W
    with contextlib.ExitStack() as es:
        kb.nuid += 1
        uid = kb.nuid

        def sb(name, shape, dt):
            return es.enter_context(nc.sbuf_tensor(f"{name}_u{uid}", list(shape), dt))

        NQC = W // 512
        KWID = [W + 128 * r for r, g in AGROUPS]
        KOFF = [0, KWID[0], KWID[0] + KWID[1]]
        qz = sb("a_qz", [128, 3, 2, W], BF16)
        qz_b = [[Buf() for c in range(NQC)] for g in range(3)]
        qraw = Rot([(sb(f"a_qraw{i}", [128, 512], BF16), Buf()) for i in range(3)])
        kn = sb("a_kn", [128, sum(KWID)], BF16)
        kn_b = [[Buf() for c in range((KWID[g] + 511) // 512)] for g in range(3)]
        nblk = [r * (W // r // 128 + 1) for r, g in AGROUPS]
        goff = [0, nblk[0], nblk[0] + nblk[1]]
        vt = (sb("a_vt", [128, sum(nblk), 512], BF16), Buf())
        vt_gc = [[Buf() for c in range(r)] for r, g in AGROUPS]
        cs = (sb("a_cs", [128, 2, W + 2048], BF16), Buf())
        acc = (sb("a_acc", [128, 2, W], F32), Buf())
        rec = Rot([(sb(f"a_rec{i}", [128, 512], F32), Buf()) for i in range(2)])
        aout = Rot([(sb(f"a_out{i}", [128, W], BF16), Buf()) for i in range(1)])
        ND_ = 5
        sqt = Rot([(sb(f"a_sq{i}", [128, 512], BF16), Buf()) for i in range(ND_)])
        y1t = Rot([(sb(f"a_y1{i}", [128, 512], BF16), Buf()) for i in range(ND_)])
        y2t = Rot([(sb(f"a_y2{i}", [128, 512], BF16), Buf()) for i in range(ND_)])
        rst = Rot([(sb(f"a_rs{i}", [128, 512], F32), Buf()) for i in range(ND_)])
        ptt = Rot([(sb(f"a_pt{i}", [128, 512], BF16), Buf()) for i in range(4)])
        qgain = G.vecs[:, l * NV + 24:l * NV + 25]
        kgain = G.vecs[:, l * NV + 25:l * NV + 26]
        kb.op("pool", "memset", qz[:], 0.0, W=[b for bb in qz_b for b in bb])

        blkg = sb("a_blkg", [128, 2, 128], BF16)
        ginv = sb("a_ginv", [128, 2], F32)
        blkg_b = Buf()
        kb.op("dve", "tensor_tensor", out=ginv[:], in0=G.vecs[:, l * NV + 24:l * NV + 26], in1=G.vecs[:, l * NV + 24:l * NV + 26],
              op=ALU.mult, R=[G.const_b], W=[blkg_b])
        kb.op("dve", "reciprocal", out=ginv[:], in_=ginv[:], W=[blkg_b])
        for i in range(2):
            kb.op("dve", "tensor_scalar", out=blkg[:, i, :], in0=G.blk, scalar1=ginv[:, i:i + 1], scalar2=None, op0=ALU.mult,
                  R=[G.const_b], W=[blkg_b])

        def nrope_stages(x, xb, c0, n, which, outs=None, outb=None, pre=None):
            st = {}

            def S1():
                if pre is not None:
                    pre()
                s_, s_b = sqt.next()
                kb.op("act", "activation", out=s_[:, 0:n], in_=x, func=AF.Square, R=[xb], W=[s_b])
                y1, y1_b = y1t.next()
                y2, y2_b = y2t.next()
                kb.op("dve", "tensor_tensor", out=y1[:, 0:n], in0=x, in1=cs[0][:, 0, c0:c0 + n], op=ALU.mult, R=[xb, cs[1]], W=[y1_b])
                kb.op("pool", "tensor_tensor", out=y2[:, 0:n], in0=x, in1=cs[0][:, 1, c0:c0 + n], op=ALU.mult, R=[xb, cs[1]], W=[y2_b])
                st.update(s_=s_, s_b=s_b, y1=y1, y1_b=y1_b, y2=y2, y2_b=y2_b)

            def S2():
                b1, b1_b = kb.bank()
                kb.mm(b1[:, 0:n], [(blkg[:, which, :], st["s_"][:, 0:n])], R=[st["s_b"], blkg_b], W=[b1_b])
                b2, b2_b = kb.bank()
                kb.mm(b2[:, 0:n], [(G.ident, st["y1"][:, 0:n]), (G.rot, st["y2"][:, 0:n])], R=[st["y1_b"], st["y2_b"], G.const_b], W=[b2_b])
                st.update(b1=b1, b1_b=b1_b, b2=b2, b2_b=b2_b)

            def S3():
                r_, r_b = rst.next()
                kb.op("act", "activation", out=r_[:, 0:n], in_=st["b1"][:, 0:n], func=AF.Ln, bias=G.epsD[:], scale=1.0 / HD,
                      R=[st["b1_b"], G.const_b], W=[r_b])
                kb.op("act", "activation", out=r_[:, 0:n], in_=r_[:, 0:n], func=AF.Exp, scale=-0.5, R=[r_b], W=[r_b])
                st.update(r_=r_, r_b=r_b)

            def S4():
                b2, b2_b, r_, r_b = st["b2"], st["b2_b"], st["r_"], st["r_b"]
                if outs is None:
                    kb.op("dve", "tensor_tensor", out=x, in0=b2[:, 0:n], in1=r_[:, 0:n], op=ALU.mult, R=[b2_b, r_b], W=[xb])
                else:
                    for i, (dst, ps) in enumerate(outs):
                        kb.op("dve", "tensor_tensor", out=dst, in0=b2[ps, 0:n], in1=r_[ps, 0:n], op=ALU.mult,
                              R=[b2_b, r_b], W=[outb])
            return (S1, S2, S3, S4)

        for w in range(NW):
            w0 = w * W
            first_w = (w == 0)
            last_w = (w == NW - 1)
            lo = max(0, w0 - 1024)
            hi = min(L, w0 + W + 1024)
            kb.dma("sp", cs[0][:, :, lo - (w0 - 1024):hi - (w0 - 1024)], S.rope[:, :, lo:hi], W=[cs[1]])
            qk_t = [sbuf_of(S, "qk", t) for t in range(lo // T, hi // T)]
            v_t = [sbuf_of(S, "v", t) for t in range(lo // T, hi // T)]
            for gi, (r, g) in enumerate(AGROUPS):
                nbw = W // r // 128
                N = 128 * (nbw + 1)
                n_lo = 64 if first_w else 0
                n_hi = N - 64 if last_w else N
                rows = S.v[w0 - 64 * r + r * n_lo:w0 - 64 * r + r * n_hi, g * 512:(g + 1) * 512]
                rv = rows.rearrange("(n r) c -> r n c", r=r)
                for c in range(r):
                    b0 = goff[gi] + c * (nbw + 1)
                    n = n_lo
                    if n_lo > 0:
                        kb.op("pool", "memset", vt[0][0:64, b0, :], 0.0, W=[vt_gc[gi][c]])
                        kb.dma("sp", vt[0][64:128, b0, :], rv[c, 0:64, :], R=v_t, W=[vt_gc[gi][c]])
                        n = 128
                    m_lo = n // 128
                    m_hi = n_hi // 128
                    if m_hi > m_lo:
                        src = rv[c, m_lo * 128 - n_lo:m_hi * 128 - n_lo, :].rearrange("(m p) c -> p m c", p=128)
                        kb.dma("sp", vt[0][:, b0 + m_lo:b0 + m_hi, :], src, R=v_t, W=[vt_gc[gi][c]])
                    if n_hi % 128:
                        kb.op("pool", "memset", vt[0][64:128, b0 + nbw, :], 0.0, W=[vt_gc[gi][c]])
                        kb.dma("sp", vt[0][0:64, b0 + nbw, :], rv[c, m_hi * 128 - n_lo:m_hi * 128 - n_lo + 64, :], R=v_t, W=[vt_gc[gi][c]])
            for hp in range(4):
                kval = []
                for r, g in AGROUPS:
                    j_lo = 64 * r if first_w else 0
                    j_hi = W + 64 * r if last_w else W + 128 * r
                    kval.append((j_lo, j_hi))
                    row0 = 1536 + g * 512 + hp * 128
                    ko = KOFF[g]
                    if j_lo > 0:
                        kb.op("pool", "memset", kn[:, ko:ko + j_lo], 0.0, W=kn_b[g])
                    if j_hi < KWID[g]:
                        kb.op("pool", "memset", kn[:, ko + j_hi:ko + KWID[g]], 0.0, W=kn_b[g])
                    kb.dma("sp", kn[:, ko + j_lo:ko + j_hi], S.qk[row0:row0 + 128, w0 - 64 * r + j_lo:w0 - 64 * r + j_hi],
                           R=qk_t, W=kn_b[g])
                stages = []
                for r, g in AGROUPS:
                    for ci, c0 in enumerate(range(0, W, 512)):
                        qx, qx_b = qraw.next()

                        def pre(qx=qx, qx_b=qx_b, g=g, c0=c0):
                            kb.dma("sp", qx[:], S.qk[g * 512 + hp * 128:g * 512 + hp * 128 + 128, w0 + c0:w0 + c0 + 512], R=qk_t, W=[qx_b])
                        stages.append(nrope_stages(qx[:], qx_b, 1024 + c0, 512, 0,
                                                   outs=[(qz[0:64, g, 0, c0:c0 + 512], slice(0, 64)), (qz[64:128, g, 1, c0:c0 + 512], slice(64, 128))],
                                                   outb=qz_b[g][ci], pre=pre))
                    j_lo, j_hi = kval[g]
                    ko = KOFF[g]
                    for ci in range(len(kn_b[g])):
                        c0 = max(j_lo, ci * 512)
                        c1 = min(j_hi, (ci + 1) * 512)
                        if c1 > c0:
                            stages.append(nrope_stages(kn[:, ko + c0:ko + c1], kn_b[g][ci], 1024 - 64 * r + c0, c1 - c0, 1))
                pipeline_n(stages, skew=1)
                units = []
                for gi, (r, g) in enumerate(AGROUPS):
                    nbw = W // r // 128
                    nb_total = L // r // 128
                    qv = qz[:, g, :, :].rearrange("p s (n r) -> p s r n", r=r)
                    kv = kn[:, KOFF[g]:KOFF[g] + KWID[g]].rearrange("p (n r) -> p r n", r=r)
                    accv = acc[0][:].rearrange("p s (n r) -> p s r n", r=r)
                    for c in range(r):
                        b0 = goff[gi] + c * (nbw + 1)
                        for bq in range(nbw):
                            mA = (w0 // r) // 128 + bq
                            variant = (1 if mA == 0 else 0) + (2 if mA + 1 == nb_total else 0)
                            st = {}
                            is_first = (c == 0 and bq == 0)
                            is_last = (c == r - 1 and bq == nbw - 1)

                            def A(st=st, g=g, qv=qv, kv=kv, c=c, bq=bq, variant=variant):
                                bk, bk_b = kb.bank()
                                qs = qv[:, :, c, 128 * bq:128 * bq + 128]
                                kb.mm_raw([(bk[:, 0:256], kv[:, c, 128 * bq:128 * bq + 128], qs, True, False),
                                           (bk[:, 256:512], kv[:, c, 128 * (bq + 1):128 * (bq + 1) + 128], qs, False, False),
                                           (bk[:, 0:512], G.ident, G.amask[variant], False, True)],
                                          R=qz_b[g] + kn_b[g] + [G.const_b], W=[bk_b])
                                pt, pt_b = ptt.next()
                                kb.op("act", "activation", out=pt[:], in_=bk[:, 0:512], func=AF.Exp, scale=float(HD) ** -0.5, R=[bk_b], W=[pt_b])
                                st.update(pt=pt, pt_b=pt_b)

                            def B(st=st, gi=gi, accv=accv, c=c, bq=bq, b0=b0, is_first=is_first, is_last=is_last, hp=hp):
                                pt, pt_b = st["pt"], st["pt_b"]
                                b2, b2_b = kb.bank()
                                vc = slice(hp * 128, (hp + 1) * 128)
                                kb.mm_multi([(b2[:, 0:256], [(vt[0][:, b0 + bq, vc], pt[:, 0:256]), (vt[0][:, b0 + bq + 1, vc], pt[:, 256:512])]),
                                             (b2[:, 256:512], [(G.ones, pt[:, 0:256]), (G.ones, pt[:, 256:512])])],
                                            R=[pt_b, vt_gc[gi][c], G.const_b], W=[b2_b])
                                tok = None
                                for h in range(2):
                                    hs = slice(h * 64, (h + 1) * 64)
                                    av = accv[hs, :, c, 128 * bq:128 * bq + 128]
                                    b2v = b2[hs, 0:512].rearrange("p (s h n) -> p s h n", s=2, h=2)[:, :, h, :]
                                    Wl = [acc[1]] if (is_first and h == 0) else []
                                    if gi == 0:
                                        tok = kb.op("dve", "tensor_copy", out=av, in_=b2v, R=[b2_b], W=Wl)
                                    else:
                                        tok = kb.op("dve", "tensor_tensor", out=av, in0=av, in1=b2v, op=ALU.add, R=[b2_b], W=Wl)
                                if is_last:
                                    acc[1].w = tok
                                    acc[1].r = {}
                            units.append((A, B))
                pipeline(units, look=2)
                ao, ao_b = aout.next()
                for c0 in range(0, W, 512):
                    rc, rc_b = rec.next()
                    kb.op("act", "activation", out=rc[:], in_=acc[0][:, 1, c0:c0 + 512], func=AF.Ln, R=[acc[1]], W=[rc_b])
                    kb.op("act", "activation", out=rc[:], in_=rc[:], func=AF.Exp, scale=-1.0, R=[rc_b], W=[rc_b])
                    kb.op("dve", "tensor_tensor", out=ao[:, c0:c0 + 512], in0=acc[0][:, 0, c0:c0 + 512], in1=rc[:], op=ALU.mult,
                          R=[acc[1], rc_b], W=[ao_b])
                kb.dma("pool", S.att[hp * 128:(hp + 1) * 128, w0:w0 + W], ao[:], R=[ao_b],
                       W=[sbuf_of(S, "att", t) for t in range(w0 // T, (w0 + W) // T)])
        kb.barrier()


LP = 16384
LS = 2048
NCORES = 8
_CACHE = {}


def kernel(**inputs):
    inputs = {k: np.asarray(v) for k, v in inputs.items()}
    xp = inputs["x_prompt"]
    xs = inputs["x_sample"]
    if "prog" not in _CACHE:
        _CACHE["prog"] = build(LP, LS)
    kb, seqs = _CACHE["prog"]
    wsrc = host_pack_weights(inputs)
    vecs = host_vecs(inputs)
    cbf = host_consts()
    lbl = host_lbl(inputs)
    pfb, pff = host_fnet_consts(LP)
    sfb, sff = host_fnet_consts(LS)
    prope = host_rope(LP)
    srope = host_rope(LS)
    zeros_p = np.zeros((D, LP), np.float32)
    in_maps = []
    for c in range(NCORES):
        px = np.ascontiguousarray(xp[c].T) if c < xp.shape[0] else zeros_p
        in_maps.append({
            "wsrc": wsrc, "vecs": vecs, "cbf": cbf, "lbl": lbl,
            "p_x": px, "s_x": np.ascontiguousarray(xs[c].T),
            "p_fnb": pfb, "p_fnf": pff, "s_fnb": sfb, "s_fnf": sff,
            "p_rope": prope, "s_rope": srope,
        })
    res = run_bass_kernel_spmd(kb.nc, in_maps, core_ids=list(range(NCORES)))
    yp = np.stack([np.asarray(res.results[c]["p_y"]).T for c in range(xp.shape[0])], axis=0).astype(np.float32)
    ys = np.stack([np.asarray(res.results[c]["s_y"]).T for c in range(NCORES)], axis=0).astype(np.float32)
    return (np.ascontiguousarray(yp), np.ascontiguousarray(ys))
```

```python
import contextlib
import numpy as np
import ml_dtypes
import concourse.bass as bass
import concourse.mybir as mybir
from concourse.bass_utils import run_bass_kernel_spmd

F32 = mybir.dt.float32
BF16 = mybir.dt.bfloat16
AF = mybir.ActivationFunctionType
ALU = mybir.AluOpType
NPBF = ml_dtypes.bfloat16

D = 1024
DFF = 2816
DEPTH = 2
KC = D // 128
FC = DFF // 128
EPS = 1e-6
T = 512
HD = 64
NQK = 3072
O_Q, O_K, O_V, O_ZF, O_ZB, O_QH, O_IH, O_GH, O_UF, O_GL = 0, 1536, 3072, 4608, 5120, 5632, 6144, 6656, 7168, 7680
IN_COLS = 10752
GW = 256
GWT = 512


class Buf:
    __slots__ = ("name", "w", "r")

    def __init__(self, name=""):
        self.name = name
        self.w = None
        self.r = {}


class Eng:
    def __init__(self, name, obj, sem):
        self.name, self.obj, self.sem = name, obj, sem
        self.cnt = 0
        self.waited = {}


class KB:
    def __init__(self):
        self.nc = bass.Bass("TRN2", target_bir_lowering=False)
        self.es = contextlib.ExitStack()
        nc = self.nc
        self.eng = {}
        for nm, obj in (("pe", nc.tensor), ("act", nc.scalar), ("dve", nc.vector),
                        ("pool", nc.gpsimd), ("sp", nc.sync)):
            sem = self.es.enter_context(nc.semaphore("s_" + nm))
            self.eng[nm] = Eng(nm, obj, sem)
        self.ND = 12
        self.dslots = {}
        self.dcnt = {}
        for q in ("sp", "pool", "act"):
            self.dslots[q] = [[self.es.enter_context(nc.semaphore(f"d_{q}{i}")), 0] for i in range(self.ND)]
            self.dcnt[q] = 0
        self.nuid = 0
        self.psum_banks = []
        self.psum_i = 0
        self.bank_range = (0, 8)

    def sbuf(self, name, shape, dtype):
        return self.es.enter_context(self.nc.sbuf_tensor(name, list(shape), dtype))

    def psum(self, name, shape, dtype):
        return self.es.enter_context(self.nc.psum_tensor(name, list(shape), dtype))

    def dram(self, name, shape, dtype, kind="Internal"):
        return self.nc.dram_tensor(name, list(shape), dtype, kind=kind).ap()

    def _wait(self, eng, toks):
        best = {}
        for sem, val in toks:
            k = id(sem)
            if k not in best or best[k][1] < val:
                best[k] = (sem, val)
        for k, (sem, val) in best.items():
            if eng.waited.get(k, 0) >= val:
                continue
            if sem is eng.sem and eng.name == "pe":
                continue
            eng.obj.wait_ge(sem, val)
            eng.waited[k] = val

    @staticmethod
    def _deps(R, W):
        toks = []
        for b in R:
            if b.w is not None:
                toks.append(b.w)
        for b in W:
            if b.w is not None:
                toks.append(b.w)
            toks.extend(b.r.values())
        return toks

    @staticmethod
    def _commit(tok, R, W):
        k = id(tok[0])
        for b in R:
            b.r[k] = tok
        for b in W:
            b.w = tok
            b.r = {}

    def op(self, E, method, *args, R=(), W=(), **kw):
        eng = self.eng[E]
        self._wait(eng, self._deps(R, W))
        ins = getattr(eng.obj, method)(*args, **kw)
        eng.cnt += 1
        ins.then_inc(eng.sem, 1)
        tok = (eng.sem, eng.cnt)
        self._commit(tok, R, W)
        return tok

    def mm_raw(self, items, R=(), W=()):
        eng = self.eng["pe"]
        self._wait(eng, self._deps(R, W))
        ins = None
        for (out, l, r, st, sp) in items:
            ins = eng.obj.matmul(out, lhsT=l, rhs=r, start=st, stop=sp)
        eng.cnt += 1
        ins.then_inc(eng.sem, 1)
        tok = (eng.sem, eng.cnt)
        self._commit(tok, R, W)
        return tok

    def mm(self, out, pairs, R=(), W=(), Rper=None, **kw):
        eng = self.eng["pe"]
        self._wait(eng, self._deps(R, W))
        n = len(pairs)
        ins = None
        for i, (l, r) in enumerate(pairs):
            if Rper is not None:
                self._wait(eng, self._deps(Rper[i], ()))
            ins = eng.obj.matmul(out, lhsT=l, rhs=r, start=(i == 0), stop=(i == n - 1), **kw)
        eng.cnt += 1
        ins.then_inc(eng.sem, 1)
        tok = (eng.sem, eng.cnt)
        allR = list(R)
        if Rper is not None:
            for rr in Rper:
                allR += list(rr)
        self._commit(tok, allR, W)
        return tok

    def mm_multi(self, groups, R=(), W=()):
        eng = self.eng["pe"]
        self._wait(eng, self._deps(R, W))
        ins = None
        for out, pairs in groups:
            n = len(pairs)
            for i, (l, r) in enumerate(pairs):
                ins = eng.obj.matmul(out, lhsT=l, rhs=r, start=(i == 0), stop=(i == n - 1))
        eng.cnt += 1
        ins.then_inc(eng.sem, 1)
        tok = (eng.sem, eng.cnt)
        self._commit(tok, R, W)
        return tok

    def dma(self, q, out, in_, R=(), W=(), **kw):
        eng = self.eng[q]
        toks = self._deps(R, W)
        slot = self.dslots[q][self.dcnt[q] % self.ND]
        self.dcnt[q] += 1
        if slot[1] > 0:
            toks.append((slot[0], slot[1]))
        self._wait(eng, toks)
        ins = eng.obj.dma_start(out=out, in_=in_, **kw)
        slot[1] += 16
        ins.then_inc(slot[0], 16)
        tok = (slot[0], slot[1])
        self._commit(tok, R, W)
        return tok

    def barrier(self):
        toks = [(e.sem, e.cnt) for e in self.eng.values() if e.cnt > 0]
        for q in self.dslots:
            toks += [(sl[0], sl[1]) for sl in self.dslots[q] if sl[1] > 0]
        for e in self.eng.values():
            self._wait(e, toks)

    def bank(self):
        lo, hi = self.bank_range
        b = self.psum_banks[lo + self.psum_i % (hi - lo)]
        self.psum_i += 1
        return b


class Rot:
    def __init__(self, items):
        self.items = items
        self.i = 0

    def next(self):
        it = self.items[self.i % len(self.items)]
        self.i += 1
        return it


def weight_plan():
    tiles = []
    groups = {}

    def add(gname, pieces):
        groups.setdefault(gname, []).append(len(tiles))
        tiles.append(pieces)

    for f in ("ffn1", "ffn2"):
        for mt in range(DFF // GW):
            add(f + "_gu", [(f + "_w_gate", D, mt * GW, GW), (f + "_w_up", D, mt * GW, GW)])
        for m in range(KC):
            add(f + "_down", [(f + "_w_down", DFF, m * 128, 128)])
    for c0 in list(range(O_Q, O_V, 512)) + list(range(O_ZF, O_IH, 512)) + [O_GH, O_UF]:
        add("in_fm", [("w_in", D, c0, 512)])
    for c0 in list(range(O_V, O_ZF, 512)) + [O_IH]:
        add("in_tm", [("w_in", D, c0, 512)])
    for m in range(KC):
        add("p5m", [("w_in", D, O_GL + br * D + m * 128, 128) for br in range(3)]
            + [(w, 512, m * 128, 128) for w in ("w_att_out", "w_hgrn_out", "w_fnet_out")])
    for c0 in (0, 512):
        add("w_out", [("w_out", D, c0, 512)])
    sizes = [sum(k * g for (_, k, _, g) in t) for t in tiles]
    offs = list(np.cumsum([0] + sizes[:-1]))
    return tiles, groups, [int(o) for o in offs], sizes, int(sum(sizes))


WTILES, WGROUPS, WOFFS, WSIZES, WLAYER = weight_plan()
IN_FM_COLS = list(range(O_Q, O_V, 512)) + list(range(O_ZF, O_IH, 512)) + [O_GH, O_UF]
IN_TM_COLS = list(range(O_V, O_ZF, 512)) + [O_IH]
WSLOT = max(s // 128 for s in WSIZES)
CAST_CH = 128 * 8192
WTOTAL = ((DEPTH * WLAYER + CAST_CH - 1) // CAST_CH) * CAST_CH


def host_pack_weights(inputs):
    flat = np.zeros(WTOTAL, np.float32)
    ws = {k: np.asarray(v) for k, v in inputs.items() if k.startswith("w_") or "_w_" in k}
    for l in range(DEPTH):
        for ti, pieces in enumerate(WTILES):
            blks = []
            for (wname, kdim, c0, g) in pieces:
                w = ws[wname][l]
                kc = kdim // 128
                blks.append(w[:, c0:c0 + g].reshape(kc, 128, g).transpose(1, 0, 2).reshape(128, kc * g))
            blk = np.concatenate(blks, axis=1)
            o = l * WLAYER + WOFFS[ti]
            flat[o:o + blk.size] = blk.reshape(-1)
    return flat


class WStream:
    def __init__(self, kb, G, order):
        self.kb, self.G, self.order = kb, G, order
        self.pos = 0
        self.issued = 0

    def _issue(self, i):
        G = self.G
        l, ti = self.order[i]
        t, b = G.wslots[G.wslot_i % len(G.wslots)]
        G.wslot_i += 1
        n = WSIZES[ti] // 128
        o = l * WLAYER + WOFFS[ti]
        src = G.wbf[o:o + WSIZES[ti]].rearrange("(p n) -> p n", p=128)
        self.kb.dma("sp", t[:, 0:n], src, R=[G.wcast_b], W=[b])
        return (t, b)

    def get(self):
        depth = len(self.G.wslots)
        if self.pos == 0:
            self.live = []
        while self.issued < min(self.pos + depth, len(self.order)):
            self.live.append(self._issue(self.issued))
            self.issued += 1
        r = self.live[self.pos]
        self.pos += 1
        return r


class Glob:
    pass


def seq_scratch(kb, name, L, dbg):
    S = Glob()
    S.L = L
    S.name = name

    def mk(nm, shape, dt):
        kind = "Internal"
        if nm in dbg.get("out", ()):
            kind = "ExternalOutput"
        if nm in dbg.get("in", ()):
            kind = "ExternalInput"
        return kb.dram(f"{name}_{nm}", shape, dt, kind=kind)

    S.x = [kb.dram(f"{name}_x", [D, L], F32, kind="ExternalInput")]
    for l in range(1, DEPTH):
        S.x.append(mk(f"xl{l}", [D, L], F32))
    S.y = kb.dram(f"{name}_y", [D, L], F32, kind="ExternalOutput")
    S.x1 = mk("x1", [D, L], F32)
    S.qk = mk("qk", [NQK, L], BF16)
    S.v = mk("v", [L, 1536], BF16)
    S.z = mk("z", [1024, L], F32)
    S.qh = mk("qh", [512, L], BF16)
    S.gh = mk("gh", [512, L], BF16)
    S.ih = mk("ih", [L, 512], BF16)
    S.ucs = mk("ucs", [L, 1024], BF16)
    S.att = mk("att", [512, L], BF16)
    S.hg = mk("hg", [512, L], BF16)
    S.fn = mk("fn", [512, L], BF16)
    S.ob = mk("ob", [512, L], F32)
    S.fnb = kb.dram(f"{name}_fnb", [128, 4 * (L // 128) + 256], BF16, kind="ExternalInput")
    S.fnf = kb.dram(f"{name}_fnf", [128, 2 * (L // 128)], F32, kind="ExternalInput")
    S.rope = kb.dram(f"{name}_rope", [128, 2, L], BF16, kind="ExternalInput")
    S.bufs = {}
    return S


def sbuf_of(S, nm, t):
    k = (nm, t)
    if k not in S.bufs:
        S.bufs[k] = Buf(f"{S.name}_{nm}_{t}")
    return S.bufs[k]


def rmsnorm_a(kb, G, P, src, src_b):
    sq, sq_b = P.sqr.next()
    for c in range(KC):
        e = "pool" if c % 3 == 2 else "dve"
        kb.op(e, "tensor_tensor", out=sq[:, c, :], in0=src[:, c, :], in1=src[:, c, :], op=ALU.mult, R=[src_b[c]], W=[sq_b[c]])
    return sq, sq_b


def rmsnorm_b(kb, G, P, sqh, src, src_b, gcol, dst, dst_b):
    sq, sq_b = sqh
    bk, bk_b = kb.bank()
    kb.mm(bk[:, 0:T], [(G.ones[:], sq[:, c, :]) for c in range(KC)], R=[G.const_b], W=[bk_b], Rper=[[sq_b[c]] for c in range(KC)])
    rs, rs_b = P.rstd.next()
    kb.op("act", "activation", out=rs[:], in_=bk[:, 0:T], func=AF.Sqrt, bias=G.epsD[:], scale=1.0 / D, R=[bk_b, G.const_b], W=[rs_b])
    kb.op("dve", "reciprocal", out=rs[:], in_=rs[:], R=[rs_b], W=[rs_b])
    for c in range(KC):
        kb.op("dve", "scalar_tensor_tensor", out=dst[:, c, :], in0=src[:, c, :], scalar=G.vecs[:, gcol + c:gcol + c + 1],
              in1=rs[:], op0=ALU.mult, op1=ALU.mult, R=[src_b[c], rs_b, G.const_b], W=[dst_b[c]])


def rmsnorm(kb, G, P, src, src_b, gcol, dst, dst_b):
    sqh = rmsnorm_a(kb, G, P, src, src_b)
    rmsnorm_b(kb, G, P, sqh, src, src_b, gcol, dst, dst_b)


def ffn(kb, G, P, ws, xn, xn_b, resid, resid_b, out, out_b, hook=None):
    for mt in range(DFF // GW):
        if mt == 5 and hook is not None:
            hook()
        wt, wb = ws.get()
        w = wt[:, 0:2 * KC * GW].rearrange("p (s k g) -> p s k g", s=2, k=KC)
        for j in range(GW // 128):
            m = mt * (GW // 128) + j
            bg, bg_b = kb.bank()
            bu, bu_b = kb.bank()
            kb.mm(bg[:, 0:T], [(w[:, 0, k, j * 128:(j + 1) * 128], xn[:, k, :]) for k in range(KC)], R=[wb], W=[bg_b],
                  Rper=[[xn_b[k]] for k in range(KC)])
            kb.mm(bu[:, 0:T], [(w[:, 1, k, j * 128:(j + 1) * 128], xn[:, k, :]) for k in range(KC)], R=[wb] + xn_b, W=[bu_b])
            sg, sg_b = P.sgt.next()
            kb.op("act", "activation", out=sg[:], in_=bg[:, 0:T], func=AF.Silu, R=[bg_b], W=[sg_b])
            kb.op("dve", "tensor_tensor", out=P.hid[:, m, :], in0=sg[:], in1=bu[:, 0:T], op=ALU.mult,
                  R=[sg_b, bu_b], W=[P.hid_b[m]])
    for m in range(KC):
        wt, wb = ws.get()
        w = wt[:, 0:FC * 128].rearrange("p (k g) -> p k g", k=FC)
        bk, bk_b = kb.bank()
        kb.mm(bk[:, 0:T], [(w[:, k, :], P.hid[:, k, :]) for k in range(FC)], R=[wb] + P.hid_b, W=[bk_b])
        kb.op("dve", "scalar_tensor_tensor", out=out[:, m, :], in0=bk[:, 0:T], scalar=0.5, in1=resid[:, m, :],
              op0=ALU.mult, op1=ALU.add, R=[bk_b, resid_b[m]], W=[out_b[m]])


def fm_view(ap2d, c0, nchunk, t0, n):
    return ap2d[c0:c0 + nchunk * 128, t0:t0 + n].rearrange("(c p) t -> p c t", p=128)


def alloc_dense(kb, es, nc):
    P = Glob()

    kb.nuid += 1
    uid = kb.nuid

    def sb(name, shape, dt):
        return es.enter_context(nc.sbuf_tensor(f"{name}_u{uid}", list(shape), dt))
    P.sb = sb

    P.xin = Rot([(sb(f"xin{i}", [128, KC, T], F32), [Buf() for _ in range(KC)]) for i in range(2)])
    P.sqr = Rot([(sb(f"sq{i}", [128, KC, T], BF16), [Buf() for _ in range(KC)]) for i in range(2)])
    P.rstd = Rot([(sb(f"rstd{i}", [128, T], F32), Buf()) for i in range(2)])
    P.xn = Rot([(sb(f"xn{i}", [128, KC, T], BF16), [Buf() for _ in range(KC)]) for i in range(3)])
    P.hid = sb("hid", [128, FC, T], BF16)
    P.hid_b = [Buf() for _ in range(FC)]
    P.sgt = Rot([(sb(f"sgt{i}", [128, T], F32), Buf()) for i in range(3)])
    P.x1 = Rot([(sb(f"x1_{i}", [128, KC, T], F32), [Buf() for _ in range(KC)]) for i in range(1)])
    P.stg = Rot([(sb(f"stg{i}", [128, 4, T], BF16), Buf()) for i in range(2)])
    P.stgf = Rot([(sb(f"stgf{i}", [128, 4, T], F32), Buf()) for i in range(1)])
    P.wslots = [(sb(f"wslot{i}", [128, WSLOT], BF16), Buf()) for i in range(5)]
    return P


def phase_p1(kb, G, S, l):
    nc = kb.nc
    L = S.L
    NT = L // T
    with contextlib.ExitStack() as es:
        P = alloc_dense(kb, es, nc)
        G.wslots = P.wslots
        order = []
        for t in range(NT):
            order += [(l, i) for i in WGROUPS["ffn1_gu"] + WGROUPS["ffn1_down"] + WGROUPS["in_fm"] + WGROUPS["in_tm"]]
        ws = WStream(kb, G, order)
        vb = l * G.NV
        def preA(t):
            xin, xin_b = P.xin.next()
            kb.dma("sp", xin[:], fm_view(S.x[l], 0, KC, t * T, T), R=[sbuf_of(S, f"x{l}", t)], W=xin_b)
            sqh = rmsnorm_a(kb, G, P, xin, xin_b)
            return (xin, xin_b, sqh)

        def preB(pa):
            xin, xin_b, sqh = pa
            xn, xn_b = P.xn.next()
            rmsnorm_b(kb, G, P, sqh, xin, xin_b, vb + 0, xn, xn_b)
            return (xin, xin_b, xn, xn_b)

        nxt = preB(preA(0))
        for t in range(NT):
            t0 = t * T
            xin, xin_b, xn, xn_b = nxt
            nxt_a = None
            x1, x1_b = P.x1.next()
            ffn(kb, G, P, ws, xn, xn_b, xin, xin_b, x1, x1_b)
            kb.dma("pool", fm_view(S.x1, 0, KC, t0, T), x1[:], R=x1_b, W=[sbuf_of(S, "x1", t)])
            hn, hn_b = P.xn.next()
            rmsnorm(kb, G, P, x1, x1_b, vb + 8, hn, hn_b)
            if t + 1 < NT:
                nxt_a = preA(t + 1)
            for j, c0 in enumerate(IN_FM_COLS):
                if j == 5 and nxt_a is not None:
                    nxt = preB(nxt_a)
                wt, wb = ws.get()
                w = wt[:, 0:KC * 512].rearrange("p (k g) -> p k g", k=KC)
                isz = (O_ZF <= c0 < O_QH)
                isu = (c0 == O_UF)
                st, st_b = (P.stgf if isz else P.stg).next()
                for mc in range(4):
                    bk, bk_b = kb.bank()
                    kb.mm(bk[:, 0:T], [(w[:, k, mc * 128:(mc + 1) * 128], hn[:, k, :]) for k in range(KC)],
                          R=[wb], W=[bk_b], Rper=[[hn_b[k]] for k in range(KC)])
                    if c0 < O_V:
                        gcol = G.vecs[:, vb + 24:vb + 25] if c0 < O_K else G.vecs[:, vb + 25:vb + 26]
                        if mc % 2 == 0:
                            kb.op("act", "activation", out=st[:, mc, :], in_=bk[:, 0:T], func=AF.Copy, scale=gcol, R=[bk_b, G.const_b], W=[st_b])
                        else:
                            kb.op("dve", "tensor_scalar", out=st[:, mc, :], in0=bk[:, 0:T], scalar1=gcol, scalar2=None, op0=ALU.mult,
                                  R=[bk_b, G.const_b], W=[st_b])
                    elif mc % 2 == 0:
                        kb.op("act", "activation", out=st[:, mc, :], in_=bk[:, 0:T], func=AF.Copy, R=[bk_b], W=[st_b])
                    else:
                        kb.op("dve", "tensor_copy", out=st[:, mc, :], in_=bk[:, 0:T], R=[bk_b], W=[st_b])
                if c0 < O_V:
                    kb.dma("pool", fm_view(S.qk, c0, 4, t0, T), st[:], R=[st_b], W=[sbuf_of(S, "qk", t)])
                elif isz:
                    kb.dma("pool", fm_view(S.z, c0 - O_ZF, 4, t0, T), st[:], R=[st_b], W=[sbuf_of(S, "z", t)])
                elif c0 == O_QH:
                    kb.dma("pool", fm_view(S.qh, 0, 4, t0, T), st[:], R=[st_b], W=[sbuf_of(S, "qh", t)])
                elif c0 == O_GH:
                    kb.dma("pool", fm_view(S.gh, 0, 4, t0, T), st[:], R=[st_b], W=[sbuf_of(S, "gh", t)])
                elif isu:
                    for tb in range(4):
                        st2, st2_b = P.stg.next()
                        for g in range(4):
                            bk, bk_b = kb.bank()
                            kb.mm(bk[:, 0:256], [(st[:, g, tb * 128:(tb + 1) * 128], G.ccsc[:])],
                                  R=[st_b, G.const_b], W=[bk_b])
                            e = "act" if g % 2 == 0 else "dve"
                            if e == "act":
                                kb.op("act", "activation", out=st2[:, g, 0:256], in_=bk[:, 0:256], func=AF.Copy, R=[bk_b], W=[st2_b])
                            else:
                                kb.op("dve", "tensor_copy", out=st2[:, g, 0:256], in_=bk[:, 0:256], R=[bk_b], W=[st2_b])
                        dst = S.ucs[t0 + tb * 128:t0 + (tb + 1) * 128, :].rearrange("p (g c) -> p g c", g=4)
                        kb.dma("pool", dst, st2[:, :, 0:256], R=[st2_b], W=[sbuf_of(S, "ucs", t)])
            for j, c0 in enumerate(IN_TM_COLS):
                wt, wb = ws.get()
                w = wt[:, 0:KC * 512].rearrange("p (k g) -> p k g", k=KC)
                st, st_b = P.stg.next()
                for tb in range(4):
                    bk, bk_b = kb.bank()
                    kb.mm(bk[:, 0:512], [(hn[:, k, tb * 128:(tb + 1) * 128], w[:, k, :]) for k in range(KC)],
                          R=[wb] + hn_b, W=[bk_b])
                    if tb % 2 == 0:
                        kb.op("act", "activation", out=st[:, tb, :], in_=bk[:, 0:512], func=AF.Copy, R=[bk_b], W=[st_b])
                    else:
                        kb.op("dve", "tensor_copy", out=st[:, tb, :], in_=bk[:, 0:512], R=[bk_b], W=[st_b])
                if c0 < O_ZF:
                    dst = S.v[t0:t0 + T, c0 - O_V:c0 - O_V + 512].rearrange("(b p) c -> p b c", p=128)
                    kb.dma("pool", dst, st[:], R=[st_b], W=[sbuf_of(S, "v", t)])
                else:
                    dst = S.ih[t0:t0 + T, :].rearrange("(b p) c -> p b c", p=128)
                    kb.dma("pool", dst, st[:], R=[st_b], W=[sbuf_of(S, "ih", t)])
        kb.barrier()


def phase_p5(kb, G, S, l):
    nc = kb.nc
    L = S.L
    NT = L // T
    last = (l + 1 >= G.nlayers)
    xout = S.y if last else S.x[l + 1]
    xout_nm = "y" if last else f"x{l + 1}"
    with contextlib.ExitStack() as es:
        P = alloc_dense(kb, es, nc)
        G.wslots = P.wslots
        P.br = [(P.sb(f"br{i}", [128, 4, T], BF16), Buf()) for i in range(3)]
        order = []
        for t in range(NT):
            order += [(l, i) for i in WGROUPS["p5m"] + WGROUPS["w_out"] + WGROUPS["ffn2_gu"] + WGROUPS["ffn2_down"]]
        ws = WStream(kb, G, order)
        vb = l * G.NV
        def preA(t):
            x1, x1_b = P.xin.next()
            kb.dma("sp", x1[:], fm_view(S.x1, 0, KC, t * T, T), R=[sbuf_of(S, "x1", t)], W=x1_b)
            for i, (nm, src) in enumerate((("att", S.att), ("hg", S.hg), ("fn", S.fn))):
                kb.dma("sp", P.br[i][0][:], fm_view(src, 0, 4, t * T, T), R=[sbuf_of(S, nm, t)], W=[P.br[i][1]])
            sqh = rmsnorm_a(kb, G, P, x1, x1_b)
            return (x1, x1_b, sqh)

        def preB(pa):
            x1, x1_b, sqh = pa
            hn, hn_b = P.xn.next()
            rmsnorm_b(kb, G, P, sqh, x1, x1_b, vb + 8, hn, hn_b)
            return (x1, x1_b, hn, hn_b)

        nxt = [preB(preA(0))]
        for t in range(NT):
            t0 = t * T
            x1, x1_b, hn, hn_b = nxt[0]
            mg, mg_b = P.xn.next()
            for m in range(KC):
                wt, wb = ws.get()
                wg = wt[:, 0:3 * KC * 128].rearrange("p (b k g) -> p b k g", b=3, k=KC)
                wo = wt[:, 3 * KC * 128:3 * KC * 128 + 3 * 4 * 128].rearrange("p (b k g) -> p b k g", b=3, k=4)
                prods = []
                for br in range(3):
                    bg, bg_b = kb.bank()
                    bp, bp_b = kb.bank()
                    kb.mm(bg[:, 0:T], [(wg[:, br, k, :], hn[:, k, :]) for k in range(KC)], R=[wb], W=[bg_b], Rper=[[hn_b[k]] for k in range(KC)])
                    kb.mm(bp[:, 0:T], [(wo[:, br, k, :], P.br[br][0][:, k, :]) for k in range(4)],
                          R=[wb, P.br[br][1]], W=[bp_b])
                    sg, sg_b = P.sgt.next()
                    kb.op("act", "activation", out=sg[:], in_=bg[:, 0:T], func=AF.Sigmoid, R=[bg_b], W=[sg_b])
                    kb.op("dve", "tensor_tensor", out=sg[:], in0=sg[:], in1=bp[:, 0:T], op=ALU.mult,
                          R=[sg_b, bp_b], W=[sg_b])
                    prods.append((sg, sg_b))
                kb.op("pool", "tensor_tensor", out=prods[0][0][:], in0=prods[0][0][:], in1=prods[1][0][:], op=ALU.add,
                      R=[prods[1][1]], W=[prods[0][1]])
                kb.op("pool", "tensor_tensor", out=mg[:, m, :], in0=prods[0][0][:], in1=prods[2][0][:], op=ALU.add,
                      R=[prods[0][1], prods[2][1]], W=[mg_b[m]])
            x2, x2_b = P.x1.next()
            for j in range(2):
                wt, wb = ws.get()
                w = wt[:, 0:KC * 512].rearrange("p (k g) -> p k g", k=KC)
                for mc in range(4):
                    m = j * 4 + mc
                    bk, bk_b = kb.bank()
                    kb.mm(bk[:, 0:T], [(w[:, k, mc * 128:(mc + 1) * 128], mg[:, k, :]) for k in range(KC)],
                          R=[wb], W=[bk_b], Rper=[[mg_b[k]] for k in range(KC)])
                    kb.op("dve", "tensor_tensor", out=x2[:, m, :], in0=bk[:, 0:T], in1=x1[:, m, :], op=ALU.add,
                          R=[bk_b, x1_b[m]], W=[x2_b[m]])
            xn, xn_b = P.xn.next()
            rmsnorm(kb, G, P, x2, x2_b, vb + 16, xn, xn_b)
            x3, x3_b = P.xin.next()
            hook = None
            if t + 1 < NT:
                pa = preA(t + 1)

                def hook(pa=pa):
                    nxt[0] = preB(pa)
            ffn(kb, G, P, ws, xn, xn_b, x2, x2_b, x3, x3_b, hook=hook)
            G.out_toks.append(kb.dma("pool", fm_view(xout, 0, KC, t0, T), x3[:], R=x3_b, W=[sbuf_of(S, xout_nm, t)]))
        kb.barrier()


NV = 32
CB_ONES, CB_CCSC, CB_IDENT, CB_BLK, CB_ROT, CB_HMF, CB_HMB, CB_AM, CB_END = 0, 128, 384, 512, 640, 768, 896, 1024, 1024 + 4 * 512


def host_consts():
    cb = np.zeros((128, CB_END), np.float32)
    cb[:, CB_ONES:CB_ONES + 128] = 1.0
    c = np.arange(128)
    ang = 2 * np.pi * np.outer(c, c) / 128.0
    cb[:, CB_CCSC:CB_CCSC + 128] = np.cos(ang) / np.sqrt(128.0)
    cb[:, CB_CCSC + 128:CB_CCSC + 256] = np.sin(ang) / np.sqrt(128.0)
    cb[:, CB_IDENT:CB_IDENT + 128] = np.eye(128)
    blk = np.zeros((128, 128))
    blk[:64, :64] = 1
    blk[64:, 64:] = 1
    cb[:, CB_BLK:CB_BLK + 128] = blk
    rot = np.zeros((128, 128))
    for h in range(2):
        for i in range(32):
            rot[h * 64 + 32 + i, h * 64 + i] = -1.0
            rot[h * 64 + i, h * 64 + 32 + i] = 1.0
    cb[:, CB_ROT:CB_ROT + 128] = rot
    sidx = np.arange(128)[:, None]
    tidx = np.arange(128)[None, :]
    same = (sidx // 64) == (tidx // 64)
    cb[:, CB_HMF:CB_HMF + 128] = (same & (sidx <= tidx))
    cb[:, CB_HMB:CB_HMB + 128] = (same & (sidx >= tidx))
    m1 = (tidx <= sidx).astype(np.float32)
    m2 = (tidx >= sidx).astype(np.float32)
    m1f = m1 * (sidx >= 64)
    m2l = m2 * (sidx < 64)
    for v, (a, b) in enumerate(((m1, m2), (m1f, m2), (m1, m2l), (m1f, m2l))):
        cb[:, CB_AM + v * 512:CB_AM + (v + 1) * 512] = (np.concatenate([a, a, b, b], axis=1) - 1.0) * 30000.0
    return cb.astype(NPBF)


def host_vecs(inputs):
    v = np.zeros((128, DEPTH * NV), np.float32)
    for l in range(DEPTH):
        b = l * NV
        for i, nm in enumerate(("ffn1_norm", "mix_norm", "ffn2_norm")):
            v[:, b + 8 * i:b + 8 * i + 8] = np.asarray(inputs[nm])[l].reshape(8, 128).T
        v[:, b + 24] = np.tile(np.asarray(inputs["q_norm"])[l], 2)
        v[:, b + 25] = np.tile(np.asarray(inputs["k_norm"])[l], 2)
        v[:, b + 26] = np.asarray(inputs["hgrn_out_norm"])[l]
    return v


def host_lbl(inputs):
    lg = np.asarray(inputs["hgrn_lb_logits"])
    return np.ascontiguousarray(lg.reshape(DEPTH * 2 * 4, 128).T)


def build(Lp, Ls, nlayers=DEPTH, dbg=None, phases=("p1", "p2", "p3", "p4", "p5")):
    dbg = dbg or {}
    kb = KB()
    nc = kb.nc
    G = Glob()
    G.NV = NV
    G.nlayers = nlayers
    G.out_toks = []
    wsrc = kb.dram("wsrc", [WTOTAL], F32, kind="ExternalInput")
    G.wbf = kb.dram("wbf", [WTOTAL], BF16)
    vecs_d = kb.dram("vecs", [128, DEPTH * NV], F32, kind="ExternalInput")
    cb_d = kb.dram("cbf", [128, CB_END], BF16, kind="ExternalInput")
    G.vecs = kb.sbuf("vecs_sb", [128, DEPTH * NV], F32)
    G.cb = kb.sbuf("cb_sb", [128, CB_END], BF16)
    G.ones = G.cb[:, CB_ONES:CB_ONES + 128]
    G.ccsc = G.cb[:, CB_CCSC:CB_CCSC + 256]
    G.ident = G.cb[:, CB_IDENT:CB_IDENT + 128]
    G.blk = G.cb[:, CB_BLK:CB_BLK + 128]
    G.rot = G.cb[:, CB_ROT:CB_ROT + 128]
    G.hmask = [G.cb[:, CB_HMF:CB_HMF + 128], G.cb[:, CB_HMB:CB_HMB + 128]]
    G.amask = [G.cb[:, CB_AM + v * 512:CB_AM + (v + 1) * 512] for v in range(4)]
    G.const_b = Buf("const")
    G.epsc = kb.sbuf("epsc", [128, 2], F32)
    G.epsD = G.epsc[:, 0:1]
    G.wcast_b = Buf("wcast")
    G.wslot_i = 0
    for i in range(8):
        kb.psum_banks.append((kb.psum(f"bank{i}", [128, 512], F32), Buf()))
    seqs = []
    if Lp:
        seqs.append(seq_scratch(kb, "p", Lp, dbg))
    if Ls:
        seqs.append(seq_scratch(kb, "s", Ls, dbg))
    kb.op("dve", "memset", G.epsc[:], EPS, W=[G.const_b])
    kb.dma("sp", G.vecs[:], vecs_d[:, :], W=[G.const_b])
    kb.dma("sp", G.cb[:], cb_d[:, :], W=[G.const_b])
    lbl_d = kb.dram("lbl", [128, DEPTH * 8], F32, kind="ExternalInput")
    G.lbe = kb.sbuf("lbe", [128, DEPTH * 8], F32)
    G.lb = kb.sbuf("lb", [128, DEPTH * 8], F32)
    G.oml = kb.sbuf("oml", [128, DEPTH * 8], F32)
    G.lbt = kb.sbuf("lbt", [128, 8], F32)
    G.rmask = kb.sbuf("rmask", [128, T], F32)
    lb_b = Buf()
    kb.dma("sp", G.lbe[:], lbl_d[:, :], W=[lb_b])
    kb.op("act", "activation", out=G.lbe[:], in_=G.lbe[:], func=AF.Exp, R=[lb_b], W=[lb_b])
    kb.op("dve", "tensor_copy", out=G.lbt[:], in_=G.lbe[:, 0:8], R=[lb_b], W=[lb_b])
    for l in range(1, DEPTH):
        kb.op("dve", "tensor_tensor", out=G.lbt[:], in0=G.lbt[:], in1=G.lbe[:, l * 8:(l + 1) * 8], op=ALU.add, R=[lb_b], W=[lb_b])
    kb.op("dve", "reciprocal", out=G.lbt[:], in_=G.lbt[:], R=[lb_b], W=[lb_b])
    kb.op("dve", "memset", G.lb[:, 0:8], 0.0, W=[lb_b])
    for l in range(1, DEPTH):
        kb.op("dve", "tensor_tensor", out=G.lbe[:, l * 8:(l + 1) * 8], in0=G.lbe[:, l * 8:(l + 1) * 8], in1=G.lbt[:], op=ALU.mult, R=[lb_b], W=[lb_b])
        kb.op("dve", "tensor_tensor", out=G.lb[:, l * 8:(l + 1) * 8], in0=G.lb[:, (l - 1) * 8:l * 8], in1=G.lbe[:, l * 8:(l + 1) * 8], op=ALU.add, R=[lb_b], W=[lb_b])
    kb.op("dve", "tensor_scalar", out=G.oml[:], in0=G.lb[:], scalar1=-1.0, scalar2=1.0, op0=ALU.mult, op1=ALU.add, R=[lb_b], W=[G.const_b])
    kb.op("dve", "memset", G.rmask[:], 1.0, W=[G.const_b])
    kb.op("dve", "memset", G.rmask[:].rearrange("p (c j) -> p c j", j=64)[:, :, 0:1], 0.0, W=[G.const_b])
    for i in range(WTOTAL // CAST_CH):
        kb.dma("pool", G.wbf[i * CAST_CH:(i + 1) * CAST_CH].rearrange("(p n) -> p n", p=128),
               wsrc[i * CAST_CH:(i + 1) * CAST_CH].rearrange("(p n) -> p n", p=128))
    kb.barrier()
    for l in range(nlayers):
        for S in seqs:
            if "p1" in phases:
                phase_p1(kb, G, S, l)
            if "p2" in phases:
                phase_p2(kb, G, S)
            if "p3" in phases:
                phase_p3(kb, G, S, l)
            if "p4" in phases:
                phase_p4(kb, G, S, l)
            if "p5" in phases:
                phase_p5(kb, G, S, l)
    kb.barrier()
    kb.es.close()
    return kb, seqs


def host_fnet_consts(L):
    N1 = L // 128
    fb = np.zeros((128, 4 * N1 + 256), np.float32)
    s1 = np.arange(N1)
    a1 = 2 * np.pi * np.outer(s1, s1) / N1
    fb[:N1, 0:N1] = np.cos(a1)
    fb[:N1, N1:2 * N1] = np.sin(a1)
    fb[:N1, 2 * N1:3 * N1] = -np.sin(a1)
    fb[:N1, 3 * N1:4 * N1] = np.cos(a1)
    s0 = np.arange(128)
    a2 = 2 * np.pi * np.outer(s0, s0) / 128.0
    fb[:, 4 * N1:4 * N1 + 128] = np.cos(a2) / np.sqrt(L)
    fb[:, 4 * N1 + 128:4 * N1 + 256] = -np.sin(a2) / np.sqrt(L)
    ff = np.zeros((128, 2 * N1), np.float32)
    at = 2 * np.pi * np.outer(s0, s1) / L
    ff[:, 0:N1] = np.cos(at)
    ff[:, N1:2 * N1] = np.sin(at)
    return fb.astype(NPBF), ff


def phase_p2(kb, G, S):
    nc = kb.nc
    L = S.L
    N1 = L // 128
    NT = L // T
    CH = 64
    cpb1 = 512 // (2 * N1)
    cpb2 = min(512 // N1, CH)
    with contextlib.ExitStack() as es:
        kb.nuid += 1
        uid = kb.nuid

        def sb(name, shape, dt):
            return es.enter_context(nc.sbuf_tensor(f"{name}_u{uid}", list(shape), dt))

        fb = sb("fn_fb", [128, 4 * N1 + 256], BF16)
        ff = sb("fn_ff", [128, 2 * N1], F32)
        cst_b = Buf()
        kb.dma("sp", fb[:], S.fnb[:, :], W=[cst_b])
        kb.dma("sp", ff[:], S.fnf[:, :], W=[cst_b])
        F1a = fb[0:N1, 0:2 * N1]
        F1b = fb[0:N1, 2 * N1:4 * N1]
        Fc2 = fb[:, 4 * N1:4 * N1 + 128]
        nFs2 = fb[:, 4 * N1 + 128:4 * N1 + 256]
        Tc = ff[:, 0:N1]
        Ts = ff[:, N1:2 * N1]
        zt = Rot([(sb(f"fn_zt{i}", [N1, 128, 256], BF16), Buf()) for i in range(1)])
        gp = sb("fn_gp", [128, CH, 2, N1], BF16)
        gp_b = Buf()
        yt = Rot([(sb(f"fn_yt{i}", [128, CH, N1], BF16), Buf()) for i in range(2)])
        tmp = Rot([(sb(f"fn_tmp{i}", [128, 4, 256], F32), Buf()) for i in range(3)])
        all_ucs = [sbuf_of(S, "ucs", t) for t in range(NT)]
        all_fn = [sbuf_of(S, "fn", t) for t in range(NT)]
        for g in range(4):
            z, z_b = zt.next()
            src = S.ucs[:, g * 256:(g + 1) * 256].rearrange("(s1 s0) c -> s1 s0 c", s0=128)
            kb.dma("sp", z[:], src, R=all_ucs, W=[z_b])
            for half in range(128 // CH):
                for c0 in range(0, CH, cpb1):
                    bk, bk_b = kb.bank()
                    groups = []
                    for j in range(cpb1):
                        ch = half * CH + c0 + j
                        groups.append((bk[:, j * 2 * N1:(j + 1) * 2 * N1],
                                       [(z[:, :, ch], F1a), (z[:, :, 128 + ch], F1b)]))
                    kb.mm_multi(groups, R=[z_b, cst_b], W=[bk_b])
                    bv = bk[:, 0:cpb1 * 2 * N1].rearrange("p (c s n) -> p c s n", c=cpb1, s=2)
                    tcb = Tc.unsqueeze(1).broadcast_to([128, cpb1, N1])
                    tsb = Ts.unsqueeze(1).broadcast_to([128, cpb1, N1])
                    tm, tm_b = tmp.next()
                    tv = tm[:].rearrange("p a b -> p (a b)")[:, 0:4 * cpb1 * N1].rearrange("p (a c n) -> p a c n", a=4, c=cpb1)
                    kb.op("dve", "tensor_tensor", out=tv[:, 0], in0=bv[:, :, 0, :], in1=tcb, op=ALU.mult, R=[bk_b, cst_b], W=[tm_b])
                    kb.op("dve", "tensor_tensor", out=tv[:, 1], in0=bv[:, :, 1, :], in1=tsb, op=ALU.mult, R=[bk_b, cst_b], W=[tm_b])
                    kb.op("dve", "tensor_tensor", out=tv[:, 2], in0=bv[:, :, 1, :], in1=tcb, op=ALU.mult, R=[bk_b, cst_b], W=[tm_b])
                    kb.op("dve", "tensor_tensor", out=tv[:, 3], in0=bv[:, :, 0, :], in1=tsb, op=ALU.mult, R=[bk_b, cst_b], W=[tm_b])
                    kb.op("pool", "tensor_tensor", out=gp[:, c0:c0 + cpb1, 0, :], in0=tv[:, 0], in1=tv[:, 1], op=ALU.subtract,
                          R=[tm_b], W=[gp_b])
                    kb.op("pool", "tensor_tensor", out=gp[:, c0:c0 + cpb1, 1, :], in0=tv[:, 2], in1=tv[:, 3], op=ALU.add,
                          R=[tm_b], W=[gp_b])
                y, y_b = yt.next()
                for i, c0 in enumerate(range(0, CH, cpb2)):
                    bk, bk_b = kb.bank()
                    kb.mm(bk[:, 0:cpb2 * N1], [(Fc2, gp[:, c0:c0 + cpb2, 0, :]), (nFs2, gp[:, c0:c0 + cpb2, 1, :])],
                          R=[gp_b, cst_b], W=[bk_b])
                    src_v = bk[:, 0:cpb2 * N1].rearrange("p (c n) -> p c n", c=cpb2)
                    if i % 2 == 0:
                        kb.op("act", "activation", out=y[:, c0:c0 + cpb2, :], in_=src_v, func=AF.Copy, R=[bk_b], W=[y_b])
                    else:
                        kb.op("dve", "tensor_copy", out=y[:, c0:c0 + cpb2, :], in_=src_v, R=[bk_b], W=[y_b])
                r0 = g * 128 + half * CH
                dst = S.fn[r0:r0 + CH, :].rearrange("c (k0 k1) -> k0 c k1", k1=N1)
                kb.dma("pool", dst, y[:], R=[y_b], W=all_fn)
        kb.barrier()


def phase_p3(kb, G, S, l):
    nc = kb.nc
    L = S.L
    NT = L // T
    H4 = 4
    with contextlib.ExitStack() as es:
        kb.nuid += 1
        uid = kb.nuid

        def sb(name, shape, dt):
            return es.enter_context(nc.sbuf_tensor(f"{name}_u{uid}", list(shape), dt))

        def rot(name, shape, dt, n):
            return Rot([(sb(f"{name}{i}", shape, dt), Buf()) for i in range(n)])

        zt = rot("h_z", [128, H4, T], F32, 1)
        qt = rot("h_q", [128, H4, T], BF16, 1)
        it = rot("h_i", [128, 4, 512], BF16, 2)
        gt = rot("h_g", [128, H4, T], BF16, 1)
        obin = rot("h_obin", [128, H4, T], F32, 1)
        Bs = (sb("h_Bs", [128, H4, T], F32), Buf())
        Blf = (sb("h_Blf", [128, H4, T], F32), Buf())
        Bkf = (sb("h_Bkf", [128, H4, T], F32), Buf())
        Bb = (sb("h_Bb", [128, H4, T], F32), Buf())
        Bg = (sb("h_Bg", [128, H4, T], F32), Buf())
        Be = (sb("h_Be", [128, H4, T], F32), Buf())
        Bgm = (sb("h_Bgm", [128, H4, T], F32), Buf())
        BE2 = (sb("h_BE2", [128, H4, T], F32), Buf())
        Bs_h, Blf_h, Bkf_h, Bb_h, Bg_h, Be_h, Bgm_h, BE2_h = [[Buf() for _ in range(H4)] for _ in range(8)]
        dec = rot("h_dec", [128, H4, 8], F32, 2)
        qd = rot("h_qd", [128, H4, T], BF16, 2)
        kd = rot("h_kd", [128, H4, T], BF16, 2)
        qe = rot("h_qe", [128, H4, T], BF16, 2)
        keT = rot("h_keT", [128, H4, T], BF16, 1)
        ketm = rot("h_ketm", [128, 4, 512], BF16, 2)
        atm = rot("h_atm", [128, 128], BF16, 6)
        ost = rot("h_ost", [128, H4, T], F32, 1)
        sq = (sb("h_sq", [128, H4, T], BF16), Buf())
        rs = (sb("h_rs", [128, H4, T], F32), Buf())
        sgt = (sb("h_sgt", [128, H4, T], F32), Buf())
        hout = rot("h_out", [128, H4, T], BF16, 1)
        St = [[sb(f"h_S{h}_{i}", [128, 128], F32) for i in range(2)] for h in range(H4)]
        Sb_ = [[sb(f"h_Sb{h}_{i}", [128, 128], BF16) for i in range(2)] for h in range(H4)]
        St_b = [[Buf(), Buf()] for h in range(H4)]
        Sb_b = [[Buf(), Buf()] for h in range(H4)]
        gain = G.vecs[:, l * NV + 26:l * NV + 27]

        def v4(t):
            return t[:].rearrange("p h (c j) -> p h c j", j=64)

        def hv(t, hh):
            return t[:, hh * 2:hh * 2 + 2, :]

        def pre_ops(d, t):
            t0 = t * T
            res = {}
            ops = []

            def o_load():
                z, z_b = zt.next()
                kb.dma("sp", z[:], fm_view(S.z, d * 512, H4, t0, T), R=[sbuf_of(S, "z", t)], W=[z_b])
                q, q_b = qt.next()
                kb.dma("sp", q[:], fm_view(S.qh, 0, H4, t0, T), R=[sbuf_of(S, "qh", t)], W=[q_b])
                iv, iv_b = it.next()
                kb.dma("sp", iv[:], S.ih[t0:t0 + T, :].rearrange("(b p) c -> p b c", p=128), R=[sbuf_of(S, "ih", t)], W=[iv_b])
                res.update(z=z, z_b=z_b, q=q, q_b=q_b, iv=iv, iv_b=iv_b)
            ops.append(o_load)

            def H(t_, h):
                return t_[:, h, :]

            def H4v(t_, h):
                return t_[:, h, :].rearrange("p (c j) -> p c j", j=64)

            def mk(stage):
                for h in range(H4):
                    ops.append(lambda h=h: stage(h))

            def s1(h):
                kb.op("act", "activation", out=H(Bs[0], h), in_=H(res["z"], h), func=AF.Sigmoid, R=[res["z_b"]], W=[Bs_h[h]])
                col = l * 8 + d * 4 + h
                kb.op("dve", "tensor_scalar", out=H(Bs[0], h), in0=H(Bs[0], h), scalar1=G.oml[:, col:col + 1],
                      scalar2=G.lb[:, col:col + 1], op0=ALU.mult, op1=ALU.add, R=[G.const_b], W=[Bs_h[h]])
            mk(s1)

            def s2(h):
                kb.op("act", "activation", out=H(Blf[0], h), in_=H(Bs[0], h), func=AF.Ln, R=[Bs_h[h]], W=[Blf_h[h]])
                e = "pool" if h % 2 else "dve"
                kb.op(e, "tensor_scalar", out=H(Bkf[0], h), in0=H(Bs[0], h), scalar1=-1.0, scalar2=1.0, op0=ALU.mult, op1=ALU.add,
                      R=[Bs_h[h]], W=[Bkf_h[h]])
            mk(s2)

            def s3(h):
                kb.op("dve", "tensor_tensor_scan", out=H(Bb[0], h), data0=G.rmask[:], data1=H(Blf[0], h), initial=0.0,
                      op0=ALU.mult, op1=ALU.add, R=[Blf_h[h], G.const_b], W=[Bb_h[h]])
            mk(s3)

            def s4a():
                dc, dc_b = dec.next()
                res.update(dc=dc, dc_b=dc_b)
                res["gbuf"], res["gbuf_h"] = (Bb, Bb_h) if d == 0 else (Bg, Bg_h)
            ops.append(s4a)

            def s4(h):
                b4 = H4v(Bb[0], h)
                bend = b4[:, :, 63:64]
                bend_bc = bend.broadcast_to([128, 8, 64])
                dc, dc_b = res["dc"], res["dc_b"]
                kb.op("act", "activation", out=dc[:, h, :].rearrange("p (c o) -> p c o", o=1), in_=bend, func=AF.Exp, R=[Bb_h[h]], W=[dc_b])
                if d == 0:
                    kb.op("pool", "tensor_tensor", out=H4v(Be[0], h), in0=bend_bc, in1=b4, op=ALU.subtract, R=[Bb_h[h]], W=[Be_h[h]])
                else:
                    kb.op("dve", "tensor_tensor", out=H4v(Bg[0], h), in0=bend_bc, in1=b4, op=ALU.subtract, R=[Bb_h[h]], W=[Bg_h[h]])
                    kb.op("dve", "tensor_tensor", out=H(Bg[0], h), in0=H(Bg[0], h), in1=H(Blf[0], h), op=ALU.add, R=[Blf_h[h]], W=[Bg_h[h]])
                    kb.op("pool", "tensor_tensor", out=H(Be[0], h), in0=H(Bb[0], h), in1=H(Blf[0], h), op=ALU.subtract,
                          R=[Bb_h[h], Blf_h[h]], W=[Be_h[h]])
            mk(s4)

            def s5(h):
                gbuf, gbuf_h = res["gbuf"], res["gbuf_h"]
                g4 = H4v(gbuf[0], h)
                gmid_bc = g4[:, :, 32:33].broadcast_to([128, 8, 64])
                kb.op("dve", "tensor_tensor", out=H4v(Bgm[0], h), in0=g4, in1=gmid_bc, op=ALU.subtract, R=[gbuf_h[h]], W=[Bgm_h[h]])
                kb.op("act", "activation", out=H(BE2[0], h), in_=H(Bgm[0], h), func=AF.Exp, scale=-1.0, R=[Bgm_h[h]], W=[BE2_h[h]])
                kb.op("act", "activation", out=H(Bgm[0], h), in_=H(Bgm[0], h), func=AF.Exp, R=[Bgm_h[h]], W=[Bgm_h[h]])
            mk(s5)

            def s6a():
                qd_t, qd_b = qd.next()
                kd_t, kd_b = kd.next()
                qe_t, qe_b = qe.next()
                ke_t, ke_b = keT.next()
                res.update(qd_t=qd_t, qd_b=qd_b, kd_t=kd_t, kd_b=kd_b, qe_t=qe_t, qe_b=qe_b, ke_t=ke_t, ke_b=ke_b)
            ops.append(s6a)

            def s6(h):
                q, q_b = res["q"], res["q_b"]
                kb.op("dve", "tensor_tensor", out=H(res["qd_t"], h), in0=H(q, h), in1=H(Bgm[0], h), op=ALU.mult, R=[q_b, Bgm_h[h]], W=[res["qd_b"]])
                e = "pool" if h >= 2 else "dve"
                kb.op(e, "tensor_tensor", out=H(res["kd_t"], h), in0=H(Bkf[0], h), in1=H(BE2[0], h), op=ALU.mult, R=[Bkf_h[h], BE2_h[h]], W=[res["kd_b"]])
            mk(s6)

            def s7(h):
                gbuf, gbuf_h = res["gbuf"], res["gbuf_h"]
                kb.op("act", "activation", out=H(BE2[0], h), in_=H(gbuf[0], h), func=AF.Exp, R=[gbuf_h[h]], W=[BE2_h[h]])
                kb.op("act", "activation", out=H(Be[0], h), in_=H(Be[0], h), func=AF.Exp, R=[Be_h[h]], W=[Be_h[h]])
            mk(s7)

            def s8(h):
                q, q_b = res["q"], res["q_b"]
                kb.op("dve", "tensor_tensor", out=H(res["qe_t"], h), in0=H(q, h), in1=H(BE2[0], h), op=ALU.mult, R=[q_b, BE2_h[h]], W=[res["qe_b"]])
                e = "pool" if h >= 2 else "dve"
                kb.op(e, "tensor_tensor", out=H(res["ke_t"], h), in0=H(Bkf[0], h), in1=H(Be[0], h), op=ALU.mult, R=[Bkf_h[h], Be_h[h]], W=[res["ke_b"]])
            mk(s8)

            def o10():
                km, km_b = ketm.next()
                ke_t, ke_b = res["ke_t"], res["ke_b"]
                for tb in range(4):
                    bk, bk_b = kb.bank()
                    kb.mm_multi([(bk[:, h * 128:(h + 1) * 128], [(ke_t[:, h, tb * 128:(tb + 1) * 128], G.ident)]) for h in range(H4)],
                                R=[ke_b, G.const_b], W=[bk_b])
                    if tb % 2 == 0:
                        kb.op("act", "activation", out=km[:, tb, :], in_=bk[:, 0:512], func=AF.Copy, R=[bk_b], W=[km_b])
                    else:
                        kb.op("dve", "tensor_copy", out=km[:, tb, :], in_=bk[:, 0:512], R=[bk_b], W=[km_b])
                res.update(km=km, km_b=km_b)
            ops.append(o10)
            return ops, res

        def block_ops(d, t, res, cur):
            t0 = t * T
            ops = []
            st = {}

            def b_init():
                o, o_b = ost.next()
                st.update(o=o, o_b=o_b)
                if d == 0:
                    gg, gg_b = gt.next()
                    kb.dma("sp", gg[:], fm_view(S.gh, 0, H4, t0, T), R=[sbuf_of(S, "gh", t)], W=[gg_b])
                    obi, obi_b = obin.next()
                    kb.dma("sp", obi[:], fm_view(S.ob, 0, H4, t0, T), R=[sbuf_of(S, "ob", t)], W=[obi_b])
                    res.update(gg=gg, gg_b=gg_b, obi=obi, obi_b=obi_b)
            ops.append(b_init)
            blocks = [0, 1, 2, 3] if d == 0 else [3, 2, 1, 0]
            corder = (0, 1) if d == 0 else (1, 0)
            for tb in blocks:
                bst = {}

                def blkA(tb=tb, bst=bst):
                    tsl = slice(tb * 128, (tb + 1) * 128)
                    ats = []
                    for h in range(H4):
                        bkA, bkA_b = kb.bank()
                        kb.mm(bkA[:, 0:128], [(res["kd_t"][:, h, tsl], res["qd_t"][:, h, tsl])], R=[res["kd_b"], res["qd_b"]], W=[bkA_b])
                        a, a_b = atm.next()
                        kb.op("dve", "tensor_tensor", out=a[:], in0=bkA[:, 0:128], in1=G.hmask[d], op=ALU.mult,
                              R=[bkA_b, G.const_b], W=[a_b])
                        ats.append((a, a_b))
                    bst["ats"] = ats
                ops.append(blkA)
                for step, ci in enumerate(corder):
                    def blkS(tb=tb, bst=bst, ci=ci):
                        iv, iv_b = res["iv"], res["iv_b"]
                        km, km_b = res["km"], res["km_b"]
                        qe_t, qe_b = res["qe_t"], res["qe_b"]
                        dc, dc_b = res["dc"], res["dc_b"]
                        csl = slice(ci * 64, (ci + 1) * 64)
                        for h in range(H4):
                            a, a_b = bst["ats"][h]
                            bkO, bkO_b = kb.psum_banks[h]
                            c_ = cur[h]
                            kb.mm(bkO[:, ci * 64:(ci + 1) * 64],
                                  [(iv[:, tb, h * 128:(h + 1) * 128], a[:, csl]),
                                   (Sb_[h][c_][:], qe_t[:, h, tb * 128 + ci * 64:tb * 128 + (ci + 1) * 64])],
                                  R=[iv_b, a_b, Sb_b[h][c_], qe_b], W=[bkO_b])
                            bkS, bkS_b = kb.bank()
                            kb.mm(bkS[:, 0:128], [(km[csl, tb, h * 128:(h + 1) * 128], iv[csl, tb, h * 128:(h + 1) * 128])],
                                  R=[km_b, iv_b], W=[bkS_b])
                            n_ = 1 - c_
                            kb.op("dve", "scalar_tensor_tensor", out=St[h][n_][:], in0=St[h][c_][:],
                                  scalar=dc[:, h, tb * 2 + ci:tb * 2 + ci + 1], in1=bkS[:, 0:128], op0=ALU.mult, op1=ALU.add,
                                  R=[St_b[h][c_], dc_b, bkS_b], W=[St_b[h][n_]])
                            kb.op("act", "activation", out=Sb_[h][n_][:], in_=St[h][n_][:], func=AF.Copy, R=[St_b[h][n_]], W=[Sb_b[h][n_]])
                            cur[h] = n_
                    ops.append(blkS)

                def blkE(tb=tb):
                    o, o_b = st["o"], st["o_b"]
                    tsl = slice(tb * 128, (tb + 1) * 128)
                    for h in range(H4):
                        bkO, bkO_b = kb.psum_banks[h]
                        if d == 1:
                            kb.op("act", "activation", out=o[:, h, tsl], in_=bkO[:, 0:128], func=AF.Copy, R=[bkO_b], W=[o_b])
                        else:
                            obi, obi_b = res["obi"], res["obi_b"]
                            kb.op("dve", "tensor_tensor", out=o[:, h, tsl], in0=bkO[:, 0:128], in1=obi[:, h, tsl], op=ALU.add,
                                  R=[bkO_b, obi_b], W=[o_b])
                ops.append(blkE)

            def fin():
                o, o_b = st["o"], st["o_b"]
                if d == 1:
                    kb.dma("pool", fm_view(S.ob, 0, H4, t0, T), o[:], R=[o_b], W=[sbuf_of(S, "ob", t)])
                else:
                    gg, gg_b = res["gg"], res["gg_b"]
                    kb.op("act", "activation", out=sq[0][:], in_=o[:], func=AF.Square, R=[o_b], W=[sq[1]])
                    for h in range(H4):
                        bk, bk_b = kb.bank()
                        kb.mm(bk[:, 0:T], [(G.ones, sq[0][:, h, :])], R=[sq[1], G.const_b], W=[bk_b])
                        kb.op("act", "activation", out=rs[0][:, h, :], in_=bk[:, 0:T], func=AF.Ln, bias=G.epsD[:], scale=1.0 / 128,
                              R=[bk_b, G.const_b], W=[rs[1]])
                    kb.op("act", "activation", out=rs[0][:], in_=rs[0][:], func=AF.Exp, scale=-0.5, R=[rs[1]], W=[rs[1]])
                    kb.op("act", "activation", out=sgt[0][:], in_=gg[:], func=AF.Silu, R=[gg_b], W=[sgt[1]])
                    kb.op("pool", "tensor_tensor", out=rs[0][:], in0=rs[0][:], in1=o[:], op=ALU.mult, R=[o_b], W=[rs[1]])
                    ho, ho_b = hout.next()
                    kb.op("dve", "scalar_tensor_tensor", out=ho[:], in0=rs[0][:], scalar=gain, in1=sgt[0][:], op0=ALU.mult, op1=ALU.mult,
                          R=[rs[1], sgt[1], G.const_b], W=[ho_b])
                    kb.dma("pool", fm_view(S.hg, 0, H4, t0, T), ho[:], R=[ho_b], W=[sbuf_of(S, "hg", t)])
            ops.append(fin)
            return ops

        kb.bank_range = (4, 8)
        for d in (1, 0):
            cur = [0] * H4
            for h in range(H4):
                kb.op("dve", "memset", St[h][0][:], 0.0, W=[St_b[h][0]])
                kb.op("pool", "memset", Sb_[h][0][:], 0.0, W=[Sb_b[h][0]])
            tiles = list(range(NT))
            if d == 1:
                tiles = tiles[::-1]
            pops, pres = pre_ops(d, tiles[0])
            for f in pops:
                f()
            for i, t in enumerate(tiles):
                bops = block_ops(d, t, pres, cur)
                if i + 1 < len(tiles):
                    nops, nres = pre_ops(d, tiles[i + 1])
                else:
                    nops, nres = [], None
                nb = len(bops)
                per = (len(nops) + nb - 2) // max(1, nb - 1) if nops else 0
                import os
                if os.environ.get('P3_NOIL'):
                    per = 0
                pi = 0
                for bi, f in enumerate(bops):
                    f()
                    if bi < nb - 1:
                        for _ in range(per):
                            if pi < len(nops):
                                nops[pi]()
                                pi += 1
                while pi < len(nops):
                    nops[pi]()
                    pi += 1
                pres = nres
            kb.barrier()
        kb.bank_range = (0, 8)


AW = 2048
AGROUPS = ((1, 0), (4, 1), (16, 2))


def host_rope(L):
    half = HD // 2
    inv = (10000.0 ** (-np.arange(half, dtype=np.float32) / half)).astype(np.float32)
    pos = np.arange(L, dtype=np.float32)
    ang = (pos[None, :] * inv[:, None]).astype(np.float32)
    fi = (np.arange(128) % 64) % 32
    return np.stack([np.cos(ang)[fi], np.sin(ang)[fi]], axis=1).astype(NPBF)


def pipeline(stages_list, look=1):
    n = len(stages_list)
    for i in range(min(look, n)):
        stages_list[i][0]()
    for i in range(n):
        if i + look < n:
            stages_list[i + look][0]()
        stages_list[i][1]()


def pipeline_n(items, skew=1):
    n = len(items)
    if n == 0:
        return
    k = len(items[0])
    for step in range(n + (k - 1) * skew):
        for s_ in range(k):
            i = step - s_ * skew
            if 0 <= i < n:
                items[i][s_]()


def phase_p4(kb, G, S, l):
    nc = kb.nc
    L = S.L
    W = AW
    NW = L // W
    with contextlib.ExitStack() as es:
        kb.nuid += 1
        uid = kb.nuid

        def sb(name, shape, dt):
            return es.enter_context(nc.sbuf_tensor(f"{name}_u{uid}", list(shape), dt))

        NQC = W // 512
        KWID = [W + 128 * r for r, g in AGROUPS]
        KOFF = [0, KWID[0], KWID[0] + KWID[1]]
        qz = sb("a_qz", [128, 3, 2, W], BF16)
        qz_b = [[Buf() for c in range(NQC)] for g in range(3)]
        qraw = Rot([(sb(f"a_qraw{i}", [128, 512], BF16), Buf()) for i in range(3)])
        kn = sb("a_kn", [128, sum(KWID)], BF16)
        kn_b = [[Buf() for c in range((KWID[g] + 511) // 512)] for g in range(3)]
        nblk = [r * (W // r // 128 + 1) for r, g in AGROUPS]
        goff = [0, nblk[0], nblk[0] + nblk[1]]
        vt = (sb("a_vt", [128, sum(nblk), 512], BF16), Buf())
        vt_gc = [[Buf() for c in range(r)] for r, g in AGROUPS]
        cs = (sb("a_cs", [128, 2, W + 2048], BF16), Buf())
        acc = (sb("a_acc", [128, 2, W], F32), Buf())
        rec = Rot([(sb(f"a_rec{i}", [128, 512], F32), Buf()) for i in range(2)])
        aout = Rot([(sb(f"a_out{i}", [128, W], BF16), Buf()) for i in range(1)])
        ND_ = 5
        sqt = Rot([(sb(f"a_sq{i}", [128, 512], BF16), Buf()) for i in range(ND_)])
        y1t = Rot([(sb(f"a_y1{i}", [128, 512], BF16), Buf()) for i in range(ND_)])
        y2t = Rot([(sb(f"a_y2{i}", [128, 512], BF16), Buf()) for i in range(ND_)])
        rst = Rot([(sb(f"a_rs{i}", [128, 512], F32), Buf()) for i in range(ND_)])
        ptt = Rot([(sb(f"a_pt{i}", [128, 512], BF16), Buf()) for i in range(4)])
        qgain = G.vecs[:, l * NV + 24:l * NV + 25]
        kgain = G.vecs[:, l * NV + 25:l * NV + 26]
        kb.op("pool", "memset", qz[:], 0.0, W=[b for bb in qz_b for b in bb])

        blkg = sb("a_blkg", [128, 2, 128], BF16)
        ginv = sb("a_ginv", [128, 2], F32)
        blkg_b = Buf()
        kb.op("dve", "tensor_tensor", out=ginv[:], in0=G.vecs[:, l * NV + 24:l * NV + 26], in1=G.vecs[:, l * NV + 24:l * NV + 26],
              op=ALU.mult, R=[G.const_b], W=[blkg_b])
        kb.op("dve", "reciprocal", out=ginv[:], in_=ginv[:], W=[blkg_b])
        for i in range(2):
            kb.op("dve", "tensor_scalar", out=blkg[:, i, :], in0=G.blk, scalar1=ginv[:, i:i + 1], scalar2=None, op0=ALU.mult,
                  R=[G.const_b], W=[blkg_b])

        def nrope_stages(x, xb, c0, n, which, outs=None, outb=None, pre=None):
            st = {}

            def S1():
                if pre is not None:
                    pre()
                s_, s_b = sqt.next()
                kb.op("act", "activation", out=s_[:, 0:n], in_=x, func=AF.Square, R=[xb], W=[s_b])
                y1, y1_b = y1t.next()
                y2, y2_b = y2t.next()
                kb.op("dve", "tensor_tensor", out=y1[:, 0:n], in0=x, in1=cs[0][:, 0, c0:c0 + n], op=ALU.mult, R=[xb, cs[1]], W=[y1_b])
                kb.op("pool", "tensor_tensor", out=y2[:, 0:n], in0=x, in1=cs[0][:, 1, c0:c0 + n], op=ALU.mult, R=[xb, cs[1]], W=[y2_b])
                st.update(s_=s_, s_b=s_b, y1=y1, y1_b=y1_b, y2=y2, y2_b=y2_b)

            def S2():
                b1, b1_b = kb.bank()
                kb.mm(b1[:, 0:n], [(blkg[:, which, :], st["s_"][:, 0:n])], R=[st["s_b"], blkg_b], W=[b1_b])
                b2, b2_b = kb.bank()
                kb.mm(b2[:, 0:n], [(G.ident, st["y1"][:, 0:n]), (G.rot, st["y2"][:, 0:n])], R=[st["y1_b"], st["y2_b"], G.const_b], W=[b2_b])
                st.update(b1=b1, b1_b=b1_b, b2=b2, b2_b=b2_b)

            def S3():
                r_, r_b = rst.next()
                kb.op("act", "activation", out=r_[:, 0:n], in_=st["b1"][:, 0:n], func=AF.Ln, bias=G.epsD[:], scale=1.0 / HD,
                      R=[st["b1_b"], G.const_b], W=[r_b])
                kb.op("act", "activation", out=r_[:, 0:n], in_=r_[:, 0:n], func=AF.Exp, scale=-0.5, R=[r_b], W=[r_b])
                st.update(r_=r_, r_b=r_b)

            def S4():
                b2, b2_b, r_, r_b = st["b2"], st["b2_b"], st["r_"], st["r_b"]
                if outs is None:
                    kb.op("dve", "tensor_tensor", out=x, in0=b2[:, 0:n], in1=r_[:, 0:n], op=ALU.mult, R=[b2_b, r_b], W=[xb])
                else:
                    for i, (dst, ps) in enumerate(outs):
                        kb.op("dve", "tensor_tensor", out=dst, in0=b2[ps, 0:n], in1=r_[ps, 0:n], op=ALU.mult,
                              R=[b2_b, r_b], W=[outb])
            return (S1, S2, S3, S4)

        for w in range(NW):
            w0 = w * W
            first_w = (w == 0)
            last_w = (w == NW - 1)
            lo = max(0, w0 - 1024)
            hi = min(L, w0 + W + 1024)
            kb.dma("sp", cs[0][:, :, lo - (w0 - 1024):hi - (w0 - 1024)], S.rope[:, :, lo:hi], W=[cs[1]])
            qk_t = [sbuf_of(S, "qk", t) for t in range(lo // T, hi // T)]
            v_t = [sbuf_of(S, "v", t) for t in range(lo // T, hi // T)]
            for gi, (r, g) in enumerate(AGROUPS):
                nbw = W // r // 128
                N = 128 * (nbw + 1)
                n_lo = 64 if first_w else 0
                n_hi = N - 64 if last_w else N
                rows = S.v[w0 - 64 * r + r * n_lo:w0 - 64 * r + r * n_hi, g * 512:(g + 1) * 512]
                rv = rows.rearrange("(n r) c -> r n c", r=r)
                for c in range(r):
                    b0 = goff[gi] + c * (nbw + 1)
                    n = n_lo
                    if n_lo > 0:
                        kb.op("pool", "memset", vt[0][0:64, b0, :], 0.0, W=[vt_gc[gi][c]])
                        kb.dma("sp", vt[0][64:128, b0, :], rv[c, 0:64, :], R=v_t, W=[vt_gc[gi][c]])
                        n = 128
                    m_lo = n // 128
                    m_hi = n_hi // 128
                    if m_hi > m_lo:
                        src = rv[c, m_lo * 128 - n_lo:m_hi * 128 - n_lo, :].rearrange("(m p) c -> p m c", p=128)
                        kb.dma("sp", vt[0][:, b0 + m_lo:b0 + m_hi, :], src, R=v_t, W=[vt_gc[gi][c]])
                    if n_hi % 128:
                        kb.op("pool", "memset", vt[0][64:128, b0 + nbw, :], 0.0, W=[vt_gc[gi][c]])
                        kb.dma("sp", vt[0][0:64, b0 + nbw, :], rv[c, m_hi * 128 - n_lo:m_hi * 128 - n_lo + 64, :], R=v_t, W=[vt_gc[gi][c]])
            for hp in range(4):
                kval = []
                for r, g in AGROUPS:
                    j_lo = 64 * r if first_w else 0
                    j_hi = W + 64 * r if last_w else W + 128 * r
                    kval.append((j_lo, j_hi))
                    row0 = 1536 + g * 512 + hp * 128
                    ko = KOFF[g]
                    if j_lo > 0:
                        kb.op("pool", "memset", kn[:, ko:ko + j_lo], 0.0, W=kn_b[g])
                    if j_hi < KWID[g]:
                        kb.op("pool", "memset", kn[:, ko + j_hi:ko + KWID[g]], 0.0, W=kn_b[g])
                    kb.dma("sp", kn[:, ko + j_lo:ko + j_hi], S.qk[row0:row0 + 128, w0 - 64 * r + j_lo:w0 - 64 * r + j_hi],
                           R=qk_t, W=kn_b[g])
                stages = []
                for r, g in AGROUPS:
                    for ci, c0 in enumerate(range(0, W, 512)):
                        qx, qx_b = qraw.next()

                        def pre(qx=qx, qx_b=qx_b, g=g, c0=c0):
                            kb.dma("sp", qx[:], S.qk[g * 512 + hp * 128:g * 512 + hp * 128 + 128, w0 + c0:w0 + c0 + 512], R=qk_t, W=[qx_b])
                        stages.append(nrope_stages(qx[:], qx_b, 1024 + c0, 512, 0,
                                                   outs=[(qz[0:64, g, 0, c0:c0 + 512], slice(0, 64)), (qz[64:128, g, 1, c0:c0 + 512], slice(64, 128))],
                                                   outb=qz_b[g][ci], pre=pre))
                    j_lo, j_hi = kval[g]
                    ko = KOFF[g]
                    for ci in range(len(kn_b[g])):
                        c0 = max(j_lo, ci * 512)
                        c1 = min(j_hi, (ci + 1) * 512)
                        if c1 > c0:
                            stages.append(nrope_stages(kn[:, ko + c0:ko + c1], kn_b[g][ci], 1024 - 64 * r + c0, c1 - c0, 1))
                pipeline_n(stages, skew=1)
                units = []
                for gi, (r, g) in enumerate(AGROUPS):
                    nbw = W // r // 128
                    nb_total = L // r // 128
                    qv = qz[:, g, :, :].rearrange("p s (n r) -> p s r n", r=r)
                    kv = kn[:, KOFF[g]:KOFF[g] + KWID[g]].rearrange("p (n r) -> p r n", r=r)
                    accv = acc[0][:].rearrange("p s (n r) -> p s r n", r=r)
                    for c in range(r):
                        b0 = goff[gi] + c * (nbw + 1)
                        for bq in range(nbw):
                            mA = (w0 // r) // 128 + bq
                            variant = (1 if mA == 0 else 0) + (2 if mA + 1 == nb_total else 0)
                            st = {}
                            is_first = (c == 0 and bq == 0)
                            is_last = (c == r - 1 and bq == nbw - 1)

                            def A(st=st, g=g, qv=qv, kv=kv, c=c, bq=bq, variant=variant):
                                bk, bk_b = kb.bank()
                                qs = qv[:, :, c, 128 * bq:128 * bq + 128]
                                kb.mm_raw([(bk[:, 0:256], kv[:, c, 128 * bq:128 * bq + 128], qs, True, False),
                                           (bk[:, 256:512], kv[:, c, 128 * (bq + 1):128 * (bq + 1) + 128], qs, False, False),
                                           (bk[:, 0:512], G.ident, G.amask[variant], False, True)],
                                          R=qz_b[g] + kn_b[g] + [G.const_b], W=[bk_b])
                                pt, pt_b = ptt.next()
                                kb.op("act", "activation", out=pt[:], in_=bk[:, 0:512], func=AF.Exp, scale=float(HD) ** -0.5, R=[bk_b], W=[pt_b])
                                st.update(pt=pt, pt_b=pt_b)

                            def B(st=st, gi=gi, accv=accv, c=c, bq=bq, b0=b0, is_first=is_first, is_last=is_last, hp=hp):
                                pt, pt_b = st["pt"], st["pt_b"]
                                b2, b2_b = kb.bank()
                                vc = slice(hp * 128, (hp + 1) * 128)
                                kb.mm_multi([(b2[:, 0:256], [(vt[0][:, b0 + bq, vc], pt[:, 0:256]), (vt[0][:, b0 + bq + 1, vc], pt[:, 256:512])]),
                                             (b2[:, 256:512], [(G.ones, pt[:, 0:256]), (G.ones, pt[:, 256:512])])],
                                            R=[pt_b, vt_gc[gi][c], G.const_b], W=[b2_b])
                                tok = None
                                for h in range(2):
                                    hs = slice(h * 64, (h + 1) * 64)
                                    av = accv[hs, :, c, 128 * bq:128 * bq + 128]
                                    b2v = b2[hs, 0:512].rearrange("p (s h n) -> p s h n", s=2, h=2)[:, :, h, :]
                                    Wl = [acc[1]] if (is_first and h == 0) else []
                                    if gi == 0:
                                        tok = kb.op("dve", "tensor_copy", out=av, in_=b2v, R=[b2_b], W=Wl)
                                    else:
                                        tok = kb.op("dve", "tensor_tensor", out=av, in0=av, in1=b2v, op=ALU.add, R=[b2_b], W=Wl)
                                if is_last:
                                    acc[1].w = tok
                                    acc[1].r = {}
                            units.append((A, B))
                pipeline(units, look=2)
                ao, ao_b = aout.next()
                for c0 in range(0, W, 512):
                    rc, rc_b = rec.next()
                    kb.op("act", "activation", out=rc[:], in_=acc[0][:, 1, c0:c0 + 512], func=AF.Ln, R=[acc[1]], W=[rc_b])
                    kb.op("act", "activation", out=rc[:], in_=rc[:], func=AF.Exp, scale=-1.0, R=[rc_b], W=[rc_b])
                    kb.op("dve", "tensor_tensor", out=ao[:, c0:c0 + 512], in0=acc[0][:, 0, c0:c0 + 512], in1=rc[:], op=ALU.mult,
                          R=[acc[1], rc_b], W=[ao_b])
                kb.dma("pool", S.att[hp * 128:(hp + 1) * 128, w0:w0 + W], ao[:], R=[ao_b],
                       W=[sbuf_of(S, "att", t) for t in range(w0 // T, (w0 + W) // T)])
        kb.barrier()


LP = 16384
LS = 2048
NCORES = 8
_CACHE = {}


def kernel(**inputs):
    inputs = {k: np.asarray(v) for k, v in inputs.items()}
    xp = inputs["x_prompt"]
    xs = inputs["x_sample"]
    if "prog" not in _CACHE:
        _CACHE["prog"] = build(LP, LS)
    kb, seqs = _CACHE["prog"]
    wsrc = host_pack_weights(inputs)
    vecs = host_vecs(inputs)
    cbf = host_consts()
    lbl = host_lbl(inputs)
    pfb, pff = host_fnet_consts(LP)
    sfb, sff = host_fnet_consts(LS)
    prope = host_rope(LP)
    srope = host_rope(LS)
    zeros_p = np.zeros((D, LP), np.float32)
    in_maps = []
    for c in range(NCORES):
        px = np.ascontiguousarray(xp[c].T) if c < xp.shape[0] else zeros_p
        in_maps.append({
            "wsrc": wsrc, "vecs": vecs, "cbf": cbf, "lbl": lbl,
            "p_x": px, "s_x": np.ascontiguousarray(xs[c].T),
            "p_fnb": pfb, "p_fnf": pff, "s_fnb": sfb, "s_fnf": sff,
            "p_rope": prope, "s_rope": srope,
        })
    res = run_bass_kernel_spmd(kb.nc, in_maps, core_ids=list(range(NCORES)))
    yp = np.stack([np.asarray(res.results[c]["p_y"]).T for c in range(xp.shape[0])], axis=0).astype(np.float32)
    ys = np.stack([np.asarray(res.results[c]["s_y"]).T for c in range(NCORES)], axis=0).astype(np.float32)
    return (np.ascontiguousarray(yp), np.ascontiguousarray(ys))
```
